# Optimizing a Trainium2 kernel written in Bass

```python
import jax
import jax.numpy as jnp
from jax import lax
import numpy as np

D_MODEL = 2048
BATCH = 4
SEQ = 8192
DEPTH = 1

CTX_LEN = 256
GRID_W = 64
WIN_H = 8
WIN_W = 16
NA_HEADS = 16
NA_HEAD_DIM = 64
NA_WIDTH = NA_HEADS * NA_HEAD_DIM
HG_HEADS = 8
HG_KEY_DIM = 128
HG_VAL_DIM = 128
HG_FDIM = HG_HEADS * HG_KEY_DIM
HG_WIDTH = HG_HEADS * HG_VAL_DIM
HG_CHUNK = 64
FFN_HIDDEN = -(-8 * D_MODEL // (3 * 256)) * 256
N_MOD = 6
EPS = 1e-6
IN_SPLIT = (NA_WIDTH, NA_WIDTH, NA_WIDTH, HG_FDIM, HG_FDIM, HG_FDIM, HG_WIDTH, HG_WIDTH, D_MODEL, D_MODEL)
IN_WIDTH = sum(IN_SPLIT)

kernel_name = 'hybrid_natten_hgrn2_dit_layer'


def rmsnorm(x, g):
    xf = x.astype(jnp.float32)
    y = xf * lax.rsqrt(jnp.mean(xf * xf, axis=-1, keepdims=True) + EPS)
    return (y * g.astype(jnp.float32)).astype(x.dtype)


def modulate(h, shift, scale):
    return h * (1 + scale) + shift


def split_columns(u):
    cuts, acc = [], 0
    for w in IN_SPLIT[:-1]:
        acc += w
        cuts.append(acc)
    return jnp.split(u, cuts, axis=-1)


def to_heads(a, n_heads):
    return a.reshape(a.shape[0], a.shape[1], n_heads, -1)


def neighbourhood_attention(q, k, v, k_ctx, v_ctx, rpb):
    b, s, h, dh = q.shape
    rows = s // GRID_W
    kh = min(WIN_H, rows)
    scale = dh ** -0.5
    qg = q.reshape(b, rows, GRID_W, h, dh)
    kg = k.reshape(b, rows, GRID_W, h, dh)
    vg = v.reshape(b, rows, GRID_W, h, dh)
    col = jnp.arange(GRID_W)
    col_start = jnp.clip(col - WIN_W // 2, 0, GRID_W - WIN_W)
    in_win = (col[None, :] >= col_start[:, None]) & (col[None, :] < col_start[:, None] + WIN_W)
    band_mask = jnp.broadcast_to(in_win[:, None, :], (GRID_W, kh, GRID_W)).reshape(GRID_W, kh * GRID_W)
    dc_idx = jnp.clip(col[None, :] - col[:, None], 1 - WIN_W, WIN_W - 1) + WIN_W - 1
    row_start = jnp.clip(jnp.arange(rows) - kh // 2, 0, rows - kh)
    nb = kh * GRID_W

    def one_row(r):
        rs = row_start[r]
        q_r = lax.dynamic_index_in_dim(qg, r, axis=1, keepdims=False)
        k_band = lax.dynamic_slice_in_dim(kg, rs, kh, axis=1).reshape(b, nb, h, dh)
        v_band = lax.dynamic_slice_in_dim(vg, rs, kh, axis=1).reshape(b, nb, h, dh)
        dr_idx = rs + jnp.arange(kh) - r + WIN_H - 1
        bias = rpb[:, dr_idx][:, :, dc_idx]
        bias = bias.transpose(0, 2, 1, 3).reshape(h, GRID_W, nb).astype(jnp.float32)
        s_band = jnp.einsum('bqhd,bnhd->bhqn', q_r, k_band, preferred_element_type=jnp.float32) * scale + bias
        s_band = jnp.where(band_mask, s_band, -jnp.inf)
        s_ctx = jnp.einsum('bqhd,bmhd->bhqm', q_r, k_ctx, preferred_element_type=jnp.float32) * scale
        p = jax.nn.softmax(jnp.concatenate([s_band, s_ctx], axis=-1), axis=-1).astype(v.dtype)
        return (jnp.einsum('bhqn,bnhd->bqhd', p[..., :nb], v_band)
                + jnp.einsum('bhqm,bmhd->bqhd', p[..., nb:], v_ctx))

    out = lax.map(one_row, jnp.arange(rows))
    return out.transpose(1, 0, 2, 3, 4).reshape(b, s, h * dh)


def context_attention(q, k, v):
    b, l, h, dh = q.shape
    s = jnp.einsum('blhd,bmhd->bhlm', q, k, preferred_element_type=jnp.float32) * dh ** -0.5
    p = jax.nn.softmax(s, axis=-1).astype(v.dtype)
    return jnp.einsum('bhlm,bmhd->blhd', p, v).reshape(b, l, h * dh)


def hgrn2_gates(f_logits, lb):
    f = lb + (1 - lb) * jax.nn.sigmoid(f_logits.astype(jnp.float32))
    return jnp.log(f), 1 - f


def hgrn2_chunk_scan(q, k, log_f, v, state0):
    b, t, h, dk = q.shape
    dv = v.shape[-1]
    n = t // HG_CHUNK

    def chunks(a):
        return a.astype(jnp.float32).reshape(b, n, HG_CHUNK, h, a.shape[-1]).transpose(1, 0, 3, 2, 4)

    tri = jnp.tril(jnp.ones((HG_CHUNK, HG_CHUNK), dtype=bool))[:, :, None]

    def step(state, inp):
        q_c, k_c, lf_c, v_c = inp
        cum = jnp.cumsum(lf_c, axis=2)
        rel = jnp.exp(jnp.where(tri, cum[:, :, :, None, :] - cum[:, :, None, :, :], -jnp.inf))
        attn = jnp.einsum('bhtk,bhsk,bhtsk->bhts', q_c, k_c, rel)
        out = (jnp.einsum('bhts,bhsv->bhtv', attn, v_c)
               + jnp.einsum('bhtk,bhkv->bhtv', q_c * jnp.exp(cum), state))
        last = cum[:, :, -1:, :]
        new_state = (jnp.exp(last[:, :, 0, :])[..., None] * state
                     + jnp.einsum('bhsk,bhsv->bhkv', k_c * jnp.exp(last - cum), v_c))
        return new_state, out

    final, out = lax.scan(step, state0, (chunks(q), chunks(k), chunks(log_f), chunks(v)))
    return out.transpose(1, 0, 3, 2, 4).reshape(b, t, h, dv), final


def hgrn2_bidirectional(q, f_fwd, f_bwd, i, lb_fwd, lb_bwd, state_fwd, state_bwd):
    lf1, k1 = hgrn2_gates(f_fwd, lb_fwd)
    lf2, k2 = hgrn2_gates(f_bwd, lb_bwd)
    o1, s1 = hgrn2_chunk_scan(q, k1, lf1, i, state_fwd)
    o2, s2 = hgrn2_chunk_scan(jnp.flip(q, 1), jnp.flip(k2, 1), jnp.flip(lf2, 1), jnp.flip(i, 1), state_bwd)
    return o1 + jnp.flip(o2, 1), s1, s2


def hgrn2_readout(o, out_gate, g):
    on = o * lax.rsqrt(jnp.mean(o * o, axis=-1, keepdims=True) + EPS) * g.astype(jnp.float32)
    on = on.reshape(o.shape[0], o.shape[1], -1)
    return (on * jax.nn.silu(out_gate.astype(jnp.float32))).astype(out_gate.dtype)


def gated_merge(o_a, o_b, gate_a, gate_b, w_pa, w_pb, w_out):
    y = jax.nn.sigmoid(gate_a) * (o_a @ w_pa) + jax.nn.sigmoid(gate_b) * (o_b @ w_pb)
    return y @ w_out


def swiglu(h, w_in, w_out):
    a, u = jnp.split(h @ w_in, 2, axis=-1)
    return (jax.nn.silu(a) * u) @ w_out


def token_mixing(h, hc, w_in, rpb, lb_f, lb_b, hg_g, w_pa, w_pb, w_out, with_ctx_out):
    q, k, v, hq, hf_f, hf_b, hi, hog, ga, gb = split_columns(h @ w_in)
    qc, kc, vc, hqc, hfc_f, hfc_b, hic, hogc, gac, gbc = split_columns(hc @ w_in)
    b = hc.shape[0]
    kc_h, vc_h = to_heads(kc, NA_HEADS), to_heads(vc, NA_HEADS)
    o_na = neighbourhood_attention(to_heads(q, NA_HEADS), to_heads(k, NA_HEADS), to_heads(v, NA_HEADS),
                                   kc_h, vc_h, rpb)
    zero = jnp.zeros((b, HG_HEADS, HG_KEY_DIM, HG_VAL_DIM), jnp.float32)
    oc_hg, s_f, s_b = hgrn2_bidirectional(to_heads(hqc, HG_HEADS), to_heads(hfc_f, HG_HEADS),
                                          to_heads(hfc_b, HG_HEADS), to_heads(hic, HG_HEADS),
                                          lb_f, lb_b, zero, zero)
    o_hg, _, _ = hgrn2_bidirectional(to_heads(hq, HG_HEADS), to_heads(hf_f, HG_HEADS),
                                     to_heads(hf_b, HG_HEADS), to_heads(hi, HG_HEADS),
                                     lb_f, lb_b, s_f, s_b)
    y = gated_merge(o_na, hgrn2_readout(o_hg, hog, hg_g), ga, gb, w_pa, w_pb, w_out)
    yc = None
    if with_ctx_out:
        oc_na = context_attention(to_heads(qc, NA_HEADS), kc_h, vc_h)
        yc = gated_merge(oc_na, hgrn2_readout(oc_hg, hogc, hg_g), gac, gbc, w_pa, w_pb, w_out)
    return y, yc


def setup_inputs(seed: int = 0) -> dict:
    key = jax.random.key(seed)
    ks = jax.random.split(key, 20)
    f32 = jnp.float32

    def nrm(k, shape, scale):
        return jax.random.normal(k, shape, f32) * scale

    return {
        'x': nrm(ks[0], (BATCH, SEQ, D_MODEL), 1.0),
        'c': nrm(ks[1], (BATCH, D_MODEL), 1.0),
        'ctx': nrm(ks[2], (BATCH, CTX_LEN, D_MODEL), 1.0),
        'c_ctx': nrm(ks[3], (D_MODEL,), 1.0),
        'w_ada': nrm(ks[4], (DEPTH, D_MODEL, N_MOD * D_MODEL), 0.5 * D_MODEL ** -0.5),
        'b_ada': nrm(ks[5], (DEPTH, N_MOD * D_MODEL), 0.01),
        'norm1_g': 1.0 + nrm(ks[6], (DEPTH, D_MODEL), 0.02),
        'w_in': nrm(ks[7], (DEPTH, D_MODEL, IN_WIDTH), D_MODEL ** -0.5),
        'na_rpb': nrm(ks[8], (DEPTH, NA_HEADS, 2 * WIN_H - 1, 2 * WIN_W - 1), 0.1),
        'hg_lb_logits': nrm(ks[9], (DEPTH + 1, 2, HG_FDIM), 1.0),
        'hg_norm_g': 1.0 + nrm(ks[10], (DEPTH, HG_VAL_DIM), 0.02),
        'w_pa': nrm(ks[11], (DEPTH, NA_WIDTH, D_MODEL), NA_WIDTH ** -0.5),
        'w_pb': nrm(ks[12], (DEPTH, HG_WIDTH, D_MODEL), HG_WIDTH ** -0.5),
        'w_out': nrm(ks[13], (DEPTH, D_MODEL, D_MODEL), D_MODEL ** -0.5),
        'norm2_g': 1.0 + nrm(ks[14], (DEPTH, D_MODEL), 0.02),
        'w_ffn_in': nrm(ks[15], (DEPTH, D_MODEL, 2 * FFN_HIDDEN), D_MODEL ** -0.5),
        'w_ffn_out': nrm(ks[16], (DEPTH, FFN_HIDDEN, D_MODEL), FFN_HIDDEN ** -0.5),
        'final_g': 1.0 + nrm(ks[17], (D_MODEL,), 0.02),
    }


def reference(x, c, ctx, c_ctx, w_ada, b_ada, norm1_g, w_in, na_rpb, hg_lb_logits, hg_norm_g,
              w_pa, w_pb, w_out, norm2_g, w_ffn_in, w_ffn_out, final_g):
    b = x.shape[0]
    lb_table = jnp.cumsum(jax.nn.softmax(hg_lb_logits.astype(jnp.float32), axis=0), axis=0)
    silu_c = jax.nn.silu(c)
    silu_cc = jax.nn.silu(c_ctx)
    xc = ctx
    for l in range(DEPTH):
        last = l == DEPTH - 1
        mod = (silu_c @ w_ada[l] + b_ada[l]).reshape(b, N_MOD, 1, D_MODEL)
        mod_c = (silu_cc @ w_ada[l] + b_ada[l]).reshape(N_MOD, 1, D_MODEL)
        sh1, sc1, g1, sh2, sc2, g2 = [mod[:, j] for j in range(N_MOD)]
        sh1c, sc1c, g1c, sh2c, sc2c, g2c = [mod_c[j] for j in range(N_MOD)]
        lb_f = lb_table[l, 0].reshape(HG_HEADS, HG_KEY_DIM)
        lb_b = lb_table[l, 1].reshape(HG_HEADS, HG_KEY_DIM)
        h = modulate(rmsnorm(x, norm1_g[l]), sh1, sc1)
        hc = modulate(rmsnorm(xc, norm1_g[l]), sh1c, sc1c)
        y, yc = token_mixing(h, hc, w_in[l], na_rpb[l], lb_f, lb_b, hg_norm_g[l],
                             w_pa[l], w_pb[l], w_out[l], not last)
        x = x + g1 * y
        x = x + g2 * swiglu(modulate(rmsnorm(x, norm2_g[l]), sh2, sc2), w_ffn_in[l], w_ffn_out[l])
        if not last:
            xc = xc + g1c * yc
            xc = xc + g2c * swiglu(modulate(rmsnorm(xc, norm2_g[l]), sh2c, sc2c), w_ffn_in[l], w_ffn_out[l])
    return rmsnorm(x, final_g)
```

```python
import os
import numpy as np
from contextlib import ExitStack
import concourse.bass as bass
import concourse.mybir as mybir
from concourse.bass_utils import run_bass_kernel_spmd

F32 = mybir.dt.float32
BF16 = mybir.dt.bfloat16
U8 = mybir.dt.uint8
AF = mybir.ActivationFunctionType
ALU = mybir.AluOpType

ENGS = ("pe", "act", "dve", "pool", "sp")
SEM_LIMIT = 1000
DMA_SEM_LIMIT = 4000
N_DMA_SEMS = 8


class Buf:
    __slots__ = ("name", "writers", "readers", "war", "state")

    def __init__(self, name=""):
        self.name = name
        self.writers = []
        self.readers = []
        self.war = []
        self.state = "w"


def _compact(lst, o):
    if not o.is_dma:
        lst[:] = [x for x in lst if x.is_dma or x.eng != o.eng]
    lst.append(o)


class _Rec:
    def __getattr__(self, name):
        def f(*a, **k):
            self.__dict__["call"] = (name, a, k)
            return self
        return f


class Op:
    __slots__ = ("eng", "fn", "deps", "is_dma", "needs_inc", "sem", "count")

    def __init__(self, eng, fn, is_dma):
        self.eng = eng
        rec = _Rec()
        fn(rec)
        self.fn = rec.call
        self.deps = []
        self.is_dma = is_dma
        self.needs_inc = False
        self.sem = None
        self.count = None


class Prog:
    def __init__(self, nc, same_engine_sync=True):
        self.nc = nc
        self.ops = {e: [] for e in ENGS}
        self.same_engine_sync = same_engine_sync
        self.final_ops = []
        self.bar = {e: None for e in ENGS}
        self.last_compute = {e: None for e in ENGS}
        self.dma_since_bar = []

    def barrier(self):
        deps = [o for o in self.last_compute.values() if o is not None] + list(self.dma_since_bar)
        for e in ENGS:
            self.bar[e] = (self.bar[e] or []) + deps
        self.dma_since_bar = []

    def op(self, eng, fn, reads=(), writes=(), pwrites=(), dma=False):
        o = Op(eng, fn, dma)
        deps = []
        if self.bar[eng]:
            deps.extend(self.bar[eng])
            self.bar[eng] = None
        for b in reads:
            deps.extend(b.writers)
        for b in writes:
            deps.extend(b.writers)
            deps.extend(b.readers)
            deps.extend(b.war)
        for b in pwrites:
            if b.state == "r":
                deps.extend(b.readers)
            else:
                deps.extend(b.war)
        seen = set()
        for d in deps:
            if id(d) in seen:
                continue
            seen.add(id(d))
            if d.eng == eng and not d.is_dma and not dma:
                if eng == "pe" or not self.same_engine_sync:
                    continue
            o.deps.append(d)
            d.needs_inc = True
        for b in reads:
            _compact(b.readers, o)
            b.state = "r"
        for b in writes:
            b.writers = [o]
            b.war = [o]
            b.readers = []
            b.state = "w"
        for b in pwrites:
            if b.state == "r":
                b.war = list(b.readers)
                b.readers = []
                b.writers = []
                b.state = "w"
            _compact(b.writers, o)
        self.ops[eng].append(o)
        if dma:
            self.dma_since_bar.append(o)
        else:
            self.last_compute[eng] = o
        return o

    def finalize(self, ops):
        for o in ops:
            o.needs_inc = True
            self.final_ops.append(o)

    def emit(self, stack):
        nc = self.nc
        nsem = [0]

        def new_sem(name):
            nsem[0] += 1
            return stack.enter_context(nc.semaphore(name))

        for e in ENGS:
            cur, cnt, ep = None, 0, 0
            dsems, dcnt, dn = None, None, 0
            for o in self.ops[e]:
                if not o.needs_inc:
                    continue
                if o.is_dma:
                    if dsems is None:
                        dsems = [new_sem(f"d_{e}_{i}") for i in range(N_DMA_SEMS)]
                        dcnt = [0] * N_DMA_SEMS
                    j = dn % N_DMA_SEMS
                    dn += 1
                    if dcnt[j] + 16 > DMA_SEM_LIMIT:
                        dsems[j] = new_sem(f"d_{e}_{j}_{dn}")
                        dcnt[j] = 0
                    dcnt[j] += 16
                    o.sem, o.count = dsems[j], dcnt[j]
                else:
                    if cur is None or cnt >= SEM_LIMIT:
                        cur = new_sem(f"c_{e}_{ep}")
                        ep += 1
                        cnt = 0
                    cnt += 1
                    o.sem, o.count = cur, cnt
        self.n_sems = nsem[0]

        def run(ename):
            def body(eng):
                waited = {}

                def wait_all(deps):
                    need = {}
                    for d in deps:
                        k = id(d.sem)
                        if waited.get(k, 0) >= d.count:
                            continue
                        if k not in need or need[k][1] < d.count:
                            need[k] = (d.sem, d.count)
                    for k, (s, c) in need.items():
                        eng.wait_ge(s, c)
                        waited[k] = c

                for o in self.ops[ename]:
                    if o.deps:
                        wait_all(o.deps)
                    name, a, k = o.fn
                    ins = getattr(eng, name)(*a, **k)
                    if o.needs_inc:
                        ins.then_inc(o.sem, 16 if o.is_dma else 1)
                if ename == "sp":
                    wait_all(self.final_ops)
            return body

        with nc.Block() as block:
            block.tensor(run("pe"))
            block.scalar(run("act"))
            block.vector(run("dve"))
            block.gpsimd(run("pool"))
            block.sync(run("sp"))


D = 2048
KC = 16
SEQ = 8192
OWN = 4096
T = 512
NT_OWN = 8
NT_ALL = 16
CTX = 256
INW = 12288
FFH = 5632
FKC = 44
EPS = 1e-6
NEG = -30000.0
DEBUG = os.environ.get("KDEBUG", "")


def build_program():
    nc = bass.Bass("TRN2", target_bir_lowering=False)
    P = Prog(nc)

    def din(name, shape, dt=F32):
        return nc.dram_tensor(name, list(shape), dt, kind="ExternalInput").ap()

    dbg_names = set(DEBUG.split(",")) if DEBUG else set()

    def dscr(name, shape, dt):
        kind = "ExternalOutput" if name in dbg_names else "Internal"
        return nc.dram_tensor(name, list(shape), dt, kind=kind).ap()

    xT = din("xT", [D, SEQ])
    ctxT = din("ctxT", [D, CTX])
    cvec = din("cvec", [128, KC, 2])
    w_ada = din("w_ada", [D, INW])
    bada = din("bada", [128, 96, 2])
    gns = din("gns", [128, 3, KC])
    w_in = din("w_in", [D, INW])
    lbl = din("lbl", [128, 2, 2, 8])
    hgg = din("hgg", [128, 1])
    btab = din("btab", [8, 128, 3 * 2 * 5 * 128])
    masks = din("masks", [128, 2, 128], U8)
    w_pa = din("w_pa", [1024, D])
    w_pb = din("w_pb", [1024, D])
    w_out = din("w_out", [D, D])
    w_fi = din("w_fi", [D, 2 * FFH])
    w_fo = din("w_fo", [FFH, D])
    outT = nc.dram_tensor("outT", [D, OWN], F32, kind="ExternalOutput").ap()

    win_s = dscr("win_s", [24, 128, KC, 512], BF16)
    wfi_s = dscr("wfi_s", [22, 128, KC, 512], BF16)
    wpa_s = dscr("wpa_s", [4, 128, 8, 512], BF16)
    wpb_s = dscr("wpb_s", [4, 128, 8, 512], BF16)
    wo_s = dscr("wo_s", [4, 128, KC, 512], BF16)
    wfo_s = dscr("wfo_s", [16, 128, FKC, 128], BF16)
    hT_s = dscr("hT_s", [NT_ALL + 1, 128, KC, T], BF16)
    qT_s = dscr("qT_s", [8, 128, OWN], BF16)
    kT_s = dscr("kT_s", [8, 128, OWN + T], BF16)
    kcT_s = dscr("kcT_s", [8, 128, CTX], BF16)
    v_s = dscr("v_s", [8, 128, 36, 128], BF16)
    vc_s = dscr("vc_s", [8, 128, 2, 128], BF16)
    qhT_s = dscr("qhT_s", [8, 128, OWN], BF16)
    lfA_s = dscr("lfA_s", [8, 128, OWN], F32)
    lfAc_s = dscr("lfAc_s", [8, 128, CTX], F32)
    lfB_s = dscr("lfB_s", [8, 128, SEQ], F32)
    lfBc_s = dscr("lfBc_s", [8, 128, CTX], F32)
    vh_s = dscr("vh_s", [8, 128, 64, 128], BF16)
    vhc_s = dscr("vhc_s", [8, 128, 2, 128], BF16)
    hogT_s = dscr("hogT_s", [8, 128, OWN], BF16)
    gaT_s = dscr("gaT_s", [16, 128, OWN], BF16)
    gbT_s = dscr("gbT_s", [16, 128, OWN], BF16)
    onaT_s = dscr("onaT_s", [8, 128, OWN], BF16)
    o1T_s = dscr("o1T_s", [8, 128, OWN], F32)
    ohgT_s = dscr("ohgT_s", [8, 128, OWN], BF16)

    B_win = [Buf(f"win{n}") for n in range(24)]
    B_wfi = [Buf(f"wfi{n}") for n in range(22)]
    B_wpa = [Buf() for _ in range(4)]
    B_wpb = [Buf() for _ in range(4)]
    B_wo = [Buf() for _ in range(4)]
    B_wfo = [Buf() for _ in range(16)]
    B_hT = [Buf(f"hT{i}") for i in range(NT_ALL + 1)]
    Bs = {n: Buf(n) for n in "qT kT kcT v vc qhT lfA lfAc lfB lfBc vh vhc hogT gaT gbT onaT o1T ohgT".split()}

    with ExitStack() as top:
        def sbt(stack, name, shape, dt):
            return stack.enter_context(nc.sbuf_tensor(name, list(shape), dt))

        def pst(stack, name, shape, dt=F32):
            return stack.enter_context(nc.psum_tensor(name, list(shape), dt))

        prm = sbt(top, "prm", [128, 8, KC], F32)
        lb = sbt(top, "lb", [128, 2, 8], F32)
        oml = sbt(top, "oml", [128, 2, 8], F32)
        gn = sbt(top, "gn", [128, 3, KC], F32)
        hgg_sb = sbt(top, "hgg_sb", [128, 1], F32)
        ones_bf = sbt(top, "ones_bf", [128, 128], BF16)
        ident = sbt(top, "ident", [128, 128], BF16)
        mask_sb = sbt(top, "mask_sb", [128, 2, 128], U8)
        B_prm, B_lb, B_gn, B_const = Buf("prm"), Buf("lb"), Buf("gn"), Buf("const")

        P.op("sp", lambda e: e.dma_start(out=gn[:], in_=gns), writes=[B_gn], dma=True)
        P.op("sp", lambda e: e.dma_start(out=hgg_sb[:], in_=hgg), pwrites=[B_gn], dma=True)
        P.op("sp", lambda e: e.dma_start(out=mask_sb[:], in_=masks), pwrites=[B_const], dma=True)
        P.op("pool", lambda e: e.memset(ones_bf[:], 1.0), pwrites=[B_const])
        P.op("pool", lambda e: e.memset(ident[:], 1.0), pwrites=[B_const])
        P.op("pool", lambda e: e.affine_select(out=ident[:], in_=ident[:], pattern=[[1, 128]], compare_op=ALU.is_equal,
                                               fill=0.0, base=0, channel_multiplier=-1), pwrites=[B_const])

        def cast_w(src, dst, bufs, kcn, bw):
            for n in range(len(bufs)):
                P.op("pool", lambda e, n=n: e.dma_start(
                    out=dst[n], in_=src[:, n * bw:(n + 1) * bw].rearrange("(c p) n -> p c n", p=128)),
                    writes=[bufs[n]], dma=True)

        cast_w(w_in, win_s, B_win, KC, 512)

        with ExitStack() as ph:
            cv = sbt(ph, "cv", [128, KC, 2], F32)
            wa = [sbt(ph, f"wa{i}", [128, KC, 512], F32) for i in range(2)]
            bada_sb = sbt(ph, "bada_sb", [128, 96, 2], F32)
            mod = sbt(ph, "mod", [128, 96, 2], F32)
            lbl_sb = sbt(ph, "lbl_sb", [128, 2, 2, 8], F32)
            ps_mod = pst(ph, "ps_mod", [128, 256, 2])[:, 0:96, :]
            B_cv, B_wa, B_bada, B_mod, B_psmod, B_lbl = Buf(), [Buf(), Buf()], Buf(), Buf(), Buf(), Buf()
            P.op("sp", lambda e: e.dma_start(out=cv[:], in_=cvec), writes=[B_cv], dma=True)
            P.op("sp", lambda e: e.dma_start(out=bada_sb[:], in_=bada), writes=[B_bada], dma=True)
            P.op("sp", lambda e: e.dma_start(out=lbl_sb[:], in_=lbl), writes=[B_lbl], dma=True)
            P.op("act", lambda e: e.activation(out=cv[:], in_=cv[:], func=AF.Silu), reads=[B_cv], writes=[B_cv])
            for n in range(24):
                w = wa[n % 2]
                P.op("sp", lambda e, n=n, w=w: e.dma_start(
                    out=w[:], in_=w_ada[:, n * 512:(n + 1) * 512].rearrange("(c p) n -> p c n", p=128)),
                    writes=[B_wa[n % 2]], dma=True)
                for j in range(4):
                    for kc in range(KC):
                        P.op("pe", lambda e, n=n, j=j, kc=kc, w=w: e.matmul(
                            ps_mod[:, n * 4 + j, :], w[:, kc, j * 128:(j + 1) * 128], cv[:, kc, :],
                            start=(kc == 0), stop=(kc == KC - 1)),
                            reads=[B_wa[n % 2], B_cv], pwrites=[B_psmod])
            P.op("dve", lambda e: e.tensor_tensor(out=mod[:], in0=ps_mod[:], in1=bada_sb[:], op=ALU.add),
                 reads=[B_psmod, B_bada], writes=[B_mod])
            def mslice(m, w):
                return mod[:, m * 16:(m + 1) * 16, w]
            P.op("dve", lambda e: e.tensor_scalar(out=prm[:, 0, :], in0=mslice(1, 0), scalar1=1.0, scalar2=None, op0=ALU.add),
                 reads=[B_mod], pwrites=[B_prm])
            P.op("dve", lambda e: e.tensor_scalar(out=prm[:, 3, :], in0=mslice(4, 0), scalar1=1.0, scalar2=None, op0=ALU.add),
                 reads=[B_mod], pwrites=[B_prm])
            P.op("dve", lambda e: e.tensor_scalar(out=prm[:, 6, :], in0=mslice(1, 1), scalar1=1.0, scalar2=None, op0=ALU.add),
                 reads=[B_mod], pwrites=[B_prm])
            P.op("pool", lambda e: e.tensor_copy(out=prm[:, 1, :], in_=mslice(0, 0)), reads=[B_mod], pwrites=[B_prm])
            P.op("pool", lambda e: e.tensor_copy(out=prm[:, 2, :], in_=mslice(2, 0)), reads=[B_mod], pwrites=[B_prm])
            P.op("pool", lambda e: e.tensor_copy(out=prm[:, 4, :], in_=mslice(3, 0)), reads=[B_mod], pwrites=[B_prm])
            P.op("pool", lambda e: e.tensor_copy(out=prm[:, 5, :], in_=mslice(5, 0)), reads=[B_mod], pwrites=[B_prm])
            P.op("pool", lambda e: e.tensor_copy(out=prm[:, 7, :], in_=mslice(0, 1)), reads=[B_mod], pwrites=[B_prm])
            B_prm2 = Buf("prm2")
            P.op("dve", lambda e: e.tensor_tensor(out=prm[:, 0, :], in0=prm[:, 0, :], in1=gn[:, 0, :], op=ALU.mult),
                 reads=[B_prm, B_gn], pwrites=[B_prm2])
            P.op("dve", lambda e: e.tensor_tensor(out=prm[:, 3, :], in0=prm[:, 3, :], in1=gn[:, 1, :], op=ALU.mult),
                 reads=[B_prm, B_gn], pwrites=[B_prm2])
            P.op("dve", lambda e: e.tensor_tensor(out=prm[:, 6, :], in0=prm[:, 6, :], in1=gn[:, 0, :], op=ALU.mult),
                 reads=[B_prm, B_gn], pwrites=[B_prm2])
            B_PRM = Buf("PRM")
            P.op("dve", lambda e: e.tensor_copy(out=prm[:, 7, :], in_=prm[:, 7, :]), reads=[B_prm, B_prm2], writes=[B_PRM])
            P.op("dve", lambda e: e.tensor_tensor(out=lb[:], in0=lbl_sb[:, 0, :, :], in1=lbl_sb[:, 1, :, :], op=ALU.subtract),
                 reads=[B_lbl], writes=[B_lb])
            P.op("act", lambda e: e.activation(out=lb[:], in_=lb[:], func=AF.Sigmoid), reads=[B_lb], writes=[B_lb])
            P.op("dve", lambda e: e.tensor_scalar(out=oml[:], in0=lb[:], scalar1=-1.0, scalar2=1.0, op0=ALU.mult, op1=ALU.add),
                 reads=[B_lb], pwrites=[B_lb])
        P.barrier()

        cast_w(w_pa, wpa_s, B_wpa, 8, 512)
        cast_w(w_pb, wpb_s, B_wpb, 8, 512)
        cast_w(w_out, wo_s, B_wo, KC, 512)
        cast_w(w_fi, wfi_s, B_wfi, KC, 512)
        cast_w(w_fo, wfo_s, B_wfo, FKC, 128)

        with ExitStack() as ph:
            xs = [sbt(ph, f"xs{i}", [128, KC, T], F32) for i in range(2)]
            xsq = sbt(ph, "xsq", [128, KC, T], BF16)
            ho = [sbt(ph, f"ho{i}", [128, KC, T], BF16) for i in range(2)]
            rstd = sbt(ph, "rstd", [128, T], F32)
            ps_ss = pst(ph, "ps_ss", [128, T])
            B_xs, B_ho = [Buf(), Buf()], [Buf(), Buf()]
            B_xsq, B_rstd, B_ss = Buf(), Buf(), Buf()
            for it in range(NT_ALL + 1):
                nt = T if it < NT_ALL else CTX
                x, bx, h, bh = xs[it % 2], B_xs[it % 2], ho[it % 2], B_ho[it % 2]
                src = xT[:, it * T:(it + 1) * T] if it < NT_ALL else ctxT
                si, bi = (0, 1) if it < NT_ALL else (6, 7)
                P.op("sp", lambda e, x=x, src=src, nt=nt: e.dma_start(
                    out=x[:, :, 0:nt], in_=src.rearrange("(c p) t -> p c t", p=128)), writes=[bx], dma=True)
                P.op("act", lambda e, x=x, nt=nt: e.activation(out=xsq[:, :, 0:nt], in_=x[:, :, 0:nt], func=AF.Square),
                     reads=[bx], writes=[B_xsq])
                for c in range(KC):
                    P.op("pe", lambda e, c=c, nt=nt: e.matmul(ps_ss[:, 0:nt], ones_bf[:], xsq[:, c, 0:nt],
                                                               start=(c == 0), stop=(c == KC - 1)),
                         reads=[B_xsq, B_const], pwrites=[B_ss])
                P.op("act", lambda e, nt=nt: e.activation(out=rstd[:, 0:nt], in_=ps_ss[:, 0:nt], func=AF.Sqrt,
                                                           scale=1.0 / D, bias=EPS), reads=[B_ss], writes=[B_rstd])
                P.op("dve", lambda e, nt=nt: e.reciprocal(out=rstd[:, 0:nt], in_=rstd[:, 0:nt]), reads=[B_rstd], writes=[B_rstd])
                bx2 = Buf()
                for c in range(KC):
                    P.op("dve", lambda e, c=c, x=x, nt=nt: e.tensor_tensor(out=x[:, c, 0:nt], in0=x[:, c, 0:nt],
                                                                         in1=rstd[:, 0:nt], op=ALU.mult),
                         reads=[bx, B_rstd], pwrites=[bx2])
                for c in range(KC):
                    P.op("act", lambda e, c=c, x=x, h=h, nt=nt, si=si, bi=bi: e.activation(
                        out=h[:, c, 0:nt], in_=x[:, c, 0:nt], func=AF.Identity,
                        scale=prm[:, si, c:c + 1], bias=prm[:, bi, c:c + 1]),
                        reads=[bx2, B_PRM], pwrites=[bh])
                P.op("pool", lambda e, h=h, it=it, nt=nt: e.dma_start(out=hT_s[it][:, :, 0:nt], in_=h[:, :, 0:nt]),
                     reads=[bh], writes=[B_hT[it]], dma=True)
                P.op("act", lambda e, x=x: e.activation(out=rstd[:, 0:1], in_=rstd[:, 0:1], func=AF.Identity),
                     reads=[bx2, B_rstd], writes=[bx, B_rstd])
        P.barrier()

        with ExitStack() as ph:
            hb = [sbt(ph, f"hb{i}", [128, KC, T], BF16) for i in range(2)]
            wb = [sbt(ph, f"wb{i}", [128, KC, 512], BF16) for i in range(3)]
            st32 = [sbt(ph, f"st32_{i}", [128, 512], F32) for i in range(4)]
            st16 = [sbt(ph, f"st16_{i}", [128, 512], BF16) for i in range(4)]
            pg = [pst(ph, f"pg{i}", [128, 512]) for i in range(4)]
            B_hb, B_wb = [Buf(), Buf()], [Buf(), Buf(), Buf()]
            B_st32, B_st16, B_pg = [Buf() for _ in range(4)], [Buf() for _ in range(4)], [Buf() for _ in range(4)]
            cnt = {"w": 0, "p": 0, "s32": 0, "s16": 0, "alt": 0}

            def evac_store(kind, psrc, bpsrc, rows, ncols, dst_ap, dst_buf, arg=None):
                if kind == "lf":
                    d, h = arg
                    i32 = cnt["s32"] % 4
                    cnt["s32"] += 1
                    s, bs = st32[i32], B_st32[i32]
                    P.op("act", lambda e: e.activation(out=s[0:rows, 0:ncols], in_=psrc, func=AF.Sigmoid),
                         reads=[bpsrc], writes=[bs])
                    P.op("act", lambda e: e.activation(out=s[0:rows, 0:ncols], in_=s[0:rows, 0:ncols], func=AF.Ln,
                                                       scale=oml[:, d, h:h + 1], bias=lb[:, d, h:h + 1]),
                         reads=[bs, B_lb], writes=[bs])
                else:
                    i16 = cnt["s16"] % 4
                    cnt["s16"] += 1
                    s, bs = st16[i16], B_st16[i16]
                    if kind == "scale":
                        P.op("dve", lambda e: e.tensor_scalar(out=s[0:rows, 0:ncols], in0=psrc, scalar1=float(arg), scalar2=None,
                                                              op0=ALU.mult), reads=[bpsrc], writes=[bs])
                    elif kind == "copy":
                        cnt["alt"] += 1
                        if cnt["alt"] % 3 == 0:
                            P.op("act", lambda e: e.activation(out=s[0:rows, 0:ncols], in_=psrc, func=AF.Identity),
                                 reads=[bpsrc], writes=[bs])
                        else:
                            P.op("dve", lambda e: e.tensor_copy(out=s[0:rows, 0:ncols], in_=psrc), reads=[bpsrc], writes=[bs])
                    elif kind == "silu":
                        P.op("act", lambda e: e.activation(out=s[0:rows, 0:ncols], in_=psrc, func=AF.Silu),
                             reads=[bpsrc], writes=[bs])
                    elif kind == "sigmoid":
                        P.op("act", lambda e: e.activation(out=s[0:rows, 0:ncols], in_=psrc, func=AF.Sigmoid),
                             reads=[bpsrc], writes=[bs])
                    else:
                        raise ValueError(kind)
                P.op("pool", lambda e: e.dma_start(out=dst_ap, in_=s[0:rows, 0:ncols] if dst_ap_shape is None else
                                                   s[0:rows, 0:ncols].rearrange("p (a b) -> p a b", b=128)),
                     reads=[bs], pwrites=[dst_buf], dma=True)

            dst_ap_shape = None

            def load_w(n):
                i = cnt["w"] % 3
                cnt["w"] += 1
                P.op("sp", lambda e: e.dma_start(out=wb[i][:], in_=win_s[n]), reads=[B_win[n]], writes=[B_wb[i]], dma=True)
                return wb[i], B_wb[i]

            def fm_block(h, bh, nt, n, dests):
                w, bw = load_w(n)
                for j in range(4):
                    ip = cnt["p"] % 4
                    cnt["p"] += 1
                    for kc in range(KC):
                        P.op("pe", lambda e, kc=kc, j=j, ip=ip: e.matmul(pg[ip][:, 0:nt], w[:, kc, j * 128:(j + 1) * 128],
                                                                       h[:, kc, 0:nt], start=(kc == 0), stop=(kc == KC - 1)),
                             reads=[bw, bh], pwrites=[B_pg[ip]])
                    kind, dst_ap, dst_buf, arg = dests[j]
                    evac_store(kind, pg[ip][:, 0:nt], B_pg[ip], 128, nt, dst_ap, dst_buf, arg)

            def tm_block(h, bh, nt, n, dst_fn, dst_buf):
                nonlocal dst_ap_shape
                w, bw = load_w(n)
                for tt in range(nt // 128):
                    ip = cnt["p"] % 4
                    cnt["p"] += 1
                    for kc in range(KC):
                        P.op("pe", lambda e, kc=kc, tt=tt, ip=ip: e.matmul(pg[ip][:], h[:, kc, tt * 128:(tt + 1) * 128],
                                                                         w[:, kc, :], start=(kc == 0), stop=(kc == KC - 1)),
                             reads=[bw, bh], pwrites=[B_pg[ip]])
                    dst_ap_shape = 3
                    evac_store("copy", pg[ip][:], B_pg[ip], 128, 512, dst_fn(tt), dst_buf)
                    dst_ap_shape = None

            order = list(range(NT_OWN)) + [NT_ALL] + list(range(NT_OWN, NT_ALL))
            for ii, it in enumerate(order):
                nt = T if it < NT_ALL else CTX
                h, bh = hb[ii % 2], B_hb[ii % 2]
                P.op("sp", lambda e, h=h, it=it, nt=nt: e.dma_start(out=h[:, :, 0:nt], in_=hT_s[it][:, :, 0:nt]),
                     reads=[B_hT[it]], writes=[bh], dma=True)
                own = it < NT_OWN
                ctx = it == NT_ALL
                halo = it == NT_OWN
                t0 = it * T
                if own:
                    for n in (0, 1):
                        fm_block(h, bh, nt, n, [("scale", qT_s[n * 4 + j][:, t0:t0 + nt], Bs["qT"], 0.125) for j in range(4)])
                if own or halo or ctx:
                    for n in (2, 3):
                        if ctx:
                            dd = [("copy", kcT_s[(n - 2) * 4 + j][:, 0:nt], Bs["kcT"], None) for j in range(4)]
                        else:
                            dd = [("copy", kT_s[(n - 2) * 4 + j][:, t0:t0 + nt], Bs["kT"], None) for j in range(4)]
                        fm_block(h, bh, nt, n, dd)
                    for n in (4, 5):
                        hp0 = (n - 4) * 4
                        if ctx:
                            tm_block(h, bh, nt, n, lambda tt, hp0=hp0: vc_s[hp0:hp0 + 4, :, tt, :].rearrange("a p b -> p a b"), Bs["vc"])
                        else:
                            tm_block(h, bh, nt, n, lambda tt, hp0=hp0, it=it: v_s[hp0:hp0 + 4, :, it * 4 + tt, :].rearrange("a p b -> p a b"), Bs["v"])
                if own:
                    for n in (6, 7):
                        fm_block(h, bh, nt, n, [("copy", qhT_s[(n - 6) * 4 + j][:, t0:t0 + nt], Bs["qhT"], None) for j in range(4)])
                if own or ctx:
                    for n in (8, 9):
                        if ctx:
                            dd = [("lf", lfAc_s[(n - 8) * 4 + j][:, 0:nt], Bs["lfAc"], (0, (n - 8) * 4 + j)) for j in range(4)]
                        else:
                            dd = [("lf", lfA_s[(n - 8) * 4 + j][:, t0:t0 + nt], Bs["lfA"], (0, (n - 8) * 4 + j)) for j in range(4)]
                        fm_block(h, bh, nt, n, dd)
                for n in (10, 11):
                    if ctx:
                        dd = [("lf", lfBc_s[(n - 10) * 4 + j][:, 0:nt], Bs["lfBc"], (1, (n - 10) * 4 + j)) for j in range(4)]
                    else:
                        dd = [("lf", lfB_s[(n - 10) * 4 + j][:, t0:t0 + nt], Bs["lfB"], (1, (n - 10) * 4 + j)) for j in range(4)]
                    fm_block(h, bh, nt, n, dd)
                for n in (12, 13):
                    hp0 = (n - 12) * 4
                    if ctx:
                        tm_block(h, bh, nt, n, lambda tt, hp0=hp0: vhc_s[hp0:hp0 + 4, :, tt, :].rearrange("a p b -> p a b"), Bs["vhc"])
                    else:
                        tm_block(h, bh, nt, n, lambda tt, hp0=hp0, it=it: vh_s[hp0:hp0 + 4, :, it * 4 + tt, :].rearrange("a p b -> p a b"), Bs["vh"])
                if own:
                    for n in (14, 15):
                        fm_block(h, bh, nt, n, [("silu", hogT_s[(n - 14) * 4 + j][:, t0:t0 + nt], Bs["hogT"], None) for j in range(4)])
                    for n in range(16, 20):
                        fm_block(h, bh, nt, n, [("sigmoid", gaT_s[(n - 16) * 4 + j][:, t0:t0 + nt], Bs["gaT"], None) for j in range(4)])
                    for n in range(20, 24):
                        fm_block(h, bh, nt, n, [("sigmoid", gbT_s[(n - 20) * 4 + j][:, t0:t0 + nt], Bs["gbT"], None) for j in range(4)])
        P.barrier()

        STOP = os.environ.get("KSTOP", "")
        if STOP == "gemm1":
            return _finish(nc, P, top, outT)

        with ExitStack() as ph:
            NKT = 34
            k_sb = [sbt(ph, f"k_sb{i}", [128, NKT * 128], BF16) for i in range(2)]
            kc_sb = [sbt(ph, f"kc_sb{i}", [128, CTX], BF16) for i in range(2)]
            v_sb = [sbt(ph, f"v_sb{i}", [128, NKT + 2, 128], BF16) for i in range(2)]
            qm = [[sbt(ph, f"qm{i}_{hh}", [128, OWN], BF16) for hh in range(2)] for i in range(2)]
            vm = [sbt(ph, f"vm{hh}", [128, NKT + 2, 128], BF16) for hh in range(2)]
            onesm = [sbt(ph, f"onesm{hh}", [128, 128], BF16) for hh in range(2)]
            bt_sb = [sbt(ph, f"bt_sb{i}", [128, 3, 2, 5, 128], BF16) for i in range(2)]
            pT = [sbt(ph, f"pT{i}", [128, 7, 128], BF16) for i in range(2)]
            rinv = [sbt(ph, f"rinv{i}", [128, 128], F32) for i in range(2)]
            o_sb = [sbt(ph, f"o_sb{i}", [128, OWN], BF16) for i in range(2)]
            psS = [[pst(ph, f"psS{i}a", [128, 4, 128]), pst(ph, f"psS{i}b", [128, 4, 128])] for i in range(2)]
            psO = [pst(ph, f"psO{i}", [128, 512])[:, 0:128] for i in range(2)]
            psR = [pst(ph, f"psR{i}", [128, 512])[:, 0:128] for i in range(2)]
            B_k, B_kc, B_v, B_bt, B_osb = [[Buf(), Buf()] for _ in range(5)]
            B_qm = [[Buf(), Buf()], [Buf(), Buf()]]
            B_vm, B_pT, B_rinv = [Buf(), Buf()], [Buf(), Buf()], [Buf(), Buf()]
            B_psS, B_psO, B_psR = [Buf(), Buf()], [Buf(), Buf()], [Buf(), Buf()]
            B_om = Buf()
            for i in range(2):
                for hh in range(2):
                    P.op("pool", lambda e, i=i, hh=hh: e.memset(qm[i][hh][:], 0.0), writes=[B_qm[i][hh]])
            for hh in range(2):
                P.op("pool", lambda e, hh=hh: e.memset(vm[hh][:], 0.0), writes=[B_vm[hh]])
                P.op("pool", lambda e, hh=hh: e.memset(onesm[hh][:], 0.0), pwrites=[B_om])
            B_om2 = Buf()
            for hh in range(2):
                P.op("pool", lambda e, hh=hh: e.memset(onesm[hh][:, hh * 64:(hh + 1) * 64], 1.0), reads=[B_om], pwrites=[B_om2])
            sidx = 0
            for hp in range(8):
                i = hp % 2
                P.op("sp", lambda e, hp=hp, i=i: e.dma_start(out=k_sb[i][:], in_=kT_s[hp][:, 0:NKT * 128]),
                     reads=[Bs["kT"]], writes=[B_k[i]], dma=True)
                P.op("sp", lambda e, hp=hp, i=i: e.dma_start(out=kc_sb[i][:], in_=kcT_s[hp]),
                     reads=[Bs["kcT"]], writes=[B_kc[i]], dma=True)
                P.op("sp", lambda e, hp=hp, i=i: e.dma_start(out=v_sb[i][:, 0:NKT, :], in_=v_s[hp][:, 0:NKT, :]),
                     reads=[Bs["v"]], writes=[B_v[i]], dma=True)
                P.op("sp", lambda e, hp=hp, i=i: e.dma_start(out=v_sb[i][:, NKT:NKT + 2, :], in_=vc_s[hp]),
                     reads=[Bs["vc"]], pwrites=[B_v[i]], dma=True)
                for hh in range(2):
                    P.op("sp", lambda e, hp=hp, i=i, hh=hh: e.dma_start(
                        out=qm[i][hh][hh * 64:(hh + 1) * 64, :], in_=qT_s[hp][hh * 64:(hh + 1) * 64, :]),
                        reads=[Bs["qT"]], writes=[B_qm[i][hh]], dma=True)
                P.op("pool", lambda e, hp=hp, i=i: e.dma_start(
                    out=bt_sb[i][:].rearrange("p a b c d -> p a (b c d)"), in_=btab[hp].rearrange("p (a x) -> p a x", a=3)),
                    writes=[B_bt[i]], dma=True)
                for hh in range(2):
                    P.op("pool", lambda e, i=i, hh=hh: e.tensor_copy(out=vm[hh][:, :, hh * 64:(hh + 1) * 64],
                                                                    in_=v_sb[i][:, :, hh * 64:(hh + 1) * 64]),
                         reads=[B_v[i]], writes=[B_vm[hh]])
                for R in range(32):
                    cls = min(R, 2)
                    bs_t = max(R - 2, 0)
                    io = R % 2
                    for hh in range(2):
                        si = sidx % 2
                        sidx += 1
                        pa_, pb_ = psS[si]
                        q_ap = qm[i][hh][:, R * 128:(R + 1) * 128]
                        for t in range(5):
                            dst = pa_[:, t, :] if t < 4 else pb_[:, 0, :]
                            P.op("pe", lambda e, dst=dst, t=t, q_ap=q_ap: e.matmul(
                                dst, k_sb[i][:, (bs_t + t) * 128:(bs_t + t + 1) * 128], q_ap, start=True, stop=False),
                                reads=[B_k[i], B_qm[i][hh]], pwrites=[B_psS[si]])
                            P.op("pe", lambda e, dst=dst, t=t, hh=hh: e.matmul(
                                dst, bt_sb[i][:, cls, hh, t, :], ident[:], start=False, stop=True),
                                reads=[B_bt[i], B_const], pwrites=[B_psS[si]])
                        for t in range(2):
                            dst = pb_[:, 1 + t, :]
                            P.op("pe", lambda e, dst=dst, t=t, q_ap=q_ap: e.matmul(
                                dst, kc_sb[i][:, t * 128:(t + 1) * 128], q_ap, start=True, stop=True),
                                reads=[B_kc[i], B_qm[i][hh]], pwrites=[B_psS[si]])
                        P.op("act", lambda e, pa_=pa_, hh=hh: e.activation(out=pT[hh][:, 0:4, :], in_=pa_[:], func=AF.Exp),
                             reads=[B_psS[si]], pwrites=[B_pT[hh]])
                        P.op("act", lambda e, pb_=pb_, hh=hh: e.activation(out=pT[hh][:, 4:7, :], in_=pb_[:, 0:3, :], func=AF.Exp),
                             reads=[B_psS[si]], pwrites=[B_pT[hh]])
                        for t in range(7):
                            vt = (bs_t + t) if t < 5 else (NKT + t - 5)
                            first = (hh == 0 and t == 0)
                            last = (hh == 1 and t == 6)
                            P.op("pe", lambda e, t=t, vt=vt, hh=hh, first=first, last=last: e.matmul(
                                psO[io][:], vm[hh][:, vt, :], pT[hh][:, t, :], start=first, stop=last),
                                reads=[B_vm[hh], B_pT[hh]], pwrites=[B_psO[io]])
                            P.op("pe", lambda e, t=t, hh=hh, first=first, last=last: e.matmul(
                                psR[io][:], onesm[hh][:], pT[hh][:, t, :], start=first, stop=last),
                                reads=[B_om2, B_pT[hh]], pwrites=[B_psR[io]])
                    P.op("dve", lambda e, io=io: e.reciprocal(out=rinv[io][:], in_=psR[io][:]), reads=[B_psR[io]], writes=[B_rinv[io]])
                    P.op("dve", lambda e, io=io, R=R, i=i: e.tensor_tensor(out=o_sb[i][:, R * 128:(R + 1) * 128], in0=psO[io][:],
                                                                      in1=rinv[io][:], op=ALU.mult),
                         reads=[B_psO[io], B_rinv[io]], pwrites=[B_osb[i]])
                P.op("pool", lambda e, hp=hp, i=i: e.dma_start(out=onaT_s[hp], in_=o_sb[i][:]),
                     reads=[B_osb[i]], pwrites=[Bs["onaT"]], dma=True)
        P.barrier()
        if STOP == "na":
            return _finish(nc, P, top, outT)

        with ExitStack() as ph:
            NH = 8
            rstA = sbt(ph, "rstA", [128, T], F32)
            rstB = sbt(ph, "rstB", [128, T], F32)
            tl = [{n: sbt(ph, f"t_{n}{i}", [128, T], F32) for n in ("lf", "cum", "b", "ek", "kk", "kt32")} for i in range(2)]
            negm = [sbt(ph, f"negm{i}", [128, 8], F32) for i in range(2)]
            qb = [[sbt(ph, f"qb{p}_{h}", [128, T], BF16) for h in range(NH)] for p in range(2)]
            tkh = [sbt(ph, f"t_khT{i}", [128, T], BF16) for i in range(2)]
            tq = [sbt(ph, f"t_q{i}", [128, T], BF16) for i in range(2)]
            B_tl = [Buf(), Buf()]
            qt = [[sbt(ph, f"qt{p}_{h}", [128, T], BF16) for h in range(NH)] for p in range(2)]
            kt = [[sbt(ph, f"kt{p}_{h}", [128, T], BF16) for h in range(NH)] for p in range(2)]
            kh = [[sbt(ph, f"kh{p}_{h}", [128, 4, 128], BF16) for h in range(NH)] for p in range(2)]
            vv = [[sbt(ph, f"vv{p}_{h}", [128, 4, 128], BF16) for h in range(NH)] for p in range(2)]
            ee = [[sbt(ph, f"ee{p}_{h}", [128, 8], F32) for h in range(NH)] for p in range(2)]
            B_ops = [[Buf() for h in range(NH)] for p in range(2)]
            S32 = [sbt(ph, f"S32_{h}", [128, 128], F32) for h in range(NH)]
            Sbf = [sbt(ph, f"Sbf_{h}", [128, 128], BF16) for h in range(NH)]
            AT = [sbt(ph, f"AT_{h}", [128, 128], BF16) for h in range(NH)]
            ATB = [sbt(ph, f"ATB_{h}", [128, 128], BF16) for h in range(NH)]
            B_S32, B_Sbf, B_AT = [Buf() for _ in range(NH)], [Buf() for _ in range(NH)], [Buf() for _ in range(NH)]
            ost = [sbt(ph, f"ost{i}", [128, T], F32) for i in range(8)]
            o1b = [sbt(ph, f"o1b{i}", [128, T], F32) for i in range(8)]
            hogb = [sbt(ph, f"hogb{i}", [128, T], BF16) for i in range(8)]
            sqb = [sbt(ph, f"sqb{i}", [128, T], BF16) for i in range(2)]
            rsb = [sbt(ph, f"rsb{i}", [128, T], F32) for i in range(2)]
            outb = [sbt(ph, f"outb{i}", [128, T], BF16) for i in range(2)]
            B_ost, B_o1b, B_hogb = [Buf() for _ in range(8)], [Buf() for _ in range(8)], [Buf() for _ in range(8)]
            B_sqb, B_rsb, B_outb = [Buf(), Buf()], [Buf(), Buf()], [Buf(), Buf()]
            psA = [pst(ph, f"psA{i}", [128, 512])[:, 0:128] for i in range(2)]
            psOo = [pst(ph, f"psOo{i}", [128, 512])[:, 0:128] for i in range(2)]
            psX = [pst(ph, f"psX{i}", [128, 512])[:, 0:128] for i in range(2)]
            psT = pst(ph, "psT", [128, 8, 128], BF16)
            psN = pst(ph, "psN", [128, T])
            B_psA, B_psOo, B_psX = [Buf(), Buf()], [Buf(), Buf()], [Buf(), Buf()]
            B_psT, B_psN = Buf(), Buf()
            B_rst = Buf()
            P.op("pool", lambda e: e.memset(rstA[:], 1.0), pwrites=[B_rst])
            P.op("pool", lambda e: e.memset(rstB[:], 1.0), pwrites=[B_rst])
            B_rst2 = Buf()
            P.op("pool", lambda e: e.memset(rstA[:].rearrange("p (c t) -> p c t", t=64)[:, :, 0:1], 0.0), reads=[B_rst], pwrites=[B_rst2])
            P.op("pool", lambda e: e.memset(rstB[:].rearrange("p (c t) -> p c t", t=64)[:, :, 63:64], 0.0), reads=[B_rst], pwrites=[B_rst2])
            for h in range(NH):
                P.op("pool", lambda e, h=h: e.memset(AT[h][:], 0.0), writes=[B_AT[h]])
                P.op("pool", lambda e, h=h: e.memset(ATB[h][:], 0.0), writes=[B_AT[h]])
            cn = {"tl": 0, "a": 0, "o": 0, "x": 0, "ost": 0, "ro": 0}

            def reset_states():
                for h in range(NH):
                    P.op("pool", lambda e, h=h: e.memset(S32[h][:], 0.0), writes=[B_S32[h]])
                    P.op("pool", lambda e, h=h: e.memset(Sbf[h][:], 0.0), writes=[B_Sbf[h]])

            def hg_tile(dirB, step, src_lf, b_lf, tok0, nt, src_v, b_v, vt0, own_t0):
                par = step % 2
                nblk = nt // 128
                nch = nt // 64
                with_out = own_t0 is not None
                rst = rstB if dirB else rstA
                rv = (lambda a: a[:, ::-1]) if dirB else (lambda a: a)
                ostb = {}
                for h in range(NH):
                    ti = cn["tl"] % 2
                    cn["tl"] += 1
                    t_, bt_ = tl[ti], B_tl[ti]
                    bo = B_ops[par][h]
                    P.op("sp", lambda e, h=h, t_=t_: e.dma_start(out=t_["lf"][:, 0:nt], in_=src_lf[h][:, tok0:tok0 + nt]),
                         reads=[b_lf], writes=[bt_], dma=True)
                    P.op("sp", lambda e, h=h: e.dma_start(out=vv[par][h][:, 0:nblk, :], in_=src_v[h][:, vt0:vt0 + nblk, :]),
                         reads=[b_v], writes=[bo], dma=True)
                    if with_out:
                        P.op("sp", lambda e, h=h, ti=ti: e.dma_start(out=tq[ti][:, 0:nt], in_=qhT_s[h][:, own_t0:own_t0 + nt]),
                             reads=[Bs["qhT"]], pwrites=[bt_], dma=True)
                    c1, c2, c3, c4 = Buf(), Buf(), Buf(), Buf()
                    P.op("dve", lambda e, t_=t_: e.tensor_tensor_scan(
                        out=rv(t_["cum"][:, 0:nt]), data0=rv(rst[:, 0:nt]), data1=rv(t_["lf"][:, 0:nt]), initial=0.0,
                        op0=ALU.mult, op1=ALU.add), reads=[bt_, B_rst2], writes=[c1])
                    P.op("act", lambda e, t_=t_: e.activation(out=t_["b"][:, 0:nt], in_=t_["cum"][:, 0:nt], func=AF.Exp),
                         reads=[c1], pwrites=[c2])
                    P.op("act", lambda e, t_=t_: e.activation(out=t_["kk"][:, 0:nt], in_=t_["lf"][:, 0:nt], func=AF.Exp),
                         reads=[bt_], pwrites=[c2])
                    P.op("pool", lambda e, t_=t_: e.tensor_scalar(out=t_["kk"][:, 0:nt], in0=t_["kk"][:, 0:nt], scalar1=-1.0, scalar2=1.0,
                                                                 op0=ALU.mult, op1=ALU.add), reads=[c2], writes=[c3])
                    edge = 0 if dirB else 63
                    MID = 31
                    cumv = t_["cum"][:, 0:nt].rearrange("p (c t) -> p c t", t=64)
                    P.op("pool", lambda e, t_=t_, ti=ti: e.tensor_scalar(out=negm[ti][:, 0:nch], in0=cumv[:, :, MID], scalar1=-1.0, scalar2=None,
                                                                        op0=ALU.mult), reads=[c1], writes=[c4])
                    P.op("pool", lambda e, t_=t_, h=h: e.tensor_copy(
                        out=ee[par][h][:, 0:nch], in_=t_["b"][:, 0:nt].rearrange("p (c t) -> p c t", t=64)[:, :, edge]),
                        reads=[c2], pwrites=[bo])
                    c6_ = Buf()
                    for c in range(nch):
                        cs = slice(c * 64, (c + 1) * 64)
                        P.op("act", lambda e, t_=t_, c=c, cs=cs: e.activation(out=t_["ek"][:, cs], in_=t_["cum"][:, cs], func=AF.Exp, scale=-1.0,
                                                                           bias=t_["cum"][:, c * 64 + edge:c * 64 + edge + 1]),
                             reads=[c1], pwrites=[c6_])
                        if with_out:
                            P.op("act", lambda e, t_=t_, c=c, cs=cs: e.activation(out=t_["kt32"][:, cs], in_=t_["cum"][:, cs], func=AF.Exp, scale=-1.0,
                                                                               bias=t_["cum"][:, c * 64 + MID:c * 64 + MID + 1]),
                                 reads=[c1], pwrites=[c6_])
                            P.op("act", lambda e, t_=t_, c=c, cs=cs, ti=ti: e.activation(out=t_["lf"][:, cs], in_=t_["cum"][:, cs], func=AF.Exp, scale=1.0,
                                                                                      bias=negm[ti][:, c:c + 1]),
                                 reads=[c1, c4, c2], pwrites=[c6_])
                    c5 = Buf()
                    P.op("dve", lambda e, t_=t_, ti=ti: e.tensor_tensor(out=tkh[ti][:, 0:nt], in0=t_["kk"][:, 0:nt], in1=t_["ek"][:, 0:nt], op=ALU.mult),
                         reads=[c3, c6_], writes=[c5])
                    if with_out:
                        P.op("pool", lambda e, t_=t_, h=h, ti=ti: e.tensor_tensor(out=qb[par][h][:, 0:nt], in0=tq[ti][:, 0:nt],
                                                                                 in1=t_["b"][:, 0:nt], op=ALU.mult),
                             reads=[bt_, c2], pwrites=[bo])
                        P.op("pool", lambda e, t_=t_, h=h, ti=ti: e.tensor_tensor(out=qt[par][h][:, 0:nt], in0=tq[ti][:, 0:nt],
                                                                                 in1=t_["lf"][:, 0:nt], op=ALU.mult),
                             reads=[bt_, c6_], pwrites=[bo])
                        P.op("pool", lambda e, t_=t_, h=h: e.tensor_tensor(out=kt[par][h][:, 0:nt], in0=t_["kk"][:, 0:nt],
                                                                          in1=t_["kt32"][:, 0:nt], op=ALU.mult),
                             reads=[c3, c6_], pwrites=[bo])
                    for bk in range(nblk):
                        P.op("pe", lambda e, bk=bk, ti=ti: e.transpose(psT[:, bk, :], tkh[ti][:, bk * 128:(bk + 1) * 128], ident[:]),
                             reads=[c5, B_const], pwrites=[B_psT])
                    P.op("dve", lambda e, h=h: e.tensor_copy(out=kh[par][h][:, 0:nblk, :], in_=psT[:, 0:nblk, :]),
                         reads=[B_psT], pwrites=[bo])
                    P.op("pool", lambda e, t_=t_: e.memset(t_["cum"][:, 0:1], 0.0), reads=[c1, c2, c3, c4, c5, c6_], writes=[bt_])
                    if with_out:
                        oi = h
                        ostb[h] = oi
                        if dirB:
                            P.op("sp", lambda e, h=h, oi=oi: e.dma_start(out=o1b[oi][:, 0:nt], in_=o1T_s[h][:, own_t0:own_t0 + nt]),
                                 reads=[Bs["o1T"]], writes=[B_o1b[oi]], dma=True)
                            P.op("sp", lambda e, h=h, oi=oi: e.dma_start(out=hogb[oi][:, 0:nt], in_=hogT_s[h][:, own_t0:own_t0 + nt]),
                                 reads=[Bs["hogT"]], writes=[B_hogb[oi]], dma=True)
                blks = range(nblk - 1, -1, -1) if dirB else range(nblk)
                chs = (1, 0) if dirB else (0, 1)
                for bk in blks:
                    for h in range(NH):
                        bo = B_ops[par][h]
                        if with_out:
                            ia = cn["a"] % 2
                            cn["a"] += 1
                            io = cn["o"] % 2
                            cn["o"] += 1
                            P.op("pe", lambda e, h=h, bk=bk, ia=ia: e.matmul(
                                psA[ia][:], kt[par][h][:, bk * 128:(bk + 1) * 128], qt[par][h][:, bk * 128:(bk + 1) * 128],
                                start=True, stop=True), reads=[bo], writes=[B_psA[ia]])
                            P.op("dve", lambda e, h=h, ia=ia: e.copy_predicated(
                                out=(ATB if dirB else AT)[h][:], mask=mask_sb[:, 1 if dirB else 0, :], data=psA[ia][:]),
                                reads=[B_psA[ia], B_const], writes=[B_AT[h]])
                            P.op("pe", lambda e, h=h, bk=bk, io=io: e.matmul(
                                psOo[io][:], vv[par][h][:, bk, :], (ATB if dirB else AT)[h][:], start=True, stop=False),
                                reads=[bo, B_AT[h]], pwrites=[B_psOo[io]])
                        for ci, c in enumerate(chs):
                            gch = bk * 2 + c
                            if with_out:
                                P.op("pe", lambda e, h=h, bk=bk, c=c, io=io, ci=ci: e.matmul(
                                    psOo[io][:, c * 64:(c + 1) * 64], Sbf[h][:],
                                    qb[par][h][:, bk * 128 + c * 64:bk * 128 + (c + 1) * 64], start=False, stop=(ci == 1)),
                                    reads=[bo, B_Sbf[h]], pwrites=[B_psOo[io]])
                            ix = cn["x"] % 2
                            cn["x"] += 1
                            P.op("pe", lambda e, h=h, bk=bk, c=c, ix=ix: e.matmul(
                                psX[ix][:], kh[par][h][c * 64:(c + 1) * 64, bk, :], vv[par][h][c * 64:(c + 1) * 64, bk, :],
                                start=True, stop=True), reads=[bo], writes=[B_psX[ix]])
                            P.op("dve", lambda e, h=h, gch=gch, ix=ix: e.scalar_tensor_tensor(
                                out=S32[h][:], in0=S32[h][:], scalar=ee[par][h][:, gch:gch + 1], in1=psX[ix][:],
                                op0=ALU.mult, op1=ALU.add), reads=[B_psX[ix], bo, B_S32[h]], writes=[B_S32[h]])
                            P.op("act", lambda e, h=h: e.activation(out=Sbf[h][:], in_=S32[h][:], func=AF.Identity),
                                 reads=[B_S32[h]], writes=[B_Sbf[h]])
                        if with_out:
                            oi = ostb[h]
                            if dirB:
                                P.op("dve", lambda e, bk=bk, io=io, oi=oi: e.tensor_tensor(
                                    out=ost[oi][:, bk * 128:(bk + 1) * 128], in0=psOo[io][:], in1=o1b[oi][:, bk * 128:(bk + 1) * 128],
                                    op=ALU.add), reads=[B_psOo[io], B_o1b[oi]], pwrites=[B_ost[oi]])
                            else:
                                P.op("act", lambda e, bk=bk, io=io, oi=oi: e.activation(
                                    out=ost[oi][:, bk * 128:(bk + 1) * 128], in_=psOo[io][:], func=AF.Identity),
                                    reads=[B_psOo[io]], pwrites=[B_ost[oi]])
                if with_out:
                    for h in range(NH):
                        oi = ostb[h]
                        if not dirB:
                            P.op("pool", lambda e, h=h, oi=oi: e.dma_start(out=o1T_s[h][:, own_t0:own_t0 + nt], in_=ost[oi][:, 0:nt]),
                                 reads=[B_ost[oi]], pwrites=[Bs["o1T"]], dma=True)
                        else:
                            ri = cn["ro"] % 2
                            cn["ro"] += 1
                            P.op("act", lambda e, oi=oi, ri=ri: e.activation(out=sqb[ri][:, 0:nt], in_=ost[oi][:, 0:nt], func=AF.Square),
                                 reads=[B_ost[oi]], writes=[B_sqb[ri]])
                            P.op("pe", lambda e, ri=ri: e.matmul(psN[:, 0:nt], ones_bf[:], sqb[ri][:, 0:nt], start=True, stop=True),
                                 reads=[B_sqb[ri], B_const], writes=[B_psN])
                            P.op("act", lambda e, ri=ri: e.activation(out=rsb[ri][:, 0:nt], in_=psN[:, 0:nt], func=AF.Sqrt,
                                                                       scale=1.0 / 128, bias=EPS), reads=[B_psN], writes=[B_rsb[ri]])
                            P.op("dve", lambda e, ri=ri: e.reciprocal(out=rsb[ri][:, 0:nt], in_=rsb[ri][:, 0:nt]),
                                 reads=[B_rsb[ri]], writes=[B_rsb[ri]])
                            P.op("dve", lambda e, ri=ri, oi=oi: e.tensor_tensor(out=rsb[ri][:, 0:nt], in0=rsb[ri][:, 0:nt],
                                                                              in1=ost[oi][:, 0:nt], op=ALU.mult),
                                 reads=[B_rsb[ri], B_ost[oi]], writes=[B_rsb[ri]])
                            P.op("dve", lambda e, ri=ri, oi=oi: e.scalar_tensor_tensor(
                                out=outb[ri][:, 0:nt], in0=rsb[ri][:, 0:nt], scalar=hgg_sb[:, 0:1], in1=hogb[oi][:, 0:nt],
                                op0=ALU.mult, op1=ALU.mult), reads=[B_rsb[ri], B_hogb[oi], B_gn], writes=[B_outb[ri]])
                            P.op("pool", lambda e, h=h, ri=ri: e.dma_start(out=ohgT_s[h][:, own_t0:own_t0 + nt], in_=outb[ri][:, 0:nt]),
                                 reads=[B_outb[ri]], pwrites=[Bs["ohgT"]], dma=True)

            lfA_v = [lfA_s[h] for h in range(8)]
            lfAc_v = [lfAc_s[h] for h in range(8)]
            lfB_v = [lfB_s[h] for h in range(8)]
            lfBc_v = [lfBc_s[h] for h in range(8)]
            vh_v = [vh_s[h] for h in range(8)]
            vhc_v = [vhc_s[h] for h in range(8)]
            step = 0
            reset_states()
            hg_tile(False, step, lfAc_v, Bs["lfAc"], 0, CTX, vhc_v, Bs["vhc"], 0, None)
            step += 1
            for it in range(NT_OWN):
                hg_tile(False, step, lfA_v, Bs["lfA"], it * T, T, vh_v, Bs["vh"], it * 4, it * T)
                step += 1
            reset_states()
            hg_tile(True, step, lfBc_v, Bs["lfBc"], 0, CTX, vhc_v, Bs["vhc"], 0, None)
            step += 1
            for it in range(NT_ALL - 1, -1, -1):
                hg_tile(True, step, lfB_v, Bs["lfB"], it * T, T, vh_v, Bs["vh"], it * 4, (it * T) if it < NT_OWN else None)
                step += 1
        P.barrier()
        if STOP == "hg":
            return _finish(nc, P, top, outT)

        with ExitStack() as ph:
            xx = sbt(ph, "xx", [128, KC, T], F32)
            hid = sbt(ph, "hid", [128, FKC, T], BF16)
            ona = hid[:, 28:36, :]
            ohg = hid[:, 36:44, :]
            sq = hid[:, 0:16, :]
            ga = sbt(ph, "ga", [128, 4, T], BF16)
            gb = sbt(ph, "gb", [128, 4, T], BF16)
            mm_ = sbt(ph, "mm_", [128, KC, T], BF16)
            h2 = mm_
            t1 = [sbt(ph, f"t1_{i}", [128, T], F32) for i in range(2)]
            t2 = [sbt(ph, f"t2_{i}", [128, T], F32) for i in range(2)]
            rs = sbt(ph, "rs", [128, T], F32)
            wq = [sbt(ph, f"wq{i}", [128, KC, 512], BF16) for i in range(3)]
            wfo_b = [sbt(ph, f"wfo_b{i}", [128, FKC // 2, 128], BF16) for i in range(2)]
            pp = [pst(ph, f"pp{i}", [128, T]) for i in range(6)]
            psn = pst(ph, "psn", [128, T])
            B_xx, B_ga, B_gb, B_mm, B_hid, B_rs = [Buf() for _ in range(6)]
            B_ona = B_ohg = B_sq = B_hid
            B_h2 = B_mm
            B_t1, B_t2 = [Buf(), Buf()], [Buf(), Buf()]
            B_wq, B_wfob = [Buf(), Buf(), Buf()], [Buf(), Buf()]
            B_pp, B_psn = [Buf() for _ in range(6)], Buf()
            c6 = {"w": 0, "p": 0, "t": 0, "f": 0}
            out_ops = []

            def ldw(src, bsrc, kcn):
                i = c6["w"] % 3
                c6["w"] += 1
                P.op("sp", lambda e: e.dma_start(out=wq[i][:, 0:kcn, :], in_=src), reads=[bsrc], writes=[B_wq[i]], dma=True)
                return wq[i], B_wq[i]

            def nextp():
                i = c6["p"] % 6
                c6["p"] += 1
                return pp[i], B_pp[i]

            def norm_to(src, bsrc, dst, bdst, si, bi, final=False, it=None):
                P.op("act", lambda e: e.activation(out=sq, in_=src[:], func=AF.Square), reads=[bsrc], writes=[B_sq])
                for c in range(KC):
                    P.op("pe", lambda e, c=c: e.matmul(psn[:], ones_bf[:], sq[:, c, :], start=(c == 0), stop=(c == KC - 1)),
                         reads=[B_sq, B_const], pwrites=[B_psn])
                P.op("act", lambda e: e.activation(out=rs[:], in_=psn[:], func=AF.Sqrt, scale=1.0 / D, bias=EPS),
                     reads=[B_psn], writes=[B_rs])
                P.op("dve", lambda e: e.reciprocal(out=rs[:], in_=rs[:]), reads=[B_rs], writes=[B_rs])
                for c in range(KC):
                    if not final:
                        i = c6["t"] % 2
                        c6["t"] += 1
                        P.op("dve", lambda e, c=c, i=i: e.tensor_tensor(out=t1[i][:], in0=src[:, c, :], in1=rs[:], op=ALU.mult),
                             reads=[bsrc, B_rs], writes=[B_t1[i]])
                        P.op("act", lambda e, c=c, i=i: e.activation(out=dst[:, c, :], in_=t1[i][:], func=AF.Identity,
                                                                   scale=prm[:, si, c:c + 1], bias=prm[:, bi, c:c + 1]),
                             reads=[B_t1[i], B_PRM], pwrites=[bdst])
                    else:
                        i = c6["t"] % 2
                        c6["t"] += 1
                        P.op("dve", lambda e, c=c, i=i: e.scalar_tensor_tensor(
                            out=t1[i][:], in0=src[:, c, :], scalar=gn[:, 2, c:c + 1], in1=rs[:], op0=ALU.mult, op1=ALU.mult),
                            reads=[bsrc, B_rs, B_gn], writes=[B_t1[i]])
                        out_ops.append(P.op("pool", lambda e, c=c, i=i: e.dma_start(
                            out=outT[c * 128:(c + 1) * 128, it * T:(it + 1) * T], in_=t1[i][:]), reads=[B_t1[i]], dma=True))

            for it in range(NT_OWN):
                t0 = it * T
                sl = slice(t0, t0 + T)
                P.op("sp", lambda e, sl=sl: e.dma_start(out=xx[:], in_=xT[:, sl].rearrange("(c p) t -> p c t", p=128)),
                     writes=[B_xx], dma=True)
                P.op("sp", lambda e, sl=sl: e.dma_start(out=ona, in_=onaT_s[:, :, sl].rearrange("c p t -> p c t")),
                     reads=[Bs["onaT"]], writes=[B_ona], dma=True)
                P.op("sp", lambda e, sl=sl: e.dma_start(out=ohg, in_=ohgT_s[:, :, sl].rearrange("c p t -> p c t")),
                     reads=[Bs["ohgT"]], pwrites=[B_ohg], dma=True)
                for n in range(4):
                    P.op("sp", lambda e, sl=sl, n=n: e.dma_start(out=ga[:], in_=gaT_s[n * 4:(n + 1) * 4, :, sl].rearrange("c p t -> p c t")),
                         reads=[Bs["gaT"]], writes=[B_ga], dma=True)
                    P.op("sp", lambda e, sl=sl, n=n: e.dma_start(out=gb[:], in_=gbT_s[n * 4:(n + 1) * 4, :, sl].rearrange("c p t -> p c t")),
                         reads=[Bs["gbT"]], writes=[B_gb], dma=True)
                    wa_, bwa = ldw(wpa_s[n], B_wpa[n], 8)
                    wb_, bwb = ldw(wpb_s[n], B_wpb[n], 8)
                    for j in range(4):
                        cj = n * 4 + j
                        p1, bp1 = nextp()
                        p2, bp2 = nextp()
                        for kc in range(8):
                            P.op("pe", lambda e, kc=kc, j=j, p1=p1, wa_=wa_: e.matmul(p1[:], wa_[:, kc, j * 128:(j + 1) * 128], ona[:, kc, :],
                                                                                     start=(kc == 0), stop=(kc == 7)),
                                 reads=[bwa, B_ona], pwrites=[bp1])
                        for kc in range(8):
                            P.op("pe", lambda e, kc=kc, j=j, p2=p2, wb_=wb_: e.matmul(p2[:], wb_[:, kc, j * 128:(j + 1) * 128], ohg[:, kc, :],
                                                                                     start=(kc == 0), stop=(kc == 7)),
                                 reads=[bwb, B_ohg], pwrites=[bp2])
                        i = c6["t"] % 2
                        c6["t"] += 1
                        P.op("dve", lambda e, cj=cj, p1=p1, i=i: e.tensor_tensor(out=t1[i][:], in0=p1[:], in1=ga[:, cj % 4, :], op=ALU.mult),
                             reads=[bp1, B_ga], writes=[B_t1[i]])
                        P.op("dve", lambda e, cj=cj, p2=p2, i=i: e.tensor_tensor(out=t2[i][:], in0=p2[:], in1=gb[:, cj % 4, :], op=ALU.mult),
                             reads=[bp2, B_gb], writes=[B_t2[i]])
                        P.op("pool", lambda e, cj=cj, i=i: e.tensor_tensor(out=mm_[:, cj, :], in0=t1[i][:], in1=t2[i][:], op=ALU.add),
                             reads=[B_t1[i], B_t2[i]], pwrites=[B_mm])
                B_x1 = Buf()
                for n in range(4):
                    w_, bw_ = ldw(wo_s[n], B_wo[n], KC)
                    for j in range(4):
                        cj = n * 4 + j
                        p1, bp1 = nextp()
                        for kc in range(KC):
                            P.op("pe", lambda e, kc=kc, j=j, p1=p1, w_=w_: e.matmul(p1[:], w_[:, kc, j * 128:(j + 1) * 128], mm_[:, kc, :],
                                                                                   start=(kc == 0), stop=(kc == KC - 1)),
                                 reads=[bw_, B_mm], pwrites=[bp1])
                        P.op("dve", lambda e, cj=cj, p1=p1: e.scalar_tensor_tensor(
                            out=xx[:, cj, :], in0=p1[:], scalar=prm[:, 2, cj:cj + 1], in1=xx[:, cj, :], op0=ALU.mult, op1=ALU.add),
                            reads=[bp1, B_xx, B_PRM], pwrites=[B_x1])
                norm_to(xx, B_x1, h2, B_h2, 3, 4)
                for g in range(11):
                    wa_, bwa = ldw(wfi_s[g], B_wfi[g], KC)
                    wu_, bwu = ldw(wfi_s[11 + g], B_wfi[11 + g], KC)
                    for j in range(4):
                        cj = g * 4 + j
                        p1, bp1 = nextp()
                        p2, bp2 = nextp()
                        for kc in range(KC):
                            P.op("pe", lambda e, kc=kc, j=j, p1=p1, wa_=wa_: e.matmul(p1[:], wa_[:, kc, j * 128:(j + 1) * 128], h2[:, kc, :],
                                                                                     start=(kc == 0), stop=(kc == KC - 1)),
                                 reads=[bwa, B_h2], pwrites=[bp1])
                        for kc in range(KC):
                            P.op("pe", lambda e, kc=kc, j=j, p2=p2, wu_=wu_: e.matmul(p2[:], wu_[:, kc, j * 128:(j + 1) * 128], h2[:, kc, :],
                                                                                     start=(kc == 0), stop=(kc == KC - 1)),
                                 reads=[bwu, B_h2], pwrites=[bp2])
                        i = c6["t"] % 2
                        c6["t"] += 1
                        P.op("act", lambda e, p1=p1, i=i: e.activation(out=t1[i][:], in_=p1[:], func=AF.Silu), reads=[bp1], writes=[B_t1[i]])
                        P.op("dve", lambda e, cj=cj, p2=p2, i=i: e.tensor_tensor(out=hid[:, cj, :], in0=p2[:], in1=t1[i][:], op=ALU.mult),
                             reads=[bp2, B_t1[i]], pwrites=[B_hid])
                B_x2 = Buf()
                for cj in range(KC):
                    p1, bp1 = nextp()
                    for half in range(2):
                        i = c6["f"] % 2
                        c6["f"] += 1
                        k0 = half * (FKC // 2)
                        P.op("sp", lambda e, cj=cj, i=i, k0=k0: e.dma_start(out=wfo_b[i][:], in_=wfo_s[cj][:, k0:k0 + FKC // 2, :]),
                             reads=[B_wfo[cj]], writes=[B_wfob[i]], dma=True)
                        for kk in range(FKC // 2):
                            kc = k0 + kk
                            P.op("pe", lambda e, kc=kc, kk=kk, p1=p1, i=i: e.matmul(p1[:], wfo_b[i][:, kk, :], hid[:, kc, :],
                                                                                 start=(kc == 0), stop=(kc == FKC - 1)),
                                 reads=[B_wfob[i], B_hid], pwrites=[bp1])
                    P.op("dve", lambda e, cj=cj, p1=p1: e.scalar_tensor_tensor(
                        out=xx[:, cj, :], in0=p1[:], scalar=prm[:, 5, cj:cj + 1], in1=xx[:, cj, :], op0=ALU.mult, op1=ALU.add),
                        reads=[bp1, B_x1, B_PRM], pwrites=[B_x2])
                norm_to(xx, B_x2, None, None, None, None, final=True, it=it)
                P.op("dve", lambda e: e.tensor_copy(out=rs[:, 0:1], in_=rs[:, 0:1]),
                     reads=[B_x2, B_rs, B_t1[0], B_t1[1]], writes=[B_xx, B_rs])
            P.finalize(out_ops)
        return _finish(nc, P, top, outT)


def _finish(nc, P, top, outT):
    if not P.final_ops:
        with nc.sbuf_tensor("zz", [128, 512], F32) as zz:
            bz = Buf()
            P.op("pool", lambda e: e.memset(zz[:], 0.0), writes=[bz])
            o = P.op("sp", lambda e: e.dma_start(out=outT[0:128, 0:512], in_=zz[:]), reads=[bz], dma=True)
            P.barrier()
            o2 = P.op("sp", lambda e: e.dma_start(out=outT[128:256, 0:512], in_=zz[:]), reads=[bz], dma=True)
            P.finalize([o, o2])
            P.emit(top)
            return nc
    P.emit(top)
    return nc


def _bias_tables(rpb, flipped):
    tab = np.full((3, 16, 5, 2, 64, 2, 64), NEG, np.float32)
    qc = np.arange(64)[:, None]
    kc = np.arange(64)[None, :]
    for cls, R in enumerate((0, 1, 2)):
        bs_t = max(R - 2, 0)
        for slot in range(5):
            for a in range(2):
                for kr2 in range(2):
                    qr = 2 * R + a
                    kr = (bs_t + slot) * 2 + kr2
                    if flipped:
                        oqr, okr, oqc, okc = 127 - qr, 127 - kr, 63 - qc, 63 - kc
                    else:
                        oqr, okr, oqc, okc = qr, kr, qc, kc
                    rs = min(max(oqr - 4, 0), 120)
                    if not (rs <= okr < rs + 8):
                        continue
                    cs = np.clip(oqc - 8, 0, 48)
                    valid_c = (okc >= cs) & (okc < cs + 16)
                    dr = okr - oqr + 7
                    dc = np.clip(okc - oqc, -15, 15) + 15
                    vals = rpb[:, dr, :][:, dc]
                    tab[cls, :, slot, a, :, kr2, :] = np.where(valid_c[None], vals, NEG)
    t = tab.reshape(3, 8, 2, 5, 2, 64, 2, 64).transpose(1, 4, 5, 0, 2, 3, 6, 7)
    return np.ascontiguousarray(t).reshape(8, 128, 3 * 2 * 5 * 128)


def _masks():
    s = np.arange(128)[:, None]
    t = np.arange(128)[None, :]
    same = (s // 64) == (t // 64)
    mA = (same & (s <= t)).astype(np.uint8)
    mB = (same & (s >= t)).astype(np.uint8)
    return np.ascontiguousarray(np.stack([mA, mB], 1))


def _fm(v, nchunk):
    return np.ascontiguousarray(v.reshape(nchunk, 128).T)


def prep_inputs(inp):
    x, c, ctx, c_ctx = inp["x"], inp["c"], inp["ctx"], inp["c_ctx"]
    w_in = inp["w_in"][0]
    w_in_f = np.concatenate([w_in[:, :4096], w_in[:, 5120:6144], w_in[:, 4096:5120], w_in[:, 6144:]], axis=1)
    w_in_f = np.ascontiguousarray(w_in_f)
    lbl_raw = inp["hg_lb_logits"]
    def lbl_of(flip):
        l = lbl_raw[:, ::-1, :] if flip else lbl_raw
        return np.ascontiguousarray(l.reshape(2, 2, 8, 128).transpose(3, 0, 1, 2))
    gns = np.ascontiguousarray(np.stack([_fm(inp["norm1_g"][0], 16), _fm(inp["norm2_g"][0], 16), _fm(inp["final_g"], 16)], 1))
    bada = _fm(inp["b_ada"][0], 96)
    bada = np.ascontiguousarray(np.stack([bada, bada], -1))
    rpb = inp["na_rpb"][0]
    btabs = [_bias_tables(rpb, False), _bias_tables(rpb, True)]
    masks = _masks()
    shared = dict(w_ada=inp["w_ada"][0], bada=bada, gns=gns, hgg=np.ascontiguousarray(inp["hg_norm_g"][0].reshape(128, 1)),
                  masks=masks, w_pa=inp["w_pa"][0], w_pb=inp["w_pb"][0], w_out=inp["w_out"][0],
                  w_fi=inp["w_ffn_in"][0], w_fo=inp["w_ffn_out"][0])
    maps = []
    for core in range(8):
        b, hf = core // 2, core % 2
        if hf == 0:
            xl = x[b]
            cl = ctx[b]
        else:
            xl = x[b][::-1]
            cl = ctx[b][::-1]
        cv = np.stack([_fm(c[b], 16), _fm(c_ctx, 16)], -1)
        m = dict(shared)
        m.update(xT=np.ascontiguousarray(xl.T), ctxT=np.ascontiguousarray(cl.T), cvec=np.ascontiguousarray(cv),
                 w_in=(w_in_f if hf else w_in), lbl=lbl_of(hf), btab=btabs[hf])
        maps.append(m)
    return maps


def assemble(results):
    out = np.empty((4, SEQ, D), np.float32)
    for core in range(8):
        b, hf = core // 2, core % 2
        o = results[core]["outT"].T
        if hf == 0:
            out[b, :OWN] = o
        else:
            out[b, OWN:] = o[::-1]
    return out


_NC = None


def kernel(**inputs):
    global _NC
    inp = {k: np.asarray(v) for k, v in inputs.items()}
    maps = prep_inputs(inp)
    if _NC is None:
        _NC = build_program()
    res = run_bass_kernel_spmd(_NC, maps, core_ids=list(range(8)))
    return assemble(res.results)


def _stats(P):
    return {e: len(P.ops[e]) for e in ENGS}
```

```python
import os
import numpy as np
from contextlib import ExitStack
import concourse.bass as bass
import concourse.mybir as mybir
from concourse.bass_utils import run_bass_kernel_spmd

F32 = mybir.dt.float32
BF16 = mybir.dt.bfloat16
U8 = mybir.dt.uint8
AF = mybir.ActivationFunctionType
ALU = mybir.AluOpType

ENGS = ("pe", "act", "dve", "pool", "sp")
SEM_LIMIT = 1000
DMA_SEM_LIMIT = 4000
N_DMA_SEMS = 8


class Buf:
    __slots__ = ("name", "writers", "readers", "war", "state")

    def __init__(self, name=""):
        self.name = name
        self.writers = []
        self.readers = []
        self.war = []
        self.state = "w"


def _compact(lst, o):
    if not o.is_dma:
        lst[:] = [x for x in lst if x.is_dma or x.eng != o.eng]
    lst.append(o)


class _Rec:
    def __getattr__(self, name):
        def f(*a, **k):
            self.__dict__["call"] = (name, a, k)
            return self
        return f


class Op:
    __slots__ = ("eng", "fn", "deps", "is_dma", "needs_inc", "sem", "count")

    def __init__(self, eng, fn, is_dma):
        self.eng = eng
        rec = _Rec()
        fn(rec)
        self.fn = rec.call
        self.deps = []
        self.is_dma = is_dma
        self.needs_inc = False
        self.sem = None
        self.count = None


class Prog:
    def __init__(self, nc, same_engine_sync=True):
        self.nc = nc
        self.ops = {e: [] for e in ENGS}
        self.same_engine_sync = same_engine_sync
        self.final_ops = []
        self.bar = {e: None for e in ENGS}
        self.last_compute = {e: None for e in ENGS}
        self.dma_since_bar = []

    def barrier(self):
        deps = [o for o in self.last_compute.values() if o is not None] + list(self.dma_since_bar)
        for e in ENGS:
            self.bar[e] = (self.bar[e] or []) + deps
        self.dma_since_bar = []

    def op(self, eng, fn, reads=(), writes=(), pwrites=(), dma=False):
        o = Op(eng, fn, dma)
        deps = []
        if self.bar[eng]:
            deps.extend(self.bar[eng])
            self.bar[eng] = None
        for b in reads:
            deps.extend(b.writers)
        for b in writes:
            deps.extend(b.writers)
            deps.extend(b.readers)
            deps.extend(b.war)
        for b in pwrites:
            if b.state == "r":
                deps.extend(b.readers)
            else:
                deps.extend(b.war)
        seen = set()
        for d in deps:
            if id(d) in seen:
                continue
            seen.add(id(d))
            if d.eng == eng and not d.is_dma and not dma:
                if eng == "pe" or not self.same_engine_sync:
                    continue
            o.deps.append(d)
            d.needs_inc = True
        for b in reads:
            _compact(b.readers, o)
            b.state = "r"
        for b in writes:
            b.writers = [o]
            b.war = [o]
            b.readers = []
            b.state = "w"
        for b in pwrites:
            if b.state == "r":
                b.war = list(b.readers)
                b.readers = []
                b.writers = []
                b.state = "w"
            _compact(b.writers, o)
        self.ops[eng].append(o)
        if dma:
            self.dma_since_bar.append(o)
        else:
            self.last_compute[eng] = o
        return o

    def finalize(self, ops):
        for o in ops:
            o.needs_inc = True
            self.final_ops.append(o)

    def emit(self, stack):
        nc = self.nc
        nsem = [0]

        def new_sem(name):
            nsem[0] += 1
            return stack.enter_context(nc.semaphore(name))

        for e in ENGS:
            cur, cnt, ep = None, 0, 0
            dsems, dcnt, dn = None, None, 0
            for o in self.ops[e]:
                if not o.needs_inc:
                    continue
                if o.is_dma:
                    if dsems is None:
                        dsems = [new_sem(f"d_{e}_{i}") for i in range(N_DMA_SEMS)]
                        dcnt = [0] * N_DMA_SEMS
                    j = dn % N_DMA_SEMS
                    dn += 1
                    if dcnt[j] + 16 > DMA_SEM_LIMIT:
                        dsems[j] = new_sem(f"d_{e}_{j}_{dn}")
                        dcnt[j] = 0
                    dcnt[j] += 16
                    o.sem, o.count = dsems[j], dcnt[j]
                else:
                    if cur is None or cnt >= SEM_LIMIT:
                        cur = new_sem(f"c_{e}_{ep}")
                        ep += 1
                        cnt = 0
                    cnt += 1
                    o.sem, o.count = cur, cnt
        self.n_sems = nsem[0]

        def run(ename):
            def body(eng):
                waited = {}

                def wait_all(deps):
                    need = {}
                    for d in deps:
                        k = id(d.sem)
                        if waited.get(k, 0) >= d.count:
                            continue
                        if k not in need or need[k][1] < d.count:
                            need[k] = (d.sem, d.count)
                    for k, (s, c) in need.items():
                        eng.wait_ge(s, c)
                        waited[k] = c

                for o in self.ops[ename]:
                    if o.deps:
                        wait_all(o.deps)
                    name, a, k = o.fn
                    ins = getattr(eng, name)(*a, **k)
                    if o.needs_inc:
                        ins.then_inc(o.sem, 16 if o.is_dma else 1)
                if ename == "sp":
                    wait_all(self.final_ops)
            return body

        with nc.Block() as block:
            block.tensor(run("pe"))
            block.scalar(run("act"))
            block.vector(run("dve"))
            block.gpsimd(run("pool"))
            block.sync(run("sp"))


D = 2048
KC = 16
SEQ = 8192
OWN = 4096
T = 512
NT_OWN = 8
NT_ALL = 16
CTX = 256
INW = 12288
FFH = 5632
FKC = 44
EPS = 1e-6
NEG = -30000.0
DEBUG = os.environ.get("KDEBUG", "")


def build_program():
    nc = bass.Bass("TRN2", target_bir_lowering=False)
    P = Prog(nc)

    def din(name, shape, dt=F32):
        return nc.dram_tensor(name, list(shape), dt, kind="ExternalInput").ap()

    dbg_names = set(DEBUG.split(",")) if DEBUG else set()

    def dscr(name, shape, dt):
        kind = "ExternalOutput" if name in dbg_names else "Internal"
        return nc.dram_tensor(name, list(shape), dt, kind=kind).ap()

    xT = din("xT", [D, SEQ])
    ctxT = din("ctxT", [D, CTX])
    cvec = din("cvec", [128, KC, 2])
    w_ada = din("w_ada", [D, INW])
    bada = din("bada", [128, 96, 2])
    gns = din("gns", [128, 3, KC])
    w_in = din("w_in", [D, INW])
    lbl = din("lbl", [128, 2, 2, 8])
    hgg = din("hgg", [128, 1])
    btab = din("btab", [8, 128, 3 * 2 * 5 * 128])
    masks = din("masks", [128, 2, 128], U8)
    w_pa = din("w_pa", [1024, D])
    w_pb = din("w_pb", [1024, D])
    w_out = din("w_out", [D, D])
    w_fi = din("w_fi", [D, 2 * FFH])
    w_fo = din("w_fo", [FFH, D])
    outT = nc.dram_tensor("outT", [D, OWN], F32, kind="ExternalOutput").ap()

    win_s = dscr("win_s", [24, 128, KC, 512], BF16)
    wfi_s = dscr("wfi_s", [22, 128, KC, 512], BF16)
    wpa_s = dscr("wpa_s", [4, 128, 8, 512], BF16)
    wpb_s = dscr("wpb_s", [4, 128, 8, 512], BF16)
    wo_s = dscr("wo_s", [4, 128, KC, 512], BF16)
    wfo_s = dscr("wfo_s", [16, 128, FKC, 128], BF16)
    hT_s = dscr("hT_s", [NT_ALL + 1, 128, KC, T], BF16)
    qT_s = dscr("qT_s", [8, 128, OWN], BF16)
    kT_s = dscr("kT_s", [8, 128, OWN + T], BF16)
    kcT_s = dscr("kcT_s", [8, 128, CTX], BF16)
    v_s = dscr("v_s", [8, 128, 36, 128], BF16)
    vc_s = dscr("vc_s", [8, 128, 2, 128], BF16)
    qhT_s = dscr("qhT_s", [8, 128, OWN], BF16)
    lfA_s = dscr("lfA_s", [8, 128, OWN], F32)
    lfAc_s = dscr("lfAc_s", [8, 128, CTX], F32)
    lfB_s = dscr("lfB_s", [8, 128, SEQ], F32)
    lfBc_s = dscr("lfBc_s", [8, 128, CTX], F32)
    vh_s = dscr("vh_s", [8, 128, 64, 128], BF16)
    vhc_s = dscr("vhc_s", [8, 128, 2, 128], BF16)
    hogT_s = dscr("hogT_s", [8, 128, OWN], BF16)
    gaT_s = dscr("gaT_s", [16, 128, OWN], BF16)
    gbT_s = dscr("gbT_s", [16, 128, OWN], BF16)
    onaT_s = dscr("onaT_s", [8, 128, OWN], BF16)
    o1T_s = dscr("o1T_s", [8, 128, OWN], F32)
    ohgT_s = dscr("ohgT_s", [8, 128, OWN], BF16)

    B_win = [Buf(f"win{n}") for n in range(24)]
    B_wfi = [Buf(f"wfi{n}") for n in range(22)]
    B_wpa = [Buf() for _ in range(4)]
    B_wpb = [Buf() for _ in range(4)]
    B_wo = [Buf() for _ in range(4)]
    B_wfo = [Buf() for _ in range(16)]
    B_hT = [Buf(f"hT{i}") for i in range(NT_ALL + 1)]
    Bs = {n: Buf(n) for n in "qT kT kcT v vc qhT lfA lfAc lfB lfBc vh vhc hogT gaT gbT onaT o1T ohgT".split()}

    with ExitStack() as top:
        def sbt(stack, name, shape, dt):
            return stack.enter_context(nc.sbuf_tensor(name, list(shape), dt))

        def pst(stack, name, shape, dt=F32):
            return stack.enter_context(nc.psum_tensor(name, list(shape), dt))

        prm = sbt(top, "prm", [128, 8, KC], F32)
        lb = sbt(top, "lb", [128, 2, 8], F32)
        oml = sbt(top, "oml", [128, 2, 8], F32)
        gn = sbt(top, "gn", [128, 3, KC], F32)
        hgg_sb = sbt(top, "hgg_sb", [128, 1], F32)
        ones_bf = sbt(top, "ones_bf", [128, 128], BF16)
        ident = sbt(top, "ident", [128, 128], BF16)
        mask_sb = sbt(top, "mask_sb", [128, 2, 128], U8)
        B_prm, B_lb, B_gn, B_const = Buf("prm"), Buf("lb"), Buf("gn"), Buf("const")

        P.op("sp", lambda e: e.dma_start(out=gn[:], in_=gns), writes=[B_gn], dma=True)
        P.op("sp", lambda e: e.dma_start(out=hgg_sb[:], in_=hgg), pwrites=[B_gn], dma=True)
        P.op("sp", lambda e: e.dma_start(out=mask_sb[:], in_=masks), pwrites=[B_const], dma=True)
        P.op("pool", lambda e: e.memset(ones_bf[:], 1.0), pwrites=[B_const])
        P.op("pool", lambda e: e.memset(ident[:], 1.0), pwrites=[B_const])
        P.op("pool", lambda e: e.affine_select(out=ident[:], in_=ident[:], pattern=[[1, 128]], compare_op=ALU.is_equal,
                                               fill=0.0, base=0, channel_multiplier=-1), pwrites=[B_const])

        def cast_w(src, dst, bufs, kcn, bw):
            for n in range(len(bufs)):
                P.op("pool", lambda e, n=n: e.dma_start(
                    out=dst[n], in_=src[:, n * bw:(n + 1) * bw].rearrange("(c p) n -> p c n", p=128)),
                    writes=[bufs[n]], dma=True)

        cast_w(w_in, win_s, B_win, KC, 512)

        with ExitStack() as ph:
            cv = sbt(ph, "cv", [128, KC, 2], F32)
            wa = [sbt(ph, f"wa{i}", [128, KC, 512], F32) for i in range(2)]
            bada_sb = sbt(ph, "bada_sb", [128, 96, 2], F32)
            mod = sbt(ph, "mod", [128, 96, 2], F32)
            lbl_sb = sbt(ph, "lbl_sb", [128, 2, 2, 8], F32)
            ps_mod = pst(ph, "ps_mod", [128, 256, 2])[:, 0:96, :]
            B_cv, B_wa, B_bada, B_mod, B_psmod, B_lbl = Buf(), [Buf(), Buf()], Buf(), Buf(), Buf(), Buf()
            P.op("sp", lambda e: e.dma_start(out=cv[:], in_=cvec), writes=[B_cv], dma=True)
            P.op("sp", lambda e: e.dma_start(out=bada_sb[:], in_=bada), writes=[B_bada], dma=True)
            P.op("sp", lambda e: e.dma_start(out=lbl_sb[:], in_=lbl), writes=[B_lbl], dma=True)
            P.op("act", lambda e: e.activation(out=cv[:], in_=cv[:], func=AF.Silu), reads=[B_cv], writes=[B_cv])
            for n in range(24):
                w = wa[n % 2]
                P.op("sp", lambda e, n=n, w=w: e.dma_start(
                    out=w[:], in_=w_ada[:, n * 512:(n + 1) * 512].rearrange("(c p) n -> p c n", p=128)),
                    writes=[B_wa[n % 2]], dma=True)
                for j in range(4):
                    for kc in range(KC):
                        P.op("pe", lambda e, n=n, j=j, kc=kc, w=w: e.matmul(
                            ps_mod[:, n * 4 + j, :], w[:, kc, j * 128:(j + 1) * 128], cv[:, kc, :],
                            start=(kc == 0), stop=(kc == KC - 1)),
                            reads=[B_wa[n % 2], B_cv], pwrites=[B_psmod])
            P.op("dve", lambda e: e.tensor_tensor(out=mod[:], in0=ps_mod[:], in1=bada_sb[:], op=ALU.add),
                 reads=[B_psmod, B_bada], writes=[B_mod])
            def mslice(m, w):
                return mod[:, m * 16:(m + 1) * 16, w]
            P.op("dve", lambda e: e.tensor_scalar(out=prm[:, 0, :], in0=mslice(1, 0), scalar1=1.0, scalar2=None, op0=ALU.add),
                 reads=[B_mod], pwrites=[B_prm])
            P.op("dve", lambda e: e.tensor_scalar(out=prm[:, 3, :], in0=mslice(4, 0), scalar1=1.0, scalar2=None, op0=ALU.add),
                 reads=[B_mod], pwrites=[B_prm])
            P.op("dve", lambda e: e.tensor_scalar(out=prm[:, 6, :], in0=mslice(1, 1), scalar1=1.0, scalar2=None, op0=ALU.add),
                 reads=[B_mod], pwrites=[B_prm])
            P.op("pool", lambda e: e.tensor_copy(out=prm[:, 1, :], in_=mslice(0, 0)), reads=[B_mod], pwrites=[B_prm])
            P.op("pool", lambda e: e.tensor_copy(out=prm[:, 2, :], in_=mslice(2, 0)), reads=[B_mod], pwrites=[B_prm])
            P.op("pool", lambda e: e.tensor_copy(out=prm[:, 4, :], in_=mslice(3, 0)), reads=[B_mod], pwrites=[B_prm])
            P.op("pool", lambda e: e.tensor_copy(out=prm[:, 5, :], in_=mslice(5, 0)), reads=[B_mod], pwrites=[B_prm])
            P.op("pool", lambda e: e.tensor_copy(out=prm[:, 7, :], in_=mslice(0, 1)), reads=[B_mod], pwrites=[B_prm])
            B_prm2 = Buf("prm2")
            P.op("dve", lambda e: e.tensor_tensor(out=prm[:, 0, :], in0=prm[:, 0, :], in1=gn[:, 0, :], op=ALU.mult),
                 reads=[B_prm, B_gn], pwrites=[B_prm2])
            P.op("dve", lambda e: e.tensor_tensor(out=prm[:, 3, :], in0=prm[:, 3, :], in1=gn[:, 1, :], op=ALU.mult),
                 reads=[B_prm, B_gn], pwrites=[B_prm2])
            P.op("dve", lambda e: e.tensor_tensor(out=prm[:, 6, :], in0=prm[:, 6, :], in1=gn[:, 0, :], op=ALU.mult),
                 reads=[B_prm, B_gn], pwrites=[B_prm2])
            B_PRM = Buf("PRM")
            P.op("dve", lambda e: e.tensor_copy(out=prm[:, 7, :], in_=prm[:, 7, :]), reads=[B_prm, B_prm2], writes=[B_PRM])
            P.op("dve", lambda e: e.tensor_tensor(out=lb[:], in0=lbl_sb[:, 0, :, :], in1=lbl_sb[:, 1, :, :], op=ALU.subtract),
                 reads=[B_lbl], writes=[B_lb])
            P.op("act", lambda e: e.activation(out=lb[:], in_=lb[:], func=AF.Sigmoid), reads=[B_lb], writes=[B_lb])
            P.op("dve", lambda e: e.tensor_scalar(out=oml[:], in0=lb[:], scalar1=-1.0, scalar2=1.0, op0=ALU.mult, op1=ALU.add),
                 reads=[B_lb], pwrites=[B_lb])
        P.barrier()

        cast_w(w_pa, wpa_s, B_wpa, 8, 512)
        cast_w(w_pb, wpb_s, B_wpb, 8, 512)
        cast_w(w_out, wo_s, B_wo, KC, 512)
        cast_w(w_fi, wfi_s, B_wfi, KC, 512)
        cast_w(w_fo, wfo_s, B_wfo, FKC, 128)

        with ExitStack() as ph:
            B_hb = [Buf(), Buf()]
            hb = [sbt(ph, f"hb{i}", [128, KC, T], BF16) for i in range(2)]
            wb = [sbt(ph, f"wb{i}", [128, KC, 512], BF16) for i in range(3)]
            xs = [sbt(ph, f"xs{i}", [128, KC, T], F32) for i in range(2)]
            xsq = sbt(ph, "xsq", [128, KC, T], BF16)
            rstd = sbt(ph, "rstd", [128, T], F32)
            ps_ss = pst(ph, "ps_ss", [128, T])
            B_xs = [Buf(), Buf()]
            B_xsq, B_rstd, B_ss = Buf(), Buf(), Buf()

            def emit_norm(ii, it):
                nt = T if it < NT_ALL else CTX
                x, bx, h, bh = xs[ii % 2], B_xs[ii % 2], hb[ii % 2], B_hb[ii % 2]
                src = xT[:, it * T:(it + 1) * T] if it < NT_ALL else ctxT
                si, bi = (0, 1) if it < NT_ALL else (6, 7)
                P.op("sp", lambda e: e.dma_start(out=x[:, :, 0:nt], in_=src.rearrange("(c p) t -> p c t", p=128)), writes=[bx], dma=True)
                P.op("act", lambda e: e.activation(out=xsq[:, :, 0:nt], in_=x[:, :, 0:nt], func=AF.Square), reads=[bx], writes=[B_xsq])
                for c in range(KC):
                    P.op("pe", lambda e, c=c: e.matmul(ps_ss[:, 0:nt], ones_bf[:], xsq[:, c, 0:nt], start=(c == 0), stop=(c == KC - 1)),
                         reads=[B_xsq, B_const], pwrites=[B_ss])
                P.op("act", lambda e: e.activation(out=rstd[:, 0:nt], in_=ps_ss[:, 0:nt], func=AF.Sqrt, scale=1.0 / D, bias=EPS),
                     reads=[B_ss], writes=[B_rstd])
                P.op("dve", lambda e: e.reciprocal(out=rstd[:, 0:nt], in_=rstd[:, 0:nt]), reads=[B_rstd], writes=[B_rstd])
                P.op("dve", lambda e: e.tensor_tensor(out=x[:, :, 0:nt], in0=x[:, :, 0:nt],
                                                      in1=rstd[:, 0:nt].unsqueeze(1).to_broadcast([128, KC, nt]), op=ALU.mult),
                     reads=[bx, B_rstd], writes=[bx])
                for c in range(KC):
                    P.op("act", lambda e, c=c: e.activation(out=h[:, c, 0:nt], in_=x[:, c, 0:nt], func=AF.Identity,
                                                           scale=prm[:, si, c:c + 1], bias=prm[:, bi, c:c + 1]),
                         reads=[bx, B_PRM], pwrites=[bh])
            st32 = [sbt(ph, f"st32_{i}", [128, 512], F32) for i in range(4)]
            st16 = [sbt(ph, f"st16_{i}", [128, 512], BF16) for i in range(4)]
            pg = [pst(ph, f"pg{i}", [128, 512]) for i in range(4)]
            B_wb = [Buf(), Buf(), Buf()]
            B_st32, B_st16, B_pg = [Buf() for _ in range(4)], [Buf() for _ in range(4)], [Buf() for _ in range(4)]
            cnt = {"w": 0, "p": 0, "s32": 0, "s16": 0, "alt": 0}

            def evac_store(kind, psrc, bpsrc, rows, ncols, dst_ap, dst_buf, arg=None):
                if kind == "lf":
                    d, h = arg
                    i32 = cnt["s32"] % 4
                    cnt["s32"] += 1
                    s, bs = st32[i32], B_st32[i32]
                    P.op("act", lambda e: e.activation(out=s[0:rows, 0:ncols], in_=psrc, func=AF.Sigmoid),
                         reads=[bpsrc], writes=[bs])
                    P.op("act", lambda e: e.activation(out=s[0:rows, 0:ncols], in_=s[0:rows, 0:ncols], func=AF.Ln,
                                                       scale=oml[:, d, h:h + 1], bias=lb[:, d, h:h + 1]),
                         reads=[bs, B_lb], writes=[bs])
                else:
                    i16 = cnt["s16"] % 4
                    cnt["s16"] += 1
                    s, bs = st16[i16], B_st16[i16]
                    if kind == "scale":
                        P.op("dve", lambda e: e.tensor_scalar(out=s[0:rows, 0:ncols], in0=psrc, scalar1=float(arg), scalar2=None,
                                                              op0=ALU.mult), reads=[bpsrc], writes=[bs])
                    elif kind == "copy":
                        cnt["alt"] += 1
                        if cnt["alt"] % 3 == 0:
                            P.op("act", lambda e: e.activation(out=s[0:rows, 0:ncols], in_=psrc, func=AF.Identity),
                                 reads=[bpsrc], writes=[bs])
                        else:
                            P.op("dve", lambda e: e.tensor_copy(out=s[0:rows, 0:ncols], in_=psrc), reads=[bpsrc], writes=[bs])
                    elif kind == "silu":
                        P.op("act", lambda e: e.activation(out=s[0:rows, 0:ncols], in_=psrc, func=AF.Silu),
                             reads=[bpsrc], writes=[bs])
                    elif kind == "sigmoid":
                        P.op("act", lambda e: e.activation(out=s[0:rows, 0:ncols], in_=psrc, func=AF.Sigmoid),
                             reads=[bpsrc], writes=[bs])
                    else:
                        raise ValueError(kind)
                P.op("pool", lambda e: e.dma_start(out=dst_ap, in_=s[0:rows, 0:ncols] if dst_ap_shape is None else
                                                   s[0:rows, 0:ncols].rearrange("p (a b) -> p a b", b=128)),
                     reads=[bs], pwrites=[dst_buf], dma=True)

            dst_ap_shape = None

            def load_w(n):
                i = cnt["w"] % 3
                cnt["w"] += 1
                P.op("sp", lambda e: e.dma_start(out=wb[i][:], in_=win_s[n]), reads=[B_win[n]], writes=[B_wb[i]], dma=True)
                return wb[i], B_wb[i]

            def fm_block(h, bh, nt, n, dests):
                w, bw = load_w(n)
                for j in range(4):
                    ip = cnt["p"] % 4
                    cnt["p"] += 1
                    for kc in range(KC):
                        P.op("pe", lambda e, kc=kc, j=j, ip=ip: e.matmul(pg[ip][:, 0:nt], w[:, kc, j * 128:(j + 1) * 128],
                                                                       h[:, kc, 0:nt], start=(kc == 0), stop=(kc == KC - 1)),
                             reads=[bw, bh], pwrites=[B_pg[ip]])
                    kind, dst_ap, dst_buf, arg = dests[j]
                    evac_store(kind, pg[ip][:, 0:nt], B_pg[ip], 128, nt, dst_ap, dst_buf, arg)

            def tm_block(h, bh, nt, n, dst_fn, dst_buf):
                nonlocal dst_ap_shape
                w, bw = load_w(n)
                for tt in range(nt // 128):
                    ip = cnt["p"] % 4
                    cnt["p"] += 1
                    for kc in range(KC):
                        P.op("pe", lambda e, kc=kc, tt=tt, ip=ip: e.matmul(pg[ip][:], h[:, kc, tt * 128:(tt + 1) * 128],
                                                                         w[:, kc, :], start=(kc == 0), stop=(kc == KC - 1)),
                             reads=[bw, bh], pwrites=[B_pg[ip]])
                    dst_ap_shape = 3
                    evac_store("copy", pg[ip][:], B_pg[ip], 128, 512, dst_fn(tt), dst_buf)
                    dst_ap_shape = None

            order = list(range(NT_OWN)) + [NT_ALL] + list(range(NT_OWN, NT_ALL))
            emit_norm(0, order[0])
            for ii, it in enumerate(order):
                nt = T if it < NT_ALL else CTX
                h, bh = hb[ii % 2], B_hb[ii % 2]
                if ii + 1 < len(order):
                    emit_norm(ii + 1, order[ii + 1])
                own = it < NT_OWN
                ctx = it == NT_ALL
                halo = it == NT_OWN
                t0 = it * T
                if own:
                    for n in (0, 1):
                        fm_block(h, bh, nt, n, [("scale", qT_s[n * 4 + j][:, t0:t0 + nt], Bs["qT"], 0.125) for j in range(4)])
                if own or halo or ctx:
                    for n in (2, 3):
                        if ctx:
                            dd = [("copy", kcT_s[(n - 2) * 4 + j][:, 0:nt], Bs["kcT"], None) for j in range(4)]
                        else:
                            dd = [("copy", kT_s[(n - 2) * 4 + j][:, t0:t0 + nt], Bs["kT"], None) for j in range(4)]
                        fm_block(h, bh, nt, n, dd)
                    for n in (4, 5):
                        hp0 = (n - 4) * 4
                        if ctx:
                            tm_block(h, bh, nt, n, lambda tt, hp0=hp0: vc_s[hp0:hp0 + 4, :, tt, :].rearrange("a p b -> p a b"), Bs["vc"])
                        else:
                            tm_block(h, bh, nt, n, lambda tt, hp0=hp0, it=it: v_s[hp0:hp0 + 4, :, it * 4 + tt, :].rearrange("a p b -> p a b"), Bs["v"])
                if own:
                    for n in (6, 7):
                        fm_block(h, bh, nt, n, [("copy", qhT_s[(n - 6) * 4 + j][:, t0:t0 + nt], Bs["qhT"], None) for j in range(4)])
                if own or ctx:
                    for n in (8, 9):
                        if ctx:
                            dd = [("lf", lfAc_s[(n - 8) * 4 + j][:, 0:nt], Bs["lfAc"], (0, (n - 8) * 4 + j)) for j in range(4)]
                        else:
                            dd = [("lf", lfA_s[(n - 8) * 4 + j][:, t0:t0 + nt], Bs["lfA"], (0, (n - 8) * 4 + j)) for j in range(4)]
                        fm_block(h, bh, nt, n, dd)
                for n in (10, 11):
                    if ctx:
                        dd = [("lf", lfBc_s[(n - 10) * 4 + j][:, 0:nt], Bs["lfBc"], (1, (n - 10) * 4 + j)) for j in range(4)]
                    else:
                        dd = [("lf", lfB_s[(n - 10) * 4 + j][:, t0:t0 + nt], Bs["lfB"], (1, (n - 10) * 4 + j)) for j in range(4)]
                    fm_block(h, bh, nt, n, dd)
                for n in (12, 13):
                    hp0 = (n - 12) * 4
                    if ctx:
                        tm_block(h, bh, nt, n, lambda tt, hp0=hp0: vhc_s[hp0:hp0 + 4, :, tt, :].rearrange("a p b -> p a b"), Bs["vhc"])
                    else:
                        tm_block(h, bh, nt, n, lambda tt, hp0=hp0, it=it: vh_s[hp0:hp0 + 4, :, it * 4 + tt, :].rearrange("a p b -> p a b"), Bs["vh"])
                if own:
                    for n in (14, 15):
                        fm_block(h, bh, nt, n, [("silu", hogT_s[(n - 14) * 4 + j][:, t0:t0 + nt], Bs["hogT"], None) for j in range(4)])
                    for n in range(16, 20):
                        fm_block(h, bh, nt, n, [("sigmoid", gaT_s[(n - 16) * 4 + j][:, t0:t0 + nt], Bs["gaT"], None) for j in range(4)])
                    for n in range(20, 24):
                        fm_block(h, bh, nt, n, [("sigmoid", gbT_s[(n - 20) * 4 + j][:, t0:t0 + nt], Bs["gbT"], None) for j in range(4)])
        P.barrier()

        STOP = os.environ.get("KSTOP", "")
        if STOP == "gemm1":
            return _finish(nc, P, top, outT)

        with ExitStack() as ph:
            NKT = 34
            k_sb = [sbt(ph, f"k_sb{i}", [128, NKT * 128], BF16) for i in range(2)]
            kc_sb = [sbt(ph, f"kc_sb{i}", [128, CTX], BF16) for i in range(2)]
            v_sb = [sbt(ph, f"v_sb{i}", [128, NKT + 2, 128], BF16) for i in range(2)]
            qm = [[sbt(ph, f"qm{i}_{hh}", [128, OWN], BF16) for hh in range(2)] for i in range(2)]
            vm = [sbt(ph, f"vm{hh}", [128, NKT + 2, 128], BF16) for hh in range(2)]
            onesm = [sbt(ph, f"onesm{hh}", [128, 128], BF16) for hh in range(2)]
            bt_sb = [sbt(ph, f"bt_sb{i}", [128, 3, 2, 5, 128], BF16) for i in range(2)]
            pT = [sbt(ph, f"pT{i}", [128, 7, 128], BF16) for i in range(2)]
            rinv = [sbt(ph, f"rinv{i}", [128, 128], F32) for i in range(2)]
            o_sb = [sbt(ph, f"o_sb{i}", [128, OWN], BF16) for i in range(2)]
            psS = [[pst(ph, f"psS{i}a", [128, 4, 128]), pst(ph, f"psS{i}b", [128, 4, 128])] for i in range(2)]
            psO = [pst(ph, f"psO{i}", [128, 512])[:, 0:128] for i in range(2)]
            psR = [pst(ph, f"psR{i}", [128, 512])[:, 0:128] for i in range(2)]
            B_k, B_kc, B_v, B_bt, B_osb = [[Buf(), Buf()] for _ in range(5)]
            B_qm = [[Buf(), Buf()], [Buf(), Buf()]]
            B_vm, B_pT, B_rinv = [Buf(), Buf()], [Buf(), Buf()], [Buf(), Buf()]
            B_psS, B_psO, B_psR = [Buf(), Buf()], [Buf(), Buf()], [Buf(), Buf()]
            B_om = Buf()
            for i in range(2):
                for hh in range(2):
                    P.op("pool", lambda e, i=i, hh=hh: e.memset(qm[i][hh][:], 0.0), writes=[B_qm[i][hh]])
            for hh in range(2):
                P.op("pool", lambda e, hh=hh: e.memset(vm[hh][:], 0.0), writes=[B_vm[hh]])
                P.op("pool", lambda e, hh=hh: e.memset(onesm[hh][:], 0.0), pwrites=[B_om])
            B_om2 = Buf()
            for hh in range(2):
                P.op("pool", lambda e, hh=hh: e.memset(onesm[hh][:, hh * 64:(hh + 1) * 64], 1.0), reads=[B_om], pwrites=[B_om2])
            sidx = 0
            for hp in range(8):
                i = hp % 2
                P.op("sp", lambda e, hp=hp, i=i: e.dma_start(out=k_sb[i][:], in_=kT_s[hp][:, 0:NKT * 128]),
                     reads=[Bs["kT"]], writes=[B_k[i]], dma=True)
                P.op("sp", lambda e, hp=hp, i=i: e.dma_start(out=kc_sb[i][:], in_=kcT_s[hp]),
                     reads=[Bs["kcT"]], writes=[B_kc[i]], dma=True)
                P.op("sp", lambda e, hp=hp, i=i: e.dma_start(out=v_sb[i][:, 0:NKT, :], in_=v_s[hp][:, 0:NKT, :]),
                     reads=[Bs["v"]], writes=[B_v[i]], dma=True)
                P.op("sp", lambda e, hp=hp, i=i: e.dma_start(out=v_sb[i][:, NKT:NKT + 2, :], in_=vc_s[hp]),
                     reads=[Bs["vc"]], pwrites=[B_v[i]], dma=True)
                for hh in range(2):
                    P.op("sp", lambda e, hp=hp, i=i, hh=hh: e.dma_start(
                        out=qm[i][hh][hh * 64:(hh + 1) * 64, :], in_=qT_s[hp][hh * 64:(hh + 1) * 64, :]),
                        reads=[Bs["qT"]], writes=[B_qm[i][hh]], dma=True)
                P.op("pool", lambda e, hp=hp, i=i: e.dma_start(
                    out=bt_sb[i][:].rearrange("p a b c d -> p a (b c d)"), in_=btab[hp].rearrange("p (a x) -> p a x", a=3)),
                    writes=[B_bt[i]], dma=True)
                for hh in range(2):
                    P.op("pool", lambda e, i=i, hh=hh: e.tensor_copy(out=vm[hh][:, :, hh * 64:(hh + 1) * 64],
                                                                    in_=v_sb[i][:, :, hh * 64:(hh + 1) * 64]),
                         reads=[B_v[i]], writes=[B_vm[hh]])
                for R in range(32):
                    cls = min(R, 2)
                    bs_t = max(R - 2, 0)
                    io = R % 2
                    for hh in range(2):
                        si = sidx % 2
                        sidx += 1
                        pa_, pb_ = psS[si]
                        q_ap = qm[i][hh][:, R * 128:(R + 1) * 128]
                        for t in range(5):
                            dst = pa_[:, t, :] if t < 4 else pb_[:, 0, :]
                            P.op("pe", lambda e, dst=dst, t=t, q_ap=q_ap: e.matmul(
                                dst, k_sb[i][:, (bs_t + t) * 128:(bs_t + t + 1) * 128], q_ap, start=True, stop=False),
                                reads=[B_k[i], B_qm[i][hh]], pwrites=[B_psS[si]])
                            P.op("pe", lambda e, dst=dst, t=t, hh=hh: e.matmul(
                                dst, bt_sb[i][:, cls, hh, t, :], ident[:], start=False, stop=True),
                                reads=[B_bt[i], B_const], pwrites=[B_psS[si]])
                        for t in range(2):
                            dst = pb_[:, 1 + t, :]
                            P.op("pe", lambda e, dst=dst, t=t, q_ap=q_ap: e.matmul(
                                dst, kc_sb[i][:, t * 128:(t + 1) * 128], q_ap, start=True, stop=True),
                                reads=[B_kc[i], B_qm[i][hh]], pwrites=[B_psS[si]])
                        P.op("act", lambda e, pa_=pa_, hh=hh: e.activation(out=pT[hh][:, 0:4, :], in_=pa_[:], func=AF.Exp),
                             reads=[B_psS[si]], pwrites=[B_pT[hh]])
                        P.op("act", lambda e, pb_=pb_, hh=hh: e.activation(out=pT[hh][:, 4:7, :], in_=pb_[:, 0:3, :], func=AF.Exp),
                             reads=[B_psS[si]], pwrites=[B_pT[hh]])
                        for t in range(7):
                            vt = (bs_t + t) if t < 5 else (NKT + t - 5)
                            first = (hh == 0 and t == 0)
                            last = (hh == 1 and t == 6)
                            P.op("pe", lambda e, t=t, vt=vt, hh=hh, first=first, last=last: e.matmul(
                                psO[io][:], vm[hh][:, vt, :], pT[hh][:, t, :], start=first, stop=last),
                                reads=[B_vm[hh], B_pT[hh]], pwrites=[B_psO[io]])
                            P.op("pe", lambda e, t=t, hh=hh, first=first, last=last: e.matmul(
                                psR[io][:], onesm[hh][:], pT[hh][:, t, :], start=first, stop=last),
                                reads=[B_om2, B_pT[hh]], pwrites=[B_psR[io]])
                    P.op("dve", lambda e, io=io: e.reciprocal(out=rinv[io][:], in_=psR[io][:]), reads=[B_psR[io]], writes=[B_rinv[io]])
                    P.op("dve", lambda e, io=io, R=R, i=i: e.tensor_tensor(out=o_sb[i][:, R * 128:(R + 1) * 128], in0=psO[io][:],
                                                                      in1=rinv[io][:], op=ALU.mult),
                         reads=[B_psO[io], B_rinv[io]], pwrites=[B_osb[i]])
                P.op("pool", lambda e, hp=hp, i=i: e.dma_start(out=onaT_s[hp], in_=o_sb[i][:]),
                     reads=[B_osb[i]], pwrites=[Bs["onaT"]], dma=True)
        P.barrier()
        if STOP == "na":
            return _finish(nc, P, top, outT)

        with ExitStack() as ph:
            NH = 8
            rstA = sbt(ph, "rstA", [128, T], F32)
            rstB = sbt(ph, "rstB", [128, T], F32)
            TN = ("lf", "cum", "b", "ek", "kk", "k32")
            tl = [{n: sbt(ph, f"t_{n}{i}", [128, T], F32) for n in TN} for i in range(2)]
            tkh = [sbt(ph, f"t_khT{i}", [128, T], BF16) for i in range(2)]
            tq = [sbt(ph, f"t_q{i}", [128, T], BF16) for i in range(2)]
            Bt = [{n: Buf() for n in TN + ("khT", "q")} for i in range(2)]
            qt = [[sbt(ph, f"qt{p}_{h}", [128, T], BF16) for h in range(NH)] for p in range(2)]
            qb = [[sbt(ph, f"qb{p}_{h}", [128, T], BF16) for h in range(NH)] for p in range(2)]
            kt = [[sbt(ph, f"kt{p}_{h}", [128, T], BF16) for h in range(NH)] for p in range(2)]
            kh = [[sbt(ph, f"kh{p}_{h}", [128, 4, 128], BF16) for h in range(NH)] for p in range(2)]
            vv = [[sbt(ph, f"vv{p}_{h}", [128, 4, 128], BF16) for h in range(NH)] for p in range(2)]
            ee = [[sbt(ph, f"ee{p}_{h}", [128, 8], F32) for h in range(NH)] for p in range(2)]
            ost = [[sbt(ph, f"ost{p}_{h}", [128, T], F32) for h in range(NH)] for p in range(2)]
            B_ops = [[Buf() for h in range(NH)] for p in range(2)]
            B_ost = [[Buf() for h in range(NH)] for p in range(2)]
            S32 = [sbt(ph, f"S32_{h}", [128, 128], F32) for h in range(NH)]
            Sbf = [sbt(ph, f"Sbf_{h}", [128, 128], BF16) for h in range(NH)]
            AT = [sbt(ph, f"AT_{h}", [128, 128], BF16) for h in range(NH)]
            ATB = [sbt(ph, f"ATB_{h}", [128, 128], BF16) for h in range(NH)]
            B_S32, B_Sbf, B_AT = [Buf() for _ in range(NH)], [Buf() for _ in range(NH)], [Buf() for _ in range(NH)]
            o1b = [sbt(ph, f"o1b{i}", [128, T], F32) for i in range(2)]
            hogb = [sbt(ph, f"hogb{i}", [128, T], BF16) for i in range(2)]
            sqb = [sbt(ph, f"sqb{i}", [128, T], BF16) for i in range(2)]
            rsb = [sbt(ph, f"rsb{i}", [128, T], F32) for i in range(2)]
            outb = [sbt(ph, f"outb{i}", [128, T], BF16) for i in range(2)]
            B_o1b, B_hogb = [Buf(), Buf()], [Buf(), Buf()]
            B_sqb, B_rsb, B_outb = [Buf(), Buf()], [Buf(), Buf()], [Buf(), Buf()]
            psA = [pst(ph, f"psA{i}", [128, 512])[:, 0:128] for i in range(2)]
            psOo = [pst(ph, f"psOo{i}", [128, 512])[:, 0:128] for i in range(2)]
            psX = [pst(ph, f"psX{i}", [128, 512])[:, 0:128] for i in range(2)]
            psT = pst(ph, "psT", [128, 8, 128], BF16)
            psN = pst(ph, "psN", [128, T])
            B_psA, B_psOo, B_psX = [Buf(), Buf()], [Buf(), Buf()], [Buf(), Buf()]
            B_psT, B_psN = Buf(), Buf()
            B_rst = Buf()
            P.op("pool", lambda e: e.memset(rstA[:], 1.0), pwrites=[B_rst])
            P.op("pool", lambda e: e.memset(rstB[:], 1.0), pwrites=[B_rst])
            B_rst2 = Buf()
            P.op("pool", lambda e: e.memset(rstA[:].rearrange("p (c t) -> p c t", t=64)[:, :, 0:1], 0.0), reads=[B_rst], pwrites=[B_rst2])
            P.op("pool", lambda e: e.memset(rstB[:].rearrange("p (c t) -> p c t", t=64)[:, :, 63:64], 0.0), reads=[B_rst], pwrites=[B_rst2])
            for h in range(NH):
                P.op("pool", lambda e, h=h: e.memset(AT[h][:], 0.0), writes=[B_AT[h]])
                P.op("pool", lambda e, h=h: e.memset(ATB[h][:], 0.0), writes=[B_AT[h]])
            cn = {"tl": 0, "a": 0, "o": 0, "x": 0, "ro": 0}
            MID = 31

            def reset_states():
                for h in range(NH):
                    P.op("pool", lambda e, h=h: e.memset(S32[h][:], 0.0), writes=[B_S32[h]])
                    P.op("pool", lambda e, h=h: e.memset(Sbf[h][:], 0.0), writes=[B_Sbf[h]])

            class TileJob:
                def __init__(self, dirB, k, src_lf, b_lf, tok0, nt, src_v, b_v, vt0, own_t0):
                    self.dirB, self.par, self.src_lf, self.b_lf, self.tok0, self.nt = dirB, k % 2, src_lf, b_lf, tok0, nt
                    self.src_v, self.b_v, self.vt0, self.own_t0 = src_v, b_v, vt0, own_t0
                    self.nblk, self.nch, self.with_out = nt // 128, nt // 64, own_t0 is not None

            def prep_head(J, h):
                dirB, par, nt, nblk, nch, with_out = J.dirB, J.par, J.nt, J.nblk, J.nch, J.with_out
                rst = rstB if dirB else rstA
                rv = (lambda a: a[:, ::-1]) if dirB else (lambda a: a)
                edge = 0 if dirB else 63
                ti = cn["tl"] % 2
                cn["tl"] += 1
                t_, b_ = tl[ti], Bt[ti]
                bo = B_ops[par][h]
                v3 = lambda a: a[:, 0:nt].rearrange("p (c t) -> p c t", t=64)
                P.op("sp", lambda e: e.dma_start(out=t_["lf"][:, 0:nt], in_=J.src_lf[h][:, J.tok0:J.tok0 + nt]),
                     reads=[J.b_lf], writes=[b_["lf"]], dma=True)
                P.op("sp", lambda e: e.dma_start(out=vv[par][h][:, 0:nblk, :], in_=J.src_v[h][:, J.vt0:J.vt0 + nblk, :]),
                     reads=[J.b_v], pwrites=[bo], dma=True)
                if with_out:
                    P.op("sp", lambda e: e.dma_start(out=tq[ti][:, 0:nt], in_=qhT_s[h][:, J.own_t0:J.own_t0 + nt]),
                         reads=[Bs["qhT"]], writes=[b_["q"]], dma=True)
                P.op("dve", lambda e: e.tensor_tensor_scan(
                    out=rv(t_["cum"][:, 0:nt]), data0=rv(rst[:, 0:nt]), data1=rv(t_["lf"][:, 0:nt]), initial=0.0,
                    op0=ALU.mult, op1=ALU.add), reads=[b_["lf"], B_rst2], writes=[b_["cum"]])
                P.op("act", lambda e: e.activation(out=t_["b"][:, 0:nt], in_=t_["cum"][:, 0:nt], func=AF.Exp),
                     reads=[b_["cum"]], writes=[b_["b"]])
                P.op("act", lambda e: e.activation(out=t_["kk"][:, 0:nt], in_=t_["lf"][:, 0:nt], func=AF.Exp),
                     reads=[b_["lf"]], writes=[b_["kk"]])
                P.op("pool", lambda e: e.tensor_scalar(out=t_["kk"][:, 0:nt], in0=t_["kk"][:, 0:nt], scalar1=-1.0, scalar2=1.0,
                                                       op0=ALU.mult, op1=ALU.add), reads=[b_["kk"]], writes=[b_["kk"]])
                P.op("pool", lambda e: e.tensor_copy(out=ee[par][h][:, 0:nch], in_=v3(t_["b"])[:, :, edge]),
                     reads=[b_["b"]], pwrites=[bo])
                P.op("dve", lambda e: e.tensor_tensor(out=v3(t_["ek"]), in0=v3(t_["cum"]),
                                                      in1=v3(t_["cum"])[:, :, edge:edge + 1].to_broadcast([128, nch, 64]), op=ALU.subtract),
                     reads=[b_["cum"]], writes=[b_["ek"]])
                P.op("act", lambda e: e.activation(out=t_["ek"][:, 0:nt], in_=t_["ek"][:, 0:nt], func=AF.Exp, scale=-1.0),
                     reads=[b_["ek"]], writes=[b_["ek"]])
                P.op("dve", lambda e: e.tensor_tensor(out=tkh[ti][:, 0:nt], in0=t_["kk"][:, 0:nt], in1=t_["ek"][:, 0:nt], op=ALU.mult),
                     reads=[b_["kk"], b_["ek"]], writes=[b_["khT"]])
                if with_out:
                    P.op("dve", lambda e: e.tensor_tensor(out=v3(t_["k32"]), in0=v3(t_["cum"]),
                                                          in1=v3(t_["cum"])[:, :, MID:MID + 1].to_broadcast([128, nch, 64]), op=ALU.subtract),
                         reads=[b_["cum"]], writes=[b_["k32"]])
                    P.op("act", lambda e: e.activation(out=t_["lf"][:, 0:nt], in_=t_["k32"][:, 0:nt], func=AF.Exp),
                         reads=[b_["k32"], b_["cum"], b_["kk"]], writes=[b_["lf"]])
                    P.op("act", lambda e: e.activation(out=t_["k32"][:, 0:nt], in_=t_["k32"][:, 0:nt], func=AF.Exp, scale=-1.0),
                         reads=[b_["k32"], b_["lf"]], writes=[b_["k32"]])
                    P.op("pool", lambda e: e.tensor_tensor(out=qb[par][h][:, 0:nt], in0=tq[ti][:, 0:nt], in1=t_["b"][:, 0:nt], op=ALU.mult),
                         reads=[b_["q"], b_["b"]], pwrites=[bo])
                    P.op("pool", lambda e: e.tensor_tensor(out=qt[par][h][:, 0:nt], in0=tq[ti][:, 0:nt], in1=t_["lf"][:, 0:nt], op=ALU.mult),
                         reads=[b_["q"], b_["lf"]], pwrites=[bo])
                    P.op("pool", lambda e: e.tensor_tensor(out=kt[par][h][:, 0:nt], in0=t_["kk"][:, 0:nt], in1=t_["k32"][:, 0:nt], op=ALU.mult),
                         reads=[b_["kk"], b_["k32"]], pwrites=[bo])
                for bk in range(nblk):
                    P.op("pe", lambda e, bk=bk: e.transpose(psT[:, bk, :], tkh[ti][:, bk * 128:(bk + 1) * 128], ident[:]),
                         reads=[b_["khT"], B_const], pwrites=[B_psT])
                P.op("dve", lambda e: e.tensor_copy(out=kh[par][h][:, 0:nblk, :], in_=psT[:, 0:nblk, :]),
                     reads=[B_psT], pwrites=[bo])

            def step_unit(J, bk, h):
                dirB, par, with_out = J.dirB, J.par, J.with_out
                chs = (1, 0) if dirB else (0, 1)
                bo = B_ops[par][h]
                ATd = ATB if dirB else AT
                if with_out:
                    ia = cn["a"] % 2
                    cn["a"] += 1
                    io = cn["o"] % 2
                    cn["o"] += 1
                    P.op("pe", lambda e: e.matmul(psA[ia][:], kt[par][h][:, bk * 128:(bk + 1) * 128], qt[par][h][:, bk * 128:(bk + 1) * 128],
                                                  start=True, stop=True), reads=[bo], writes=[B_psA[ia]])
                    P.op("dve", lambda e: e.copy_predicated(out=ATd[h][:], mask=mask_sb[:, 1 if dirB else 0, :], data=psA[ia][:]),
                         reads=[B_psA[ia], B_const], writes=[B_AT[h]])
                    P.op("pe", lambda e: e.matmul(psOo[io][:], vv[par][h][:, bk, :], ATd[h][:], start=True, stop=False),
                         reads=[bo, B_AT[h]], pwrites=[B_psOo[io]])
                for ci, c in enumerate(chs):
                    gch = bk * 2 + c
                    if with_out:
                        P.op("pe", lambda e, c=c, ci=ci: e.matmul(
                            psOo[io][:, c * 64:(c + 1) * 64], Sbf[h][:],
                            qb[par][h][:, bk * 128 + c * 64:bk * 128 + (c + 1) * 64], start=False, stop=(ci == 1)),
                            reads=[bo, B_Sbf[h]], pwrites=[B_psOo[io]])
                    ix = cn["x"] % 2
                    cn["x"] += 1
                    P.op("pe", lambda e, c=c, ix=ix: e.matmul(
                        psX[ix][:], kh[par][h][c * 64:(c + 1) * 64, bk, :], vv[par][h][c * 64:(c + 1) * 64, bk, :],
                        start=True, stop=True), reads=[bo], writes=[B_psX[ix]])
                    P.op("dve", lambda e, gch=gch, ix=ix: e.scalar_tensor_tensor(
                        out=S32[h][:], in0=S32[h][:], scalar=ee[par][h][:, gch:gch + 1], in1=psX[ix][:],
                        op0=ALU.mult, op1=ALU.add), reads=[B_psX[ix], bo, B_S32[h]], writes=[B_S32[h]])
                    P.op("act", lambda e: e.activation(out=Sbf[h][:], in_=S32[h][:], func=AF.Identity),
                         reads=[B_S32[h]], writes=[B_Sbf[h]])
                if with_out:
                    P.op("act", lambda e: e.activation(out=ost[par][h][:, bk * 128:(bk + 1) * 128], in_=psOo[io][:], func=AF.Identity),
                         reads=[B_psOo[io]], pwrites=[B_ost[par][h]])

            def out_head(J, h):
                if not J.with_out:
                    return
                par, nt, t0_ = J.par, J.nt, J.own_t0
                if not J.dirB:
                    P.op("pool", lambda e: e.dma_start(out=o1T_s[h][:, t0_:t0_ + nt], in_=ost[par][h][:, 0:nt]),
                         reads=[B_ost[par][h]], pwrites=[Bs["o1T"]], dma=True)
                    return
                ri = cn["ro"] % 2
                cn["ro"] += 1
                P.op("sp", lambda e: e.dma_start(out=o1b[ri][:, 0:nt], in_=o1T_s[h][:, t0_:t0_ + nt]),
                     reads=[Bs["o1T"]], writes=[B_o1b[ri]], dma=True)
                P.op("sp", lambda e: e.dma_start(out=hogb[ri][:, 0:nt], in_=hogT_s[h][:, t0_:t0_ + nt]),
                     reads=[Bs["hogT"]], writes=[B_hogb[ri]], dma=True)
                P.op("dve", lambda e: e.tensor_tensor(out=o1b[ri][:, 0:nt], in0=o1b[ri][:, 0:nt], in1=ost[par][h][:, 0:nt], op=ALU.add),
                     reads=[B_ost[par][h], B_o1b[ri]], writes=[B_o1b[ri]])
                P.op("act", lambda e: e.activation(out=sqb[ri][:, 0:nt], in_=o1b[ri][:, 0:nt], func=AF.Square),
                     reads=[B_o1b[ri]], writes=[B_sqb[ri]])
                P.op("pe", lambda e: e.matmul(psN[:, 0:nt], ones_bf[:], sqb[ri][:, 0:nt], start=True, stop=True),
                     reads=[B_sqb[ri], B_const], writes=[B_psN])
                P.op("act", lambda e: e.activation(out=rsb[ri][:, 0:nt], in_=psN[:, 0:nt], func=AF.Sqrt, scale=1.0 / 128, bias=EPS),
                     reads=[B_psN], writes=[B_rsb[ri]])
                P.op("dve", lambda e: e.reciprocal(out=rsb[ri][:, 0:nt], in_=rsb[ri][:, 0:nt]), reads=[B_rsb[ri]], writes=[B_rsb[ri]])
                P.op("dve", lambda e: e.tensor_tensor(out=rsb[ri][:, 0:nt], in0=rsb[ri][:, 0:nt], in1=o1b[ri][:, 0:nt], op=ALU.mult),
                     reads=[B_rsb[ri], B_o1b[ri]], writes=[B_rsb[ri]])
                P.op("dve", lambda e: e.scalar_tensor_tensor(
                    out=outb[ri][:, 0:nt], in0=rsb[ri][:, 0:nt], scalar=hgg_sb[:, 0:1], in1=hogb[ri][:, 0:nt],
                    op0=ALU.mult, op1=ALU.mult), reads=[B_rsb[ri], B_hogb[ri], B_gn], writes=[B_outb[ri]])
                P.op("pool", lambda e: e.dma_start(out=ohgT_s[h][:, t0_:t0_ + nt], in_=outb[ri][:, 0:nt]),
                     reads=[B_outb[ri]], pwrites=[Bs["ohgT"]], dma=True)

            lfA_v = [lfA_s[h] for h in range(8)]
            lfAc_v = [lfAc_s[h] for h in range(8)]
            lfB_v = [lfB_s[h] for h in range(8)]
            lfBc_v = [lfBc_s[h] for h in range(8)]
            vh_v = [vh_s[h] for h in range(8)]
            vhc_v = [vhc_s[h] for h in range(8)]
            jobs = []
            k = 0
            jobs.append(TileJob(False, k, lfAc_v, Bs["lfAc"], 0, CTX, vhc_v, Bs["vhc"], 0, None)); k += 1
            for it in range(NT_OWN):
                jobs.append(TileJob(False, k, lfA_v, Bs["lfA"], it * T, T, vh_v, Bs["vh"], it * 4, it * T)); k += 1
            jobs.append(TileJob(True, k, lfBc_v, Bs["lfBc"], 0, CTX, vhc_v, Bs["vhc"], 0, None)); k += 1
            for it in range(NT_ALL - 1, -1, -1):
                jobs.append(TileJob(True, k, lfB_v, Bs["lfB"], it * T, T, vh_v, Bs["vh"], it * 4, (it * T) if it < NT_OWN else None)); k += 1

            for h in range(NH):
                prep_head(jobs[0], h)
            reset_states()
            for ji, J in enumerate(jobs):
                if ji > 0 and J.dirB and not jobs[ji - 1].dirB:
                    reset_states()
                nxt = jobs[ji + 1] if ji + 1 < len(jobs) else None
                prv = jobs[ji - 1] if ji > 0 else None
                side = []
                for h in range(NH):
                    if prv is not None and prv.with_out:
                        side.append(("out", prv, h))
                    if nxt is not None:
                        side.append(("prep", nxt, h))
                blks = list(range(J.nblk - 1, -1, -1)) if J.dirB else list(range(J.nblk))
                units = [(bk, h) for bk in blks for h in range(NH)]
                per = max(1, len(units) // max(1, len(side)))
                si = 0
                for ui, (bk, h) in enumerate(units):
                    step_unit(J, bk, h)
                    if (ui + 1) % per == 0 and si < len(side):
                        kind, JJ, hh = side[si]
                        si += 1
                        (out_head if kind == "out" else prep_head)(JJ, hh)
                while si < len(side):
                    kind, JJ, hh = side[si]
                    si += 1
                    (out_head if kind == "out" else prep_head)(JJ, hh)
            for h in range(NH):
                out_head(jobs[-1], h)
        P.barrier()
        if STOP == "hg":
            return _finish(nc, P, top, outT)

        with ExitStack() as ph:
            xx = sbt(ph, "xx", [128, KC, T], F32)
            hid = sbt(ph, "hid", [128, FKC, T], BF16)
            ona = hid[:, 28:36, :]
            ohg = hid[:, 36:44, :]
            sq = hid[:, 0:16, :]
            ga = sbt(ph, "ga", [128, 4, T], BF16)
            gb = sbt(ph, "gb", [128, 4, T], BF16)
            mm_ = sbt(ph, "mm_", [128, KC, T], BF16)
            h2 = mm_
            t1 = [sbt(ph, f"t1_{i}", [128, T], F32) for i in range(2)]
            t2 = [sbt(ph, f"t2_{i}", [128, T], F32) for i in range(2)]
            rs = sbt(ph, "rs", [128, T], F32)
            wq = [sbt(ph, f"wq{i}", [128, KC, 512], BF16) for i in range(3)]
            wfo_b = [sbt(ph, f"wfo_b{i}", [128, FKC // 2, 128], BF16) for i in range(2)]
            pp = [pst(ph, f"pp{i}", [128, T]) for i in range(6)]
            psn = pst(ph, "psn", [128, T])
            B_xx, B_ga, B_gb, B_mm, B_hid, B_rs = [Buf() for _ in range(6)]
            B_ona = B_ohg = B_sq = B_hid
            B_h2 = B_mm
            B_t1, B_t2 = [Buf(), Buf()], [Buf(), Buf()]
            B_wq, B_wfob = [Buf(), Buf(), Buf()], [Buf(), Buf()]
            B_pp, B_psn = [Buf() for _ in range(6)], Buf()
            c6 = {"w": 0, "p": 0, "t": 0, "f": 0}
            out_ops = []

            def ldw(src, bsrc, kcn):
                i = c6["w"] % 3
                c6["w"] += 1
                P.op("sp", lambda e: e.dma_start(out=wq[i][:, 0:kcn, :], in_=src), reads=[bsrc], writes=[B_wq[i]], dma=True)
                return wq[i], B_wq[i]

            def nextp():
                i = c6["p"] % 6
                c6["p"] += 1
                return pp[i], B_pp[i]

            def norm_to(src, bsrc, dst, bdst, si, bi, final=False, it=None):
                P.op("act", lambda e: e.activation(out=sq, in_=src[:], func=AF.Square), reads=[bsrc], writes=[B_sq])
                for c in range(KC):
                    P.op("pe", lambda e, c=c: e.matmul(psn[:], ones_bf[:], sq[:, c, :], start=(c == 0), stop=(c == KC - 1)),
                         reads=[B_sq, B_const], pwrites=[B_psn])
                P.op("act", lambda e: e.activation(out=rs[:], in_=psn[:], func=AF.Sqrt, scale=1.0 / D, bias=EPS),
                     reads=[B_psn], writes=[B_rs])
                P.op("dve", lambda e: e.reciprocal(out=rs[:], in_=rs[:]), reads=[B_rs], writes=[B_rs])
                for c in range(KC):
                    if not final:
                        i = c6["t"] % 2
                        c6["t"] += 1
                        P.op("dve", lambda e, c=c, i=i: e.tensor_tensor(out=t1[i][:], in0=src[:, c, :], in1=rs[:], op=ALU.mult),
                             reads=[bsrc, B_rs], writes=[B_t1[i]])
                        P.op("act", lambda e, c=c, i=i: e.activation(out=dst[:, c, :], in_=t1[i][:], func=AF.Identity,
                                                                   scale=prm[:, si, c:c + 1], bias=prm[:, bi, c:c + 1]),
                             reads=[B_t1[i], B_PRM], pwrites=[bdst])
                    else:
                        i = c6["t"] % 2
                        c6["t"] += 1
                        P.op("dve", lambda e, c=c, i=i: e.scalar_tensor_tensor(
                            out=t1[i][:], in0=src[:, c, :], scalar=gn[:, 2, c:c + 1], in1=rs[:], op0=ALU.mult, op1=ALU.mult),
                            reads=[bsrc, B_rs, B_gn], writes=[B_t1[i]])
                        out_ops.append(P.op("pool", lambda e, c=c, i=i: e.dma_start(
                            out=outT[c * 128:(c + 1) * 128, it * T:(it + 1) * T], in_=t1[i][:]), reads=[B_t1[i]], dma=True))

            for it in range(NT_OWN):
                t0 = it * T
                sl = slice(t0, t0 + T)
                P.op("sp", lambda e, sl=sl: e.dma_start(out=xx[:], in_=xT[:, sl].rearrange("(c p) t -> p c t", p=128)),
                     writes=[B_xx], dma=True)
                P.op("sp", lambda e, sl=sl: e.dma_start(out=ona, in_=onaT_s[:, :, sl].rearrange("c p t -> p c t")),
                     reads=[Bs["onaT"]], writes=[B_ona], dma=True)
                P.op("sp", lambda e, sl=sl: e.dma_start(out=ohg, in_=ohgT_s[:, :, sl].rearrange("c p t -> p c t")),
                     reads=[Bs["ohgT"]], pwrites=[B_ohg], dma=True)
                for n in range(4):
                    P.op("sp", lambda e, sl=sl, n=n: e.dma_start(out=ga[:], in_=gaT_s[n * 4:(n + 1) * 4, :, sl].rearrange("c p t -> p c t")),
                         reads=[Bs["gaT"]], writes=[B_ga], dma=True)
                    P.op("sp", lambda e, sl=sl, n=n: e.dma_start(out=gb[:], in_=gbT_s[n * 4:(n + 1) * 4, :, sl].rearrange("c p t -> p c t")),
                         reads=[Bs["gbT"]], writes=[B_gb], dma=True)
                    wa_, bwa = ldw(wpa_s[n], B_wpa[n], 8)
                    wb_, bwb = ldw(wpb_s[n], B_wpb[n], 8)
                    for j in range(4):
                        cj = n * 4 + j
                        p1, bp1 = nextp()
                        p2, bp2 = nextp()
                        for kc in range(8):
                            P.op("pe", lambda e, kc=kc, j=j, p1=p1, wa_=wa_: e.matmul(p1[:], wa_[:, kc, j * 128:(j + 1) * 128], ona[:, kc, :],
                                                                                     start=(kc == 0), stop=(kc == 7)),
                                 reads=[bwa, B_ona], pwrites=[bp1])
                        for kc in range(8):
                            P.op("pe", lambda e, kc=kc, j=j, p2=p2, wb_=wb_: e.matmul(p2[:], wb_[:, kc, j * 128:(j + 1) * 128], ohg[:, kc, :],
                                                                                     start=(kc == 0), stop=(kc == 7)),
                                 reads=[bwb, B_ohg], pwrites=[bp2])
                        i = c6["t"] % 2
                        c6["t"] += 1
                        P.op("dve", lambda e, cj=cj, p1=p1, i=i: e.tensor_tensor(out=t1[i][:], in0=p1[:], in1=ga[:, cj % 4, :], op=ALU.mult),
                             reads=[bp1, B_ga], writes=[B_t1[i]])
                        P.op("dve", lambda e, cj=cj, p2=p2, i=i: e.tensor_tensor(out=t2[i][:], in0=p2[:], in1=gb[:, cj % 4, :], op=ALU.mult),
                             reads=[bp2, B_gb], writes=[B_t2[i]])
                        P.op("pool", lambda e, cj=cj, i=i: e.tensor_tensor(out=mm_[:, cj, :], in0=t1[i][:], in1=t2[i][:], op=ALU.add),
                             reads=[B_t1[i], B_t2[i]], pwrites=[B_mm])
                B_x1 = Buf()
                for n in range(4):
                    w_, bw_ = ldw(wo_s[n], B_wo[n], KC)
                    for j in range(4):
                        cj = n * 4 + j
                        p1, bp1 = nextp()
                        for kc in range(KC):
                            P.op("pe", lambda e, kc=kc, j=j, p1=p1, w_=w_: e.matmul(p1[:], w_[:, kc, j * 128:(j + 1) * 128], mm_[:, kc, :],
                                                                                   start=(kc == 0), stop=(kc == KC - 1)),
                                 reads=[bw_, B_mm], pwrites=[bp1])
                        P.op("dve", lambda e, cj=cj, p1=p1: e.scalar_tensor_tensor(
                            out=xx[:, cj, :], in0=p1[:], scalar=prm[:, 2, cj:cj + 1], in1=xx[:, cj, :], op0=ALU.mult, op1=ALU.add),
                            reads=[bp1, B_xx, B_PRM], pwrites=[B_x1])
                norm_to(xx, B_x1, h2, B_h2, 3, 4)
                for g in range(11):
                    wa_, bwa = ldw(wfi_s[g], B_wfi[g], KC)
                    wu_, bwu = ldw(wfi_s[11 + g], B_wfi[11 + g], KC)
                    for j in range(4):
                        cj = g * 4 + j
                        p1, bp1 = nextp()
                        p2, bp2 = nextp()
                        for kc in range(KC):
                            P.op("pe", lambda e, kc=kc, j=j, p1=p1, wa_=wa_: e.matmul(p1[:], wa_[:, kc, j * 128:(j + 1) * 128], h2[:, kc, :],
                                                                                     start=(kc == 0), stop=(kc == KC - 1)),
                                 reads=[bwa, B_h2], pwrites=[bp1])
                        for kc in range(KC):
                            P.op("pe", lambda e, kc=kc, j=j, p2=p2, wu_=wu_: e.matmul(p2[:], wu_[:, kc, j * 128:(j + 1) * 128], h2[:, kc, :],
                                                                                     start=(kc == 0), stop=(kc == KC - 1)),
                                 reads=[bwu, B_h2], pwrites=[bp2])
                        i = c6["t"] % 2
                        c6["t"] += 1
                        P.op("act", lambda e, p1=p1, i=i: e.activation(out=t1[i][:], in_=p1[:], func=AF.Silu), reads=[bp1], writes=[B_t1[i]])
                        P.op("dve", lambda e, cj=cj, p2=p2, i=i: e.tensor_tensor(out=hid[:, cj, :], in0=p2[:], in1=t1[i][:], op=ALU.mult),
                             reads=[bp2, B_t1[i]], pwrites=[B_hid])
                B_x2 = Buf()
                for cj in range(KC):
                    p1, bp1 = nextp()
                    for half in range(2):
                        i = c6["f"] % 2
                        c6["f"] += 1
                        k0 = half * (FKC // 2)
                        P.op("sp", lambda e, cj=cj, i=i, k0=k0: e.dma_start(out=wfo_b[i][:], in_=wfo_s[cj][:, k0:k0 + FKC // 2, :]),
                             reads=[B_wfo[cj]], writes=[B_wfob[i]], dma=True)
                        for kk in range(FKC // 2):
                            kc = k0 + kk
                            P.op("pe", lambda e, kc=kc, kk=kk, p1=p1, i=i: e.matmul(p1[:], wfo_b[i][:, kk, :], hid[:, kc, :],
                                                                                 start=(kc == 0), stop=(kc == FKC - 1)),
                                 reads=[B_wfob[i], B_hid], pwrites=[bp1])
                    P.op("dve", lambda e, cj=cj, p1=p1: e.scalar_tensor_tensor(
                        out=xx[:, cj, :], in0=p1[:], scalar=prm[:, 5, cj:cj + 1], in1=xx[:, cj, :], op0=ALU.mult, op1=ALU.add),
                        reads=[bp1, B_x1, B_PRM], pwrites=[B_x2])
                norm_to(xx, B_x2, None, None, None, None, final=True, it=it)
                P.op("dve", lambda e: e.tensor_copy(out=rs[:, 0:1], in_=rs[:, 0:1]),
                     reads=[B_x2, B_rs, B_t1[0], B_t1[1]], writes=[B_xx, B_rs])
            P.finalize(out_ops)
        return _finish(nc, P, top, outT)


def _finish(nc, P, top, outT):
    if not P.final_ops:
        with nc.sbuf_tensor("zz", [128, 512], F32) as zz:
            bz = Buf()
            P.op("pool", lambda e: e.memset(zz[:], 0.0), writes=[bz])
            o = P.op("sp", lambda e: e.dma_start(out=outT[0:128, 0:512], in_=zz[:]), reads=[bz], dma=True)
            P.barrier()
            o2 = P.op("sp", lambda e: e.dma_start(out=outT[128:256, 0:512], in_=zz[:]), reads=[bz], dma=True)
            P.finalize([o, o2])
            P.emit(top)
            return nc
    P.emit(top)
    return nc


def _bias_tables(rpb, flipped):
    tab = np.full((3, 16, 5, 2, 64, 2, 64), NEG, np.float32)
    qc = np.arange(64)[:, None]
    kc = np.arange(64)[None, :]
    for cls, R in enumerate((0, 1, 2)):
        bs_t = max(R - 2, 0)
        for slot in range(5):
            for a in range(2):
                for kr2 in range(2):
                    qr = 2 * R + a
                    kr = (bs_t + slot) * 2 + kr2
                    if flipped:
                        oqr, okr, oqc, okc = 127 - qr, 127 - kr, 63 - qc, 63 - kc
                    else:
                        oqr, okr, oqc, okc = qr, kr, qc, kc
                    rs = min(max(oqr - 4, 0), 120)
                    if not (rs <= okr < rs + 8):
                        continue
                    cs = np.clip(oqc - 8, 0, 48)
                    valid_c = (okc >= cs) & (okc < cs + 16)
                    dr = okr - oqr + 7
                    dc = np.clip(okc - oqc, -15, 15) + 15
                    vals = rpb[:, dr, :][:, dc]
                    tab[cls, :, slot, a, :, kr2, :] = np.where(valid_c[None], vals, NEG)
    t = tab.reshape(3, 8, 2, 5, 2, 64, 2, 64).transpose(1, 4, 5, 0, 2, 3, 6, 7)
    return np.ascontiguousarray(t).reshape(8, 128, 3 * 2 * 5 * 128)


def _masks():
    s = np.arange(128)[:, None]
    t = np.arange(128)[None, :]
    same = (s // 64) == (t // 64)
    mA = (same & (s <= t)).astype(np.uint8)
    mB = (same & (s >= t)).astype(np.uint8)
    return np.ascontiguousarray(np.stack([mA, mB], 1))


def _fm(v, nchunk):
    return np.ascontiguousarray(v.reshape(nchunk, 128).T)


def prep_inputs(inp):
    x, c, ctx, c_ctx = inp["x"], inp["c"], inp["ctx"], inp["c_ctx"]
    w_in = inp["w_in"][0]
    w_in_f = np.concatenate([w_in[:, :4096], w_in[:, 5120:6144], w_in[:, 4096:5120], w_in[:, 6144:]], axis=1)
    w_in_f = np.ascontiguousarray(w_in_f)
    lbl_raw = inp["hg_lb_logits"]
    def lbl_of(flip):
        l = lbl_raw[:, ::-1, :] if flip else lbl_raw
        return np.ascontiguousarray(l.reshape(2, 2, 8, 128).transpose(3, 0, 1, 2))
    gns = np.ascontiguousarray(np.stack([_fm(inp["norm1_g"][0], 16), _fm(inp["norm2_g"][0], 16), _fm(inp["final_g"], 16)], 1))
    bada = _fm(inp["b_ada"][0], 96)
    bada = np.ascontiguousarray(np.stack([bada, bada], -1))
    rpb = inp["na_rpb"][0]
    btabs = [_bias_tables(rpb, False), _bias_tables(rpb, True)]
    masks = _masks()
    shared = dict(w_ada=inp["w_ada"][0], bada=bada, gns=gns, hgg=np.ascontiguousarray(inp["hg_norm_g"][0].reshape(128, 1)),
                  masks=masks, w_pa=inp["w_pa"][0], w_pb=inp["w_pb"][0], w_out=inp["w_out"][0],
                  w_fi=inp["w_ffn_in"][0], w_fo=inp["w_ffn_out"][0])
    maps = []
    for core in range(8):
        b, hf = core // 2, core % 2
        if hf == 0:
            xl = x[b]
            cl = ctx[b]
        else:
            xl = x[b][::-1]
            cl = ctx[b][::-1]
        cv = np.stack([_fm(c[b], 16), _fm(c_ctx, 16)], -1)
        m = dict(shared)
        m.update(xT=np.ascontiguousarray(xl.T), ctxT=np.ascontiguousarray(cl.T), cvec=np.ascontiguousarray(cv),
                 w_in=(w_in_f if hf else w_in), lbl=lbl_of(hf), btab=btabs[hf])
        maps.append(m)
    return maps


def assemble(results):
    out = np.empty((4, SEQ, D), np.float32)
    for core in range(8):
        b, hf = core // 2, core % 2
        o = results[core]["outT"].T
        if hf == 0:
            out[b, :OWN] = o
        else:
            out[b, OWN:] = o[::-1]
    return out


_NC = None


def kernel(**inputs):
    global _NC
    inp = {k: np.asarray(v) for k, v in inputs.items()}
    maps = prep_inputs(inp)
    if _NC is None:
        _NC = build_program()
    res = run_bass_kernel_spmd(_NC, maps, core_ids=list(range(8)))
    return assemble(res.results)


def _stats(P):
    return {e: len(P.ops[e]) for e in ENGS}
```

```python
import os
import numpy as np
from contextlib import ExitStack
import concourse.bass as bass
import concourse.mybir as mybir
from concourse.bass_utils import run_bass_kernel_spmd

F32 = mybir.dt.float32
BF16 = mybir.dt.bfloat16
U8 = mybir.dt.uint8
AF = mybir.ActivationFunctionType
ALU = mybir.AluOpType

ENGS = ("pe", "act", "dve", "pool", "sp")
SEM_LIMIT = 1000
DMA_SEM_LIMIT = 4000
N_DMA_SEMS = 8


class Buf:
    __slots__ = ("name", "writers", "readers", "war", "state")

    def __init__(self, name=""):
        self.name = name
        self.writers = []
        self.readers = []
        self.war = []
        self.state = "w"


def _compact(lst, o):
    if not o.is_dma:
        lst[:] = [x for x in lst if x.is_dma or x.eng != o.eng]
    lst.append(o)


class _Rec:
    def __getattr__(self, name):
        def f(*a, **k):
            self.__dict__["call"] = (name, a, k)
            return self
        return f


class Op:
    __slots__ = ("eng", "fn", "deps", "is_dma", "needs_inc", "sem", "count")

    def __init__(self, eng, fn, is_dma):
        self.eng = eng
        rec = _Rec()
        fn(rec)
        self.fn = rec.call
        self.deps = []
        self.is_dma = is_dma
        self.needs_inc = False
        self.sem = None
        self.count = None


class Prog:
    def __init__(self, nc, same_engine_sync=True):
        self.nc = nc
        self.ops = {e: [] for e in ENGS}
        self.same_engine_sync = same_engine_sync
        self.final_ops = []
        self.bar = {e: None for e in ENGS}
        self.last_compute = {e: None for e in ENGS}
        self.dma_since_bar = []

    def barrier(self):
        deps = [o for o in self.last_compute.values() if o is not None] + list(self.dma_since_bar)
        for e in ENGS:
            self.bar[e] = (self.bar[e] or []) + deps
        self.dma_since_bar = []

    def op(self, eng, fn, reads=(), writes=(), pwrites=(), dma=False):
        o = Op(eng, fn, dma)
        deps = []
        if self.bar[eng]:
            deps.extend(self.bar[eng])
            self.bar[eng] = None
        for b in reads:
            deps.extend(b.writers)
        for b in writes:
            deps.extend(b.writers)
            deps.extend(b.readers)
            deps.extend(b.war)
        for b in pwrites:
            if b.state == "r":
                deps.extend(b.readers)
            else:
                deps.extend(b.war)
        seen = set()
        for d in deps:
            if id(d) in seen:
                continue
            seen.add(id(d))
            if d.eng == eng and not d.is_dma and not dma:
                if eng == "pe" or not self.same_engine_sync:
                    continue
            o.deps.append(d)
            d.needs_inc = True
        for b in reads:
            _compact(b.readers, o)
            b.state = "r"
        for b in writes:
            b.writers = [o]
            b.war = [o]
            b.readers = []
            b.state = "w"
        for b in pwrites:
            if b.state == "r":
                b.war = list(b.readers)
                b.readers = []
                b.writers = []
                b.state = "w"
            _compact(b.writers, o)
        self.ops[eng].append(o)
        if dma:
            self.dma_since_bar.append(o)
        else:
            self.last_compute[eng] = o
        return o

    def finalize(self, ops):
        for o in ops:
            o.needs_inc = True
            self.final_ops.append(o)

    def emit(self, stack):
        nc = self.nc
        nsem = [0]

        def new_sem(name):
            nsem[0] += 1
            return stack.enter_context(nc.semaphore(name))

        for e in ENGS:
            cur, cnt, ep = None, 0, 0
            dsems, dcnt, dn = None, None, 0
            for o in self.ops[e]:
                if not o.needs_inc:
                    continue
                if o.is_dma:
                    if dsems is None:
                        dsems = [new_sem(f"d_{e}_{i}") for i in range(N_DMA_SEMS)]
                        dcnt = [0] * N_DMA_SEMS
                    j = dn % N_DMA_SEMS
                    dn += 1
                    if dcnt[j] + 16 > DMA_SEM_LIMIT:
                        dsems[j] = new_sem(f"d_{e}_{j}_{dn}")
                        dcnt[j] = 0
                    dcnt[j] += 16
                    o.sem, o.count = dsems[j], dcnt[j]
                else:
                    if cur is None or cnt >= SEM_LIMIT:
                        cur = new_sem(f"c_{e}_{ep}")
                        ep += 1
                        cnt = 0
                    cnt += 1
                    o.sem, o.count = cur, cnt
        self.n_sems = nsem[0]

        def run(ename):
            def body(eng):
                waited = {}

                def wait_all(deps):
                    need = {}
                    for d in deps:
                        k = id(d.sem)
                        if waited.get(k, 0) >= d.count:
                            continue
                        if k not in need or need[k][1] < d.count:
                            need[k] = (d.sem, d.count)
                    for k, (s, c) in need.items():
                        eng.wait_ge(s, c)
                        waited[k] = c

                for o in self.ops[ename]:
                    if o.deps:
                        wait_all(o.deps)
                    name, a, k = o.fn
                    ins = getattr(eng, name)(*a, **k)
                    if o.needs_inc:
                        ins.then_inc(o.sem, 16 if o.is_dma else 1)
                if ename == "sp":
                    wait_all(self.final_ops)
            return body

        with nc.Block() as block:
            block.tensor(run("pe"))
            block.scalar(run("act"))
            block.vector(run("dve"))
            block.gpsimd(run("pool"))
            block.sync(run("sp"))


D = 2048
KC = 16
SEQ = 8192
OWN = 4096
T = 512
NT_OWN = 8
NT_ALL = 16
CTX = 256
INW = 12288
FFH = 5632
FKC = 44
EPS = 1e-6
NEG = -30000.0
DEBUG = os.environ.get("KDEBUG", "")


def build_program():
    nc = bass.Bass("TRN2", target_bir_lowering=False)
    P = Prog(nc)

    def din(name, shape, dt=F32):
        return nc.dram_tensor(name, list(shape), dt, kind="ExternalInput").ap()

    dbg_names = set(DEBUG.split(",")) if DEBUG else set()

    def dscr(name, shape, dt):
        kind = "ExternalOutput" if name in dbg_names else "Internal"
        return nc.dram_tensor(name, list(shape), dt, kind=kind).ap()

    xT = din("xT", [D, SEQ])
    ctxT = din("ctxT", [D, CTX])
    cvec = din("cvec", [128, KC, 2])
    w_ada = din("w_ada", [D, INW])
    bada = din("bada", [128, 96, 2])
    gns = din("gns", [128, 3, KC])
    w_in = din("w_in", [D, INW])
    lbl = din("lbl", [128, 2, 2, 8])
    hgg = din("hgg", [128, 1])
    btab = din("btab", [8, 128, 3 * 2 * 5 * 128])
    masks = din("masks", [128, 2, 128], U8)
    w_pa = din("w_pa", [1024, D])
    w_pb = din("w_pb", [1024, D])
    w_out = din("w_out", [D, D])
    w_fi = din("w_fi", [D, 2 * FFH])
    w_fo = din("w_fo", [FFH, D])
    outT = nc.dram_tensor("outT", [D, OWN], F32, kind="ExternalOutput").ap()

    win_s = dscr("win_s", [24, 128, KC, 512], BF16)
    wfi_s = dscr("wfi_s", [22, 128, KC, 512], BF16)
    wpa_s = dscr("wpa_s", [4, 128, 8, 512], BF16)
    wpb_s = dscr("wpb_s", [4, 128, 8, 512], BF16)
    wo_s = dscr("wo_s", [4, 128, KC, 512], BF16)
    wfo_s = dscr("wfo_s", [16, 128, FKC, 128], BF16)
    hT_s = dscr("hT_s", [NT_ALL + 1, 128, KC, T], BF16)
    qT_s = dscr("qT_s", [8, 128, OWN], BF16)
    kT_s = dscr("kT_s", [8, 128, OWN + T], BF16)
    kcT_s = dscr("kcT_s", [8, 128, CTX], BF16)
    v_s = dscr("v_s", [8, 128, 36, 128], BF16)
    vc_s = dscr("vc_s", [8, 128, 2, 128], BF16)
    qhT_s = dscr("qhT_s", [8, 128, OWN], BF16)
    lfA_s = dscr("lfA_s", [8, 128, OWN], F32)
    lfAc_s = dscr("lfAc_s", [8, 128, CTX], F32)
    lfB_s = dscr("lfB_s", [8, 128, SEQ], F32)
    lfBc_s = dscr("lfBc_s", [8, 128, CTX], F32)
    vh_s = dscr("vh_s", [8, 128, 64, 128], BF16)
    vhc_s = dscr("vhc_s", [8, 128, 2, 128], BF16)
    hogT_s = dscr("hogT_s", [8, 128, OWN], BF16)
    gaT_s = dscr("gaT_s", [16, 128, OWN], BF16)
    gbT_s = dscr("gbT_s", [16, 128, OWN], BF16)
    onaT_s = dscr("onaT_s", [8, 128, OWN], BF16)
    o1T_s = dscr("o1T_s", [8, 128, OWN], F32)
    ohgT_s = dscr("ohgT_s", [8, 128, OWN], BF16)

    B_win = [Buf(f"win{n}") for n in range(24)]
    B_wfi = [Buf(f"wfi{n}") for n in range(22)]
    B_wpa = [Buf() for _ in range(4)]
    B_wpb = [Buf() for _ in range(4)]
    B_wo = [Buf() for _ in range(4)]
    B_wfo = [Buf() for _ in range(16)]
    B_hT = [Buf(f"hT{i}") for i in range(NT_ALL + 1)]
    Bs = {n: Buf(n) for n in "qT kT kcT v vc qhT lfA lfAc lfB lfBc vh vhc hogT gaT gbT onaT o1T ohgT".split()}

    with ExitStack() as top:
        def sbt(stack, name, shape, dt):
            return stack.enter_context(nc.sbuf_tensor(name, list(shape), dt))

        def pst(stack, name, shape, dt=F32):
            return stack.enter_context(nc.psum_tensor(name, list(shape), dt))

        prm = sbt(top, "prm", [128, 8, KC], F32)
        lb = sbt(top, "lb", [128, 2, 8], F32)
        oml = sbt(top, "oml", [128, 2, 8], F32)
        gn = sbt(top, "gn", [128, 3, KC], F32)
        hgg_sb = sbt(top, "hgg_sb", [128, 1], F32)
        ones_bf = sbt(top, "ones_bf", [128, 128], BF16)
        ident = sbt(top, "ident", [128, 128], BF16)
        mask_sb = sbt(top, "mask_sb", [128, 2, 128], U8)
        B_prm, B_lb, B_gn, B_const = Buf("prm"), Buf("lb"), Buf("gn"), Buf("const")

        P.op("sp", lambda e: e.dma_start(out=gn[:], in_=gns), writes=[B_gn], dma=True)
        P.op("sp", lambda e: e.dma_start(out=hgg_sb[:], in_=hgg), pwrites=[B_gn], dma=True)
        P.op("sp", lambda e: e.dma_start(out=mask_sb[:], in_=masks), pwrites=[B_const], dma=True)
        P.op("pool", lambda e: e.memset(ones_bf[:], 1.0), pwrites=[B_const])
        P.op("pool", lambda e: e.memset(ident[:], 1.0), pwrites=[B_const])
        P.op("pool", lambda e: e.affine_select(out=ident[:], in_=ident[:], pattern=[[1, 128]], compare_op=ALU.is_equal,
                                               fill=0.0, base=0, channel_multiplier=-1), pwrites=[B_const])

        def cast_w(src, dst, bufs, kcn, bw):
            for n in range(len(bufs)):
                P.op("pool", lambda e, n=n: e.dma_start(
                    out=dst[n], in_=src[:, n * bw:(n + 1) * bw].rearrange("(c p) n -> p c n", p=128)),
                    writes=[bufs[n]], dma=True)

        cast_w(w_in, win_s, B_win, KC, 512)

        with ExitStack() as ph:
            cv = sbt(ph, "cv", [128, KC, 2], F32)
            wa = [sbt(ph, f"wa{i}", [128, KC, 512], F32) for i in range(2)]
            bada_sb = sbt(ph, "bada_sb", [128, 96, 2], F32)
            mod = sbt(ph, "mod", [128, 96, 2], F32)
            lbl_sb = sbt(ph, "lbl_sb", [128, 2, 2, 8], F32)
            ps_mod = pst(ph, "ps_mod", [128, 256, 2])[:, 0:96, :]
            B_cv, B_wa, B_bada, B_mod, B_psmod, B_lbl = Buf(), [Buf(), Buf()], Buf(), Buf(), Buf(), Buf()
            P.op("sp", lambda e: e.dma_start(out=cv[:], in_=cvec), writes=[B_cv], dma=True)
            P.op("sp", lambda e: e.dma_start(out=bada_sb[:], in_=bada), writes=[B_bada], dma=True)
            P.op("sp", lambda e: e.dma_start(out=lbl_sb[:], in_=lbl), writes=[B_lbl], dma=True)
            P.op("act", lambda e: e.activation(out=cv[:], in_=cv[:], func=AF.Silu), reads=[B_cv], writes=[B_cv])
            for n in range(24):
                w = wa[n % 2]
                P.op("sp", lambda e, n=n, w=w: e.dma_start(
                    out=w[:], in_=w_ada[:, n * 512:(n + 1) * 512].rearrange("(c p) n -> p c n", p=128)),
                    writes=[B_wa[n % 2]], dma=True)
                for j in range(4):
                    for kc in range(KC):
                        P.op("pe", lambda e, n=n, j=j, kc=kc, w=w: e.matmul(
                            ps_mod[:, n * 4 + j, :], w[:, kc, j * 128:(j + 1) * 128], cv[:, kc, :],
                            start=(kc == 0), stop=(kc == KC - 1)),
                            reads=[B_wa[n % 2], B_cv], pwrites=[B_psmod])
            P.op("dve", lambda e: e.tensor_tensor(out=mod[:], in0=ps_mod[:], in1=bada_sb[:], op=ALU.add),
                 reads=[B_psmod, B_bada], writes=[B_mod])
            def mslice(m, w):
                return mod[:, m * 16:(m + 1) * 16, w]
            P.op("dve", lambda e: e.tensor_scalar(out=prm[:, 0, :], in0=mslice(1, 0), scalar1=1.0, scalar2=None, op0=ALU.add),
                 reads=[B_mod], pwrites=[B_prm])
            P.op("dve", lambda e: e.tensor_scalar(out=prm[:, 3, :], in0=mslice(4, 0), scalar1=1.0, scalar2=None, op0=ALU.add),
                 reads=[B_mod], pwrites=[B_prm])
            P.op("dve", lambda e: e.tensor_scalar(out=prm[:, 6, :], in0=mslice(1, 1), scalar1=1.0, scalar2=None, op0=ALU.add),
                 reads=[B_mod], pwrites=[B_prm])
            P.op("pool", lambda e: e.tensor_copy(out=prm[:, 1, :], in_=mslice(0, 0)), reads=[B_mod], pwrites=[B_prm])
            P.op("pool", lambda e: e.tensor_copy(out=prm[:, 2, :], in_=mslice(2, 0)), reads=[B_mod], pwrites=[B_prm])
            P.op("pool", lambda e: e.tensor_copy(out=prm[:, 4, :], in_=mslice(3, 0)), reads=[B_mod], pwrites=[B_prm])
            P.op("pool", lambda e: e.tensor_copy(out=prm[:, 5, :], in_=mslice(5, 0)), reads=[B_mod], pwrites=[B_prm])
            P.op("pool", lambda e: e.tensor_copy(out=prm[:, 7, :], in_=mslice(0, 1)), reads=[B_mod], pwrites=[B_prm])
            B_prm2 = Buf("prm2")
            P.op("dve", lambda e: e.tensor_tensor(out=prm[:, 0, :], in0=prm[:, 0, :], in1=gn[:, 0, :], op=ALU.mult),
                 reads=[B_prm, B_gn], pwrites=[B_prm2])
            P.op("dve", lambda e: e.tensor_tensor(out=prm[:, 3, :], in0=prm[:, 3, :], in1=gn[:, 1, :], op=ALU.mult),
                 reads=[B_prm, B_gn], pwrites=[B_prm2])
            P.op("dve", lambda e: e.tensor_tensor(out=prm[:, 6, :], in0=prm[:, 6, :], in1=gn[:, 0, :], op=ALU.mult),
                 reads=[B_prm, B_gn], pwrites=[B_prm2])
            B_PRM = Buf("PRM")
            P.op("dve", lambda e: e.tensor_copy(out=prm[:, 7, :], in_=prm[:, 7, :]), reads=[B_prm, B_prm2], writes=[B_PRM])
            P.op("dve", lambda e: e.tensor_tensor(out=lb[:], in0=lbl_sb[:, 0, :, :], in1=lbl_sb[:, 1, :, :], op=ALU.subtract),
                 reads=[B_lbl], writes=[B_lb])
            P.op("act", lambda e: e.activation(out=lb[:], in_=lb[:], func=AF.Sigmoid), reads=[B_lb], writes=[B_lb])
            P.op("dve", lambda e: e.tensor_scalar(out=oml[:], in0=lb[:], scalar1=-1.0, scalar2=1.0, op0=ALU.mult, op1=ALU.add),
                 reads=[B_lb], pwrites=[B_lb])
        P.barrier()

        cast_w(w_pa, wpa_s, B_wpa, 8, 512)
        cast_w(w_pb, wpb_s, B_wpb, 8, 512)
        cast_w(w_out, wo_s, B_wo, KC, 512)
        cast_w(w_fi, wfi_s, B_wfi, KC, 512)
        cast_w(w_fo, wfo_s, B_wfo, FKC, 128)

        with ExitStack() as ph:
            B_hb = [Buf(), Buf()]
            hb = [sbt(ph, f"hb{i}", [128, KC, T], BF16) for i in range(2)]
            wb = [sbt(ph, f"wb{i}", [128, KC, 512], BF16) for i in range(3)]
            xs = [sbt(ph, f"xs{i}", [128, KC, T], F32) for i in range(2)]
            xsq = sbt(ph, "xsq", [128, KC, T], BF16)
            rstd = sbt(ph, "rstd", [128, T], F32)
            ps_ss = pst(ph, "ps_ss", [128, T])
            B_xs = [Buf(), Buf()]
            B_xsq, B_rstd, B_ss = Buf(), Buf(), Buf()

            def emit_norm(ii, it):
                nt = T if it < NT_ALL else CTX
                x, bx, h, bh = xs[ii % 2], B_xs[ii % 2], hb[ii % 2], B_hb[ii % 2]
                src = xT[:, it * T:(it + 1) * T] if it < NT_ALL else ctxT
                si, bi = (0, 1) if it < NT_ALL else (6, 7)
                P.op("sp", lambda e: e.dma_start(out=x[:, :, 0:nt], in_=src.rearrange("(c p) t -> p c t", p=128)), writes=[bx], dma=True)
                P.op("act", lambda e: e.activation(out=xsq[:, :, 0:nt], in_=x[:, :, 0:nt], func=AF.Square), reads=[bx], writes=[B_xsq])
                for c in range(KC):
                    P.op("pe", lambda e, c=c: e.matmul(ps_ss[:, 0:nt], ones_bf[:], xsq[:, c, 0:nt], start=(c == 0), stop=(c == KC - 1)),
                         reads=[B_xsq, B_const], pwrites=[B_ss])
                P.op("act", lambda e: e.activation(out=rstd[:, 0:nt], in_=ps_ss[:, 0:nt], func=AF.Sqrt, scale=1.0 / D, bias=EPS),
                     reads=[B_ss], writes=[B_rstd])
                P.op("dve", lambda e: e.reciprocal(out=rstd[:, 0:nt], in_=rstd[:, 0:nt]), reads=[B_rstd], writes=[B_rstd])
                P.op("dve", lambda e: e.tensor_tensor(out=x[:, :, 0:nt], in0=x[:, :, 0:nt],
                                                      in1=rstd[:, 0:nt].unsqueeze(1).to_broadcast([128, KC, nt]), op=ALU.mult),
                     reads=[bx, B_rstd], writes=[bx])
                for c in range(KC):
                    P.op("act", lambda e, c=c: e.activation(out=h[:, c, 0:nt], in_=x[:, c, 0:nt], func=AF.Identity,
                                                           scale=prm[:, si, c:c + 1], bias=prm[:, bi, c:c + 1]),
                         reads=[bx, B_PRM], pwrites=[bh])
            st32 = [sbt(ph, f"st32_{i}", [128, 512], F32) for i in range(4)]
            st16 = [sbt(ph, f"st16_{i}", [128, 512], BF16) for i in range(4)]
            pg = [pst(ph, f"pg{i}", [128, 512]) for i in range(4)]
            B_wb = [Buf(), Buf(), Buf()]
            B_st32, B_st16, B_pg = [Buf() for _ in range(4)], [Buf() for _ in range(4)], [Buf() for _ in range(4)]
            cnt = {"w": 0, "p": 0, "s32": 0, "s16": 0, "alt": 0}

            def evac_store(kind, psrc, bpsrc, rows, ncols, dst_ap, dst_buf, arg=None):
                if kind == "lf":
                    d, h = arg
                    i32 = cnt["s32"] % 4
                    cnt["s32"] += 1
                    s, bs = st32[i32], B_st32[i32]
                    P.op("act", lambda e: e.activation(out=s[0:rows, 0:ncols], in_=psrc, func=AF.Sigmoid),
                         reads=[bpsrc], writes=[bs])
                    P.op("act", lambda e: e.activation(out=s[0:rows, 0:ncols], in_=s[0:rows, 0:ncols], func=AF.Ln,
                                                       scale=oml[:, d, h:h + 1], bias=lb[:, d, h:h + 1]),
                         reads=[bs, B_lb], writes=[bs])
                else:
                    i16 = cnt["s16"] % 4
                    cnt["s16"] += 1
                    s, bs = st16[i16], B_st16[i16]
                    if kind == "scale":
                        P.op("dve", lambda e: e.tensor_scalar(out=s[0:rows, 0:ncols], in0=psrc, scalar1=float(arg), scalar2=None,
                                                              op0=ALU.mult), reads=[bpsrc], writes=[bs])
                    elif kind == "copy":
                        cnt["alt"] += 1
                        if cnt["alt"] % 3 == 0:
                            P.op("act", lambda e: e.activation(out=s[0:rows, 0:ncols], in_=psrc, func=AF.Identity),
                                 reads=[bpsrc], writes=[bs])
                        else:
                            P.op("dve", lambda e: e.tensor_copy(out=s[0:rows, 0:ncols], in_=psrc), reads=[bpsrc], writes=[bs])
                    elif kind == "silu":
                        P.op("act", lambda e: e.activation(out=s[0:rows, 0:ncols], in_=psrc, func=AF.Silu),
                             reads=[bpsrc], writes=[bs])
                    elif kind == "sigmoid":
                        P.op("act", lambda e: e.activation(out=s[0:rows, 0:ncols], in_=psrc, func=AF.Sigmoid),
                             reads=[bpsrc], writes=[bs])
                    else:
                        raise ValueError(kind)
                P.op("pool", lambda e: e.dma_start(out=dst_ap, in_=s[0:rows, 0:ncols] if dst_ap_shape is None else
                                                   s[0:rows, 0:ncols].rearrange("p (a b) -> p a b", b=128)),
                     reads=[bs], pwrites=[dst_buf], dma=True)

            dst_ap_shape = None

            def load_w(n):
                i = cnt["w"] % 3
                cnt["w"] += 1
                P.op("sp", lambda e: e.dma_start(out=wb[i][:], in_=win_s[n]), reads=[B_win[n]], writes=[B_wb[i]], dma=True)
                return wb[i], B_wb[i]

            def fm_block(h, bh, nt, n, dests):
                w, bw = load_w(n)
                for j in range(4):
                    ip = cnt["p"] % 4
                    cnt["p"] += 1
                    for kc in range(KC):
                        P.op("pe", lambda e, kc=kc, j=j, ip=ip: e.matmul(pg[ip][:, 0:nt], w[:, kc, j * 128:(j + 1) * 128],
                                                                       h[:, kc, 0:nt], start=(kc == 0), stop=(kc == KC - 1)),
                             reads=[bw, bh], pwrites=[B_pg[ip]])
                    kind, dst_ap, dst_buf, arg = dests[j]
                    evac_store(kind, pg[ip][:, 0:nt], B_pg[ip], 128, nt, dst_ap, dst_buf, arg)

            def tm_block(h, bh, nt, n, dst_fn, dst_buf):
                nonlocal dst_ap_shape
                w, bw = load_w(n)
                for tt in range(nt // 128):
                    ip = cnt["p"] % 4
                    cnt["p"] += 1
                    for kc in range(KC):
                        P.op("pe", lambda e, kc=kc, tt=tt, ip=ip: e.matmul(pg[ip][:], h[:, kc, tt * 128:(tt + 1) * 128],
                                                                         w[:, kc, :], start=(kc == 0), stop=(kc == KC - 1)),
                             reads=[bw, bh], pwrites=[B_pg[ip]])
                    dst_ap_shape = 3
                    evac_store("copy", pg[ip][:], B_pg[ip], 128, 512, dst_fn(tt), dst_buf)
                    dst_ap_shape = None

            order = list(range(NT_OWN)) + [NT_ALL] + list(range(NT_OWN, NT_ALL))
            emit_norm(0, order[0])
            for ii, it in enumerate(order):
                nt = T if it < NT_ALL else CTX
                h, bh = hb[ii % 2], B_hb[ii % 2]
                if ii + 1 < len(order):
                    emit_norm(ii + 1, order[ii + 1])
                own = it < NT_OWN
                ctx = it == NT_ALL
                halo = it == NT_OWN
                t0 = it * T
                if own:
                    for n in (0, 1):
                        fm_block(h, bh, nt, n, [("scale", qT_s[n * 4 + j][:, t0:t0 + nt], Bs["qT"], 0.125) for j in range(4)])
                if own or halo or ctx:
                    for n in (2, 3):
                        if ctx:
                            dd = [("copy", kcT_s[(n - 2) * 4 + j][:, 0:nt], Bs["kcT"], None) for j in range(4)]
                        else:
                            dd = [("copy", kT_s[(n - 2) * 4 + j][:, t0:t0 + nt], Bs["kT"], None) for j in range(4)]
                        fm_block(h, bh, nt, n, dd)
                    for n in (4, 5):
                        hp0 = (n - 4) * 4
                        if ctx:
                            tm_block(h, bh, nt, n, lambda tt, hp0=hp0: vc_s[hp0:hp0 + 4, :, tt, :].rearrange("a p b -> p a b"), Bs["vc"])
                        else:
                            tm_block(h, bh, nt, n, lambda tt, hp0=hp0, it=it: v_s[hp0:hp0 + 4, :, it * 4 + tt, :].rearrange("a p b -> p a b"), Bs["v"])
                if own:
                    for n in (6, 7):
                        fm_block(h, bh, nt, n, [("copy", qhT_s[(n - 6) * 4 + j][:, t0:t0 + nt], Bs["qhT"], None) for j in range(4)])
                if own or ctx:
                    for n in (8, 9):
                        if ctx:
                            dd = [("lf", lfAc_s[(n - 8) * 4 + j][:, 0:nt], Bs["lfAc"], (0, (n - 8) * 4 + j)) for j in range(4)]
                        else:
                            dd = [("lf", lfA_s[(n - 8) * 4 + j][:, t0:t0 + nt], Bs["lfA"], (0, (n - 8) * 4 + j)) for j in range(4)]
                        fm_block(h, bh, nt, n, dd)
                for n in (10, 11):
                    if ctx:
                        dd = [("lf", lfBc_s[(n - 10) * 4 + j][:, 0:nt], Bs["lfBc"], (1, (n - 10) * 4 + j)) for j in range(4)]
                    else:
                        dd = [("lf", lfB_s[(n - 10) * 4 + j][:, t0:t0 + nt], Bs["lfB"], (1, (n - 10) * 4 + j)) for j in range(4)]
                    fm_block(h, bh, nt, n, dd)
                for n in (12, 13):
                    hp0 = (n - 12) * 4
                    if ctx:
                        tm_block(h, bh, nt, n, lambda tt, hp0=hp0: vhc_s[hp0:hp0 + 4, :, tt, :].rearrange("a p b -> p a b"), Bs["vhc"])
                    else:
                        tm_block(h, bh, nt, n, lambda tt, hp0=hp0, it=it: vh_s[hp0:hp0 + 4, :, it * 4 + tt, :].rearrange("a p b -> p a b"), Bs["vh"])
                if own:
                    for n in (14, 15):
                        fm_block(h, bh, nt, n, [("silu", hogT_s[(n - 14) * 4 + j][:, t0:t0 + nt], Bs["hogT"], None) for j in range(4)])
                    for n in range(16, 20):
                        fm_block(h, bh, nt, n, [("sigmoid", gaT_s[(n - 16) * 4 + j][:, t0:t0 + nt], Bs["gaT"], None) for j in range(4)])
                    for n in range(20, 24):
                        fm_block(h, bh, nt, n, [("sigmoid", gbT_s[(n - 20) * 4 + j][:, t0:t0 + nt], Bs["gbT"], None) for j in range(4)])
        P.barrier()

        STOP = os.environ.get("KSTOP", "")
        if STOP == "gemm1":
            return _finish(nc, P, top, outT)

        with ExitStack() as ph:
            NKT = 34
            k_sb = [sbt(ph, f"k_sb{i}", [128, NKT * 128], BF16) for i in range(2)]
            kc_sb = [sbt(ph, f"kc_sb{i}", [128, CTX], BF16) for i in range(2)]
            v_sb = [sbt(ph, f"v_sb{i}", [128, NKT + 2, 128], BF16) for i in range(2)]
            qm = [[sbt(ph, f"qm{i}_{hh}", [128, OWN], BF16) for hh in range(2)] for i in range(2)]
            vm = [sbt(ph, f"vm{hh}", [128, NKT + 2, 128], BF16) for hh in range(2)]
            onesm = [sbt(ph, f"onesm{hh}", [128, 128], BF16) for hh in range(2)]
            bt_sb = [sbt(ph, f"bt_sb{i}", [128, 3, 2, 5, 128], BF16) for i in range(2)]
            pT = [sbt(ph, f"pT{i}", [128, 7, 128], BF16) for i in range(2)]
            rinv = [sbt(ph, f"rinv{i}", [128, 128], F32) for i in range(2)]
            o_sb = [sbt(ph, f"o_sb{i}", [128, OWN], BF16) for i in range(2)]
            psS = [[pst(ph, f"psS{i}a", [128, 4, 128]), pst(ph, f"psS{i}b", [128, 4, 128])] for i in range(2)]
            psO = [pst(ph, f"psO{i}", [128, 512])[:, 0:128] for i in range(2)]
            psR = [pst(ph, f"psR{i}", [128, 512])[:, 0:128] for i in range(2)]
            B_k, B_kc, B_v, B_bt, B_osb = [[Buf(), Buf()] for _ in range(5)]
            B_qm = [[Buf(), Buf()], [Buf(), Buf()]]
            B_vm, B_pT, B_rinv = [Buf(), Buf()], [Buf(), Buf()], [Buf(), Buf()]
            B_psS, B_psO, B_psR = [Buf(), Buf()], [Buf(), Buf()], [Buf(), Buf()]
            B_om = Buf()
            for i in range(2):
                for hh in range(2):
                    P.op("pool", lambda e, i=i, hh=hh: e.memset(qm[i][hh][:], 0.0), writes=[B_qm[i][hh]])
            for hh in range(2):
                P.op("pool", lambda e, hh=hh: e.memset(vm[hh][:], 0.0), writes=[B_vm[hh]])
                P.op("pool", lambda e, hh=hh: e.memset(onesm[hh][:], 0.0), pwrites=[B_om])
            B_om2 = Buf()
            for hh in range(2):
                P.op("pool", lambda e, hh=hh: e.memset(onesm[hh][:, hh * 64:(hh + 1) * 64], 1.0), reads=[B_om], pwrites=[B_om2])
            sidx = 0
            for hp in range(8):
                i = hp % 2
                P.op("sp", lambda e, hp=hp, i=i: e.dma_start(out=k_sb[i][:], in_=kT_s[hp][:, 0:NKT * 128]),
                     reads=[Bs["kT"]], writes=[B_k[i]], dma=True)
                P.op("sp", lambda e, hp=hp, i=i: e.dma_start(out=kc_sb[i][:], in_=kcT_s[hp]),
                     reads=[Bs["kcT"]], writes=[B_kc[i]], dma=True)
                P.op("sp", lambda e, hp=hp, i=i: e.dma_start(out=v_sb[i][:, 0:NKT, :], in_=v_s[hp][:, 0:NKT, :]),
                     reads=[Bs["v"]], writes=[B_v[i]], dma=True)
                P.op("sp", lambda e, hp=hp, i=i: e.dma_start(out=v_sb[i][:, NKT:NKT + 2, :], in_=vc_s[hp]),
                     reads=[Bs["vc"]], pwrites=[B_v[i]], dma=True)
                for hh in range(2):
                    P.op("sp", lambda e, hp=hp, i=i, hh=hh: e.dma_start(
                        out=qm[i][hh][hh * 64:(hh + 1) * 64, :], in_=qT_s[hp][hh * 64:(hh + 1) * 64, :]),
                        reads=[Bs["qT"]], writes=[B_qm[i][hh]], dma=True)
                P.op("pool", lambda e, hp=hp, i=i: e.dma_start(
                    out=bt_sb[i][:].rearrange("p a b c d -> p a (b c d)"), in_=btab[hp].rearrange("p (a x) -> p a x", a=3)),
                    writes=[B_bt[i]], dma=True)
                for hh in range(2):
                    P.op("pool", lambda e, i=i, hh=hh: e.tensor_copy(out=vm[hh][:, :, hh * 64:(hh + 1) * 64],
                                                                    in_=v_sb[i][:, :, hh * 64:(hh + 1) * 64]),
                         reads=[B_v[i]], writes=[B_vm[hh]])
                for R in range(32):
                    cls = min(R, 2)
                    bs_t = max(R - 2, 0)
                    io = R % 2
                    for hh in range(2):
                        si = sidx % 2
                        sidx += 1
                        pa_, pb_ = psS[si]
                        q_ap = qm[i][hh][:, R * 128:(R + 1) * 128]
                        for t in range(5):
                            dst = pa_[:, t, :] if t < 4 else pb_[:, 0, :]
                            P.op("pe", lambda e, dst=dst, t=t, q_ap=q_ap: e.matmul(
                                dst, k_sb[i][:, (bs_t + t) * 128:(bs_t + t + 1) * 128], q_ap, start=True, stop=False),
                                reads=[B_k[i], B_qm[i][hh]], pwrites=[B_psS[si]])
                            P.op("pe", lambda e, dst=dst, t=t, hh=hh: e.matmul(
                                dst, bt_sb[i][:, cls, hh, t, :], ident[:], start=False, stop=True),
                                reads=[B_bt[i], B_const], pwrites=[B_psS[si]])
                        for t in range(2):
                            dst = pb_[:, 1 + t, :]
                            P.op("pe", lambda e, dst=dst, t=t, q_ap=q_ap: e.matmul(
                                dst, kc_sb[i][:, t * 128:(t + 1) * 128], q_ap, start=True, stop=True),
                                reads=[B_kc[i], B_qm[i][hh]], pwrites=[B_psS[si]])
                        P.op("act", lambda e, pa_=pa_, hh=hh: e.activation(out=pT[hh][:, 0:4, :], in_=pa_[:], func=AF.Exp),
                             reads=[B_psS[si]], pwrites=[B_pT[hh]])
                        P.op("act", lambda e, pb_=pb_, hh=hh: e.activation(out=pT[hh][:, 4:7, :], in_=pb_[:, 0:3, :], func=AF.Exp),
                             reads=[B_psS[si]], pwrites=[B_pT[hh]])
                        for t in range(7):
                            vt = (bs_t + t) if t < 5 else (NKT + t - 5)
                            first = (hh == 0 and t == 0)
                            last = (hh == 1 and t == 6)
                            P.op("pe", lambda e, t=t, vt=vt, hh=hh, first=first, last=last: e.matmul(
                                psO[io][:], vm[hh][:, vt, :], pT[hh][:, t, :], start=first, stop=last),
                                reads=[B_vm[hh], B_pT[hh]], pwrites=[B_psO[io]])
                            P.op("pe", lambda e, t=t, hh=hh, first=first, last=last: e.matmul(
                                psR[io][:], onesm[hh][:], pT[hh][:, t, :], start=first, stop=last),
                                reads=[B_om2, B_pT[hh]], pwrites=[B_psR[io]])
                    P.op("dve", lambda e, io=io: e.reciprocal(out=rinv[io][:], in_=psR[io][:]), reads=[B_psR[io]], writes=[B_rinv[io]])
                    P.op("dve", lambda e, io=io, R=R, i=i: e.tensor_tensor(out=o_sb[i][:, R * 128:(R + 1) * 128], in0=psO[io][:],
                                                                      in1=rinv[io][:], op=ALU.mult),
                         reads=[B_psO[io], B_rinv[io]], pwrites=[B_osb[i]])
                P.op("pool", lambda e, hp=hp, i=i: e.dma_start(out=onaT_s[hp], in_=o_sb[i][:]),
                     reads=[B_osb[i]], pwrites=[Bs["onaT"]], dma=True)
        P.barrier()
        if STOP == "na":
            return _finish(nc, P, top, outT)

        with ExitStack() as ph:
            NH = 8
            rstA = sbt(ph, "rstA", [128, T], F32)
            rstB = sbt(ph, "rstB", [128, T], F32)
            TN = ("lf", "cum", "b", "ek", "kk", "k32")
            tl = [{n: sbt(ph, f"t_{n}{i}", [128, T], F32) for n in TN} for i in range(2)]
            tkh = [sbt(ph, f"t_khT{i}", [128, T], BF16) for i in range(2)]
            tq = [sbt(ph, f"t_q{i}", [128, T], BF16) for i in range(2)]
            Bt = [{n: Buf() for n in TN + ("khT", "q")} for i in range(2)]
            qt = [[sbt(ph, f"qt{p}_{h}", [128, T], BF16) for h in range(NH)] for p in range(2)]
            qb = [[sbt(ph, f"qb{p}_{h}", [128, T], BF16) for h in range(NH)] for p in range(2)]
            kt = [[sbt(ph, f"kt{p}_{h}", [128, T], BF16) for h in range(NH)] for p in range(2)]
            kh = [[[sbt(ph, f"kh{p}_{h}_{c}", [128, 4, 128], BF16) for c in range(2)] for h in range(NH)] for p in range(2)]
            B_khz = Buf()
            for p in range(2):
                for h in range(NH):
                    for c in range(2):
                        P.op("pool", lambda e, p=p, h=h, c=c: e.memset(kh[p][h][c][:], 0.0), pwrites=[B_khz])
            vv = [[sbt(ph, f"vv{p}_{h}", [128, 4, 128], BF16) for h in range(NH)] for p in range(2)]
            ee = [[sbt(ph, f"ee{p}_{h}", [128, 8], F32) for h in range(NH)] for p in range(2)]
            ost = [[sbt(ph, f"ost{p}_{h}", [128, T], F32) for h in range(NH)] for p in range(2)]
            B_ops = [[Buf() for h in range(NH)] for p in range(2)]
            B_ost = [[Buf() for h in range(NH)] for p in range(2)]
            S32 = [sbt(ph, f"S32_{h}", [128, 128], F32) for h in range(NH)]
            Sbf = [sbt(ph, f"Sbf_{h}", [128, 128], BF16) for h in range(NH)]
            ATs = [[[sbt(ph, f"AT_{d}_{q}_{h}", [128, 128], BF16) for h in range(NH)] for q in range(2)] for d in range(2)]
            B_ATs = [[[Buf() for h in range(NH)] for q in range(2)] for d in range(2)]
            B_S32, B_Sbf = [Buf() for _ in range(NH)], [Buf() for _ in range(NH)]
            o1b = [sbt(ph, f"o1b{i}", [128, T], F32) for i in range(2)]
            hogb = [sbt(ph, f"hogb{i}", [128, T], BF16) for i in range(2)]
            sqb = [sbt(ph, f"sqb{i}", [128, T], BF16) for i in range(2)]
            rsb = [sbt(ph, f"rsb{i}", [128, T], F32) for i in range(2)]
            outb = [sbt(ph, f"outb{i}", [128, T], BF16) for i in range(2)]
            B_o1b, B_hogb = [Buf(), Buf()], [Buf(), Buf()]
            B_sqb, B_rsb, B_outb = [Buf(), Buf()], [Buf(), Buf()], [Buf(), Buf()]
            _bA = [pst(ph, f"psA{i}", [128, 512]) for i in range(2)]
            _bX = [pst(ph, f"psX{i}", [128, 512]) for i in range(2)]
            _bO = [pst(ph, f"psOo{i}", [128, 512]) for i in range(2)]
            psA = [_bA[h % 2][:, 0:128] for h in range(8)]
            psX = [_bX[h % 2][:, 0:128] for h in range(8)]
            psOo = [_bO[h % 2][:, 0:128] for h in range(8)]
            psT = pst(ph, "psT", [128, 8, 128], BF16)
            psN = pst(ph, "psN", [128, T])
            B_psA = [Buf() for _ in range(2)] * 4
            B_psX = [Buf() for _ in range(2)] * 4
            B_psOo = [Buf() for _ in range(2)] * 4
            B_psT, B_psN = Buf(), Buf()
            B_rst = Buf()
            P.op("pool", lambda e: e.memset(rstA[:], 1.0), pwrites=[B_rst])
            P.op("pool", lambda e: e.memset(rstB[:], 1.0), pwrites=[B_rst])
            B_rst2 = Buf()
            P.op("pool", lambda e: e.memset(rstA[:].rearrange("p (c t) -> p c t", t=64)[:, :, 0:1], 0.0), reads=[B_rst], pwrites=[B_rst2])
            P.op("pool", lambda e: e.memset(rstB[:].rearrange("p (c t) -> p c t", t=64)[:, :, 63:64], 0.0), reads=[B_rst], pwrites=[B_rst2])
            for d in range(2):
                for q in range(2):
                    for h in range(NH):
                        P.op("pool", lambda e, d=d, q=q, h=h: e.memset(ATs[d][q][h][:], 0.0), writes=[B_ATs[d][q][h]])
            cn = {"tl": 0, "a": 0, "o": 0, "x": 0, "ro": 0}
            MID = 31

            def reset_states():
                for h in range(NH):
                    P.op("pool", lambda e, h=h: e.memset(S32[h][:], 0.0), writes=[B_S32[h]])
                    P.op("pool", lambda e, h=h: e.memset(Sbf[h][:], 0.0), writes=[B_Sbf[h]])

            class TileJob:
                def __init__(self, dirB, k, src_lf, b_lf, tok0, nt, src_v, b_v, vt0, own_t0):
                    self.dirB, self.par, self.src_lf, self.b_lf, self.tok0, self.nt = dirB, k % 2, src_lf, b_lf, tok0, nt
                    self.src_v, self.b_v, self.vt0, self.own_t0 = src_v, b_v, vt0, own_t0
                    self.nblk, self.nch, self.with_out = nt // 128, nt // 64, own_t0 is not None

            def prep_head(J, h):
                dirB, par, nt, nblk, nch, with_out = J.dirB, J.par, J.nt, J.nblk, J.nch, J.with_out
                rst = rstB if dirB else rstA
                rv = (lambda a: a[:, ::-1]) if dirB else (lambda a: a)
                edge = 0 if dirB else 63
                ti = cn["tl"] % 2
                cn["tl"] += 1
                t_, b_ = tl[ti], Bt[ti]
                bo = B_ops[par][h]
                v3 = lambda a: a[:, 0:nt].rearrange("p (c t) -> p c t", t=64)
                P.op("sp", lambda e: e.dma_start(out=t_["lf"][:, 0:nt], in_=J.src_lf[h][:, J.tok0:J.tok0 + nt]),
                     reads=[J.b_lf], writes=[b_["lf"]], dma=True)
                P.op("sp", lambda e: e.dma_start(out=vv[par][h][:, 0:nblk, :], in_=J.src_v[h][:, J.vt0:J.vt0 + nblk, :]),
                     reads=[J.b_v], pwrites=[bo], dma=True)
                if with_out:
                    P.op("sp", lambda e: e.dma_start(out=tq[ti][:, 0:nt], in_=qhT_s[h][:, J.own_t0:J.own_t0 + nt]),
                         reads=[Bs["qhT"]], writes=[b_["q"]], dma=True)
                P.op("dve", lambda e: e.tensor_tensor_scan(
                    out=rv(t_["cum"][:, 0:nt]), data0=rv(rst[:, 0:nt]), data1=rv(t_["lf"][:, 0:nt]), initial=0.0,
                    op0=ALU.mult, op1=ALU.add), reads=[b_["lf"], B_rst2], writes=[b_["cum"]])
                P.op("act", lambda e: e.activation(out=t_["b"][:, 0:nt], in_=t_["cum"][:, 0:nt], func=AF.Exp),
                     reads=[b_["cum"]], writes=[b_["b"]])
                P.op("act", lambda e: e.activation(out=t_["kk"][:, 0:nt], in_=t_["lf"][:, 0:nt], func=AF.Exp),
                     reads=[b_["lf"]], writes=[b_["kk"]])
                P.op("pool", lambda e: e.tensor_scalar(out=t_["kk"][:, 0:nt], in0=t_["kk"][:, 0:nt], scalar1=-1.0, scalar2=1.0,
                                                       op0=ALU.mult, op1=ALU.add), reads=[b_["kk"]], writes=[b_["kk"]])
                P.op("pool", lambda e: e.tensor_copy(out=ee[par][h][:, 0:nch], in_=v3(t_["b"])[:, :, edge]),
                     reads=[b_["b"]], pwrites=[bo])
                P.op("dve", lambda e: e.tensor_tensor(out=v3(t_["ek"]), in0=v3(t_["cum"]),
                                                      in1=v3(t_["cum"])[:, :, edge:edge + 1].to_broadcast([128, nch, 64]), op=ALU.subtract),
                     reads=[b_["cum"]], writes=[b_["ek"]])
                P.op("act", lambda e: e.activation(out=t_["ek"][:, 0:nt], in_=t_["ek"][:, 0:nt], func=AF.Exp, scale=-1.0),
                     reads=[b_["ek"]], writes=[b_["ek"]])
                P.op("dve", lambda e: e.tensor_tensor(out=tkh[ti][:, 0:nt], in0=t_["kk"][:, 0:nt], in1=t_["ek"][:, 0:nt], op=ALU.mult),
                     reads=[b_["kk"], b_["ek"]], writes=[b_["khT"]])
                if with_out:
                    P.op("dve", lambda e: e.tensor_tensor(out=v3(t_["k32"]), in0=v3(t_["cum"]),
                                                          in1=v3(t_["cum"])[:, :, MID:MID + 1].to_broadcast([128, nch, 64]), op=ALU.subtract),
                         reads=[b_["cum"]], writes=[b_["k32"]])
                    P.op("act", lambda e: e.activation(out=t_["lf"][:, 0:nt], in_=t_["k32"][:, 0:nt], func=AF.Exp),
                         reads=[b_["k32"], b_["cum"], b_["kk"]], writes=[b_["lf"]])
                    P.op("act", lambda e: e.activation(out=t_["k32"][:, 0:nt], in_=t_["k32"][:, 0:nt], func=AF.Exp, scale=-1.0),
                         reads=[b_["k32"], b_["lf"]], writes=[b_["k32"]])
                    P.op("pool", lambda e: e.tensor_tensor(out=qb[par][h][:, 0:nt], in0=tq[ti][:, 0:nt], in1=t_["b"][:, 0:nt], op=ALU.mult),
                         reads=[b_["q"], b_["b"]], pwrites=[bo])
                    P.op("pool", lambda e: e.tensor_tensor(out=qt[par][h][:, 0:nt], in0=tq[ti][:, 0:nt], in1=t_["lf"][:, 0:nt], op=ALU.mult),
                         reads=[b_["q"], b_["lf"]], pwrites=[bo])
                    P.op("pool", lambda e: e.tensor_tensor(out=kt[par][h][:, 0:nt], in0=t_["kk"][:, 0:nt], in1=t_["k32"][:, 0:nt], op=ALU.mult),
                         reads=[b_["kk"], b_["k32"]], pwrites=[bo])
                for bk in range(nblk):
                    P.op("pe", lambda e, bk=bk: e.transpose(psT[:, bk, :], tkh[ti][:, bk * 128:(bk + 1) * 128], ident[:]),
                         reads=[b_["khT"], B_const], pwrites=[B_psT])
                for c in range(2):
                    P.op("dve", lambda e, c=c: e.tensor_copy(out=kh[par][h][c][c * 64:(c + 1) * 64, 0:nblk, :],
                                                            in_=psT[c * 64:(c + 1) * 64, 0:nblk, :]),
                         reads=[B_psT, B_khz], pwrites=[bo])

            def stage_A(J, bk, q, hs):
                par = J.par
                ATd = ATs[1 if J.dirB else 0][q]
                B_AT = B_ATs[1 if J.dirB else 0][q]
                for h in hs:
                    bo = B_ops[par][h]
                    P.op("pe", lambda e: e.matmul(psA[h], kt[par][h][:, bk * 128:(bk + 1) * 128], qt[par][h][:, bk * 128:(bk + 1) * 128],
                                                  start=True, stop=True), reads=[bo], writes=[B_psA[h]])
                for h in hs:
                    P.op("dve", lambda e: e.copy_predicated(out=ATd[h][:], mask=mask_sb[:, 1 if J.dirB else 0, :], data=psA[h]),
                         reads=[B_psA[h], B_const], writes=[B_AT[h]])

            def stage_M(J, bk, ci, q, hs):
                par, with_out = J.par, J.with_out
                c = ((1, 0) if J.dirB else (0, 1))[ci]
                ATd = ATs[1 if J.dirB else 0][q]
                B_AT = B_ATs[1 if J.dirB else 0][q]
                cs = slice(c * 64, (c + 1) * 64)
                for h in hs:
                    bo = B_ops[par][h]
                    if with_out:
                        if ci == 0:
                            P.op("pe", lambda e: e.matmul(psOo[h], vv[par][h][:, bk, :], ATd[h][:], start=True, stop=False),
                                 reads=[bo, B_AT[h]], pwrites=[B_psOo[h]])
                        P.op("pe", lambda e: e.matmul(psOo[h][:, cs], Sbf[h][:], qb[par][h][:, bk * 128 + c * 64:bk * 128 + (c + 1) * 64],
                                                      start=False, stop=(ci == 1)), reads=[bo, B_Sbf[h]], pwrites=[B_psOo[h]])
                    P.op("pe", lambda e: e.matmul(psX[h], kh[par][h][c][:, bk, :], vv[par][h][:, bk, :], start=True, stop=True),
                         reads=[bo], writes=[B_psX[h]])

            def stage_U(J, bk, ci, hs):
                par, with_out = J.par, J.with_out
                c = ((1, 0) if J.dirB else (0, 1))[ci]
                gch = bk * 2 + c
                for h in hs:
                    bo = B_ops[par][h]
                    P.op("dve", lambda e: e.scalar_tensor_tensor(
                        out=S32[h][:], in0=S32[h][:], scalar=ee[par][h][:, gch:gch + 1], in1=psX[h],
                        op0=ALU.mult, op1=ALU.add), reads=[B_psX[h], bo, B_S32[h]], writes=[B_S32[h]])
                    P.op("act", lambda e: e.activation(out=Sbf[h][:], in_=S32[h][:], func=AF.Identity),
                         reads=[B_S32[h]], writes=[B_Sbf[h]])
                if with_out and ci == 1:
                    for h in hs:
                        P.op("act", lambda e: e.activation(out=ost[par][h][:, bk * 128:(bk + 1) * 128], in_=psOo[h], func=AF.Identity),
                             reads=[B_psOo[h]], pwrites=[B_ost[par][h]])

            def out_head(J, h):
                if not J.with_out:
                    return
                par, nt, t0_ = J.par, J.nt, J.own_t0
                if not J.dirB:
                    P.op("pool", lambda e: e.dma_start(out=o1T_s[h][:, t0_:t0_ + nt], in_=ost[par][h][:, 0:nt]),
                         reads=[B_ost[par][h]], pwrites=[Bs["o1T"]], dma=True)
                    return
                ri = cn["ro"] % 2
                cn["ro"] += 1
                P.op("sp", lambda e: e.dma_start(out=o1b[ri][:, 0:nt], in_=o1T_s[h][:, t0_:t0_ + nt]),
                     reads=[Bs["o1T"]], writes=[B_o1b[ri]], dma=True)
                P.op("sp", lambda e: e.dma_start(out=hogb[ri][:, 0:nt], in_=hogT_s[h][:, t0_:t0_ + nt]),
                     reads=[Bs["hogT"]], writes=[B_hogb[ri]], dma=True)
                P.op("dve", lambda e: e.tensor_tensor(out=o1b[ri][:, 0:nt], in0=o1b[ri][:, 0:nt], in1=ost[par][h][:, 0:nt], op=ALU.add),
                     reads=[B_ost[par][h], B_o1b[ri]], writes=[B_o1b[ri]])
                P.op("act", lambda e: e.activation(out=sqb[ri][:, 0:nt], in_=o1b[ri][:, 0:nt], func=AF.Square),
                     reads=[B_o1b[ri]], writes=[B_sqb[ri]])
                P.op("pe", lambda e: e.matmul(psN[:, 0:nt], ones_bf[:], sqb[ri][:, 0:nt], start=True, stop=True),
                     reads=[B_sqb[ri], B_const], writes=[B_psN])
                P.op("act", lambda e: e.activation(out=rsb[ri][:, 0:nt], in_=psN[:, 0:nt], func=AF.Sqrt, scale=1.0 / 128, bias=EPS),
                     reads=[B_psN], writes=[B_rsb[ri]])
                P.op("dve", lambda e: e.reciprocal(out=rsb[ri][:, 0:nt], in_=rsb[ri][:, 0:nt]), reads=[B_rsb[ri]], writes=[B_rsb[ri]])
                P.op("dve", lambda e: e.tensor_tensor(out=rsb[ri][:, 0:nt], in0=rsb[ri][:, 0:nt], in1=o1b[ri][:, 0:nt], op=ALU.mult),
                     reads=[B_rsb[ri], B_o1b[ri]], writes=[B_rsb[ri]])
                P.op("dve", lambda e: e.scalar_tensor_tensor(
                    out=outb[ri][:, 0:nt], in0=rsb[ri][:, 0:nt], scalar=hgg_sb[:, 0:1], in1=hogb[ri][:, 0:nt],
                    op0=ALU.mult, op1=ALU.mult), reads=[B_rsb[ri], B_hogb[ri], B_gn], writes=[B_outb[ri]])
                P.op("pool", lambda e: e.dma_start(out=ohgT_s[h][:, t0_:t0_ + nt], in_=outb[ri][:, 0:nt]),
                     reads=[B_outb[ri]], pwrites=[Bs["ohgT"]], dma=True)

            lfA_v = [lfA_s[h] for h in range(8)]
            lfAc_v = [lfAc_s[h] for h in range(8)]
            lfB_v = [lfB_s[h] for h in range(8)]
            lfBc_v = [lfBc_s[h] for h in range(8)]
            vh_v = [vh_s[h] for h in range(8)]
            vhc_v = [vhc_s[h] for h in range(8)]
            jobs = []
            k = 0
            jobs.append(TileJob(False, k, lfAc_v, Bs["lfAc"], 0, CTX, vhc_v, Bs["vhc"], 0, None)); k += 1
            for it in range(NT_OWN):
                jobs.append(TileJob(False, k, lfA_v, Bs["lfA"], it * T, T, vh_v, Bs["vh"], it * 4, it * T)); k += 1
            jobs.append(TileJob(True, k, lfBc_v, Bs["lfBc"], 0, CTX, vhc_v, Bs["vhc"], 0, None)); k += 1
            for it in range(NT_ALL - 1, -1, -1):
                jobs.append(TileJob(True, k, lfB_v, Bs["lfB"], it * T, T, vh_v, Bs["vh"], it * 4, (it * T) if it < NT_OWN else None)); k += 1

            for h in range(NH):
                prep_head(jobs[0], h)
            reset_states()
            for ji, J in enumerate(jobs):
                if ji > 0 and J.dirB and not jobs[ji - 1].dirB:
                    reset_states()
                nxt = jobs[ji + 1] if ji + 1 < len(jobs) else None
                prv = jobs[ji - 1] if ji > 0 else None
                side = []
                for h in range(NH):
                    if prv is not None and prv.with_out:
                        side.append(("out", prv, h))
                    if nxt is not None:
                        side.append(("prep", nxt, h))
                blks = list(range(J.nblk - 1, -1, -1)) if J.dirB else list(range(J.nblk))
                si = 0
                nslots = len(blks) * 3
                per = -(-len(side) // nslots) if side else 0

                def do_side(n):
                    nonlocal si
                    for _ in range(n):
                        if si < len(side):
                            kind, JJ, hh = side[si]
                            si += 1
                            (out_head if kind == "out" else prep_head)(JJ, hh)

                G = [[0, 1], [2, 3], [4, 5], [6, 7]]
                units = [(bi, bk, g) for bi, bk in enumerate(blks) for g in range(4)]
                nslots = len(units) * 2
                per = -(-len(side) // nslots) if side else 0
                for ui, (bi, bk, g) in enumerate(units):
                    hs = G[g]
                    if J.with_out:
                        stage_A(J, bk, bi % 2, hs)
                    stage_M(J, bk, 0, bi % 2, hs)
                    stage_U(J, bk, 0, hs)
                    stage_M(J, bk, 1, bi % 2, hs)
                    stage_U(J, bk, 1, hs)
                    do_side(2 * per)
                do_side(len(side))
            for h in range(NH):
                out_head(jobs[-1], h)
        P.barrier()
        if STOP == "hg":
            return _finish(nc, P, top, outT)

        with ExitStack() as ph:
            xx = sbt(ph, "xx", [128, KC, T], F32)
            hid = sbt(ph, "hid", [128, FKC, T], BF16)
            ona = hid[:, 28:36, :]
            ohg = hid[:, 36:44, :]
            sq = hid[:, 0:16, :]
            ga = sbt(ph, "ga", [128, 4, T], BF16)
            gb = sbt(ph, "gb", [128, 4, T], BF16)
            mm_ = sbt(ph, "mm_", [128, KC, T], BF16)
            h2 = mm_
            t1 = [sbt(ph, f"t1_{i}", [128, T], F32) for i in range(2)]
            t2 = [sbt(ph, f"t2_{i}", [128, T], F32) for i in range(2)]
            rs = sbt(ph, "rs", [128, T], F32)
            wq = [sbt(ph, f"wq{i}", [128, KC, 512], BF16) for i in range(3)]
            wfo_b = [sbt(ph, f"wfo_b{i}", [128, FKC // 2, 128], BF16) for i in range(2)]
            pp = [pst(ph, f"pp{i}", [128, T]) for i in range(6)]
            psn = pst(ph, "psn", [128, T])
            B_xx, B_ga, B_gb, B_mm, B_hid, B_rs = [Buf() for _ in range(6)]
            B_ona = B_ohg = B_sq = B_hid
            B_h2 = B_mm
            B_t1, B_t2 = [Buf(), Buf()], [Buf(), Buf()]
            B_wq, B_wfob = [Buf(), Buf(), Buf()], [Buf(), Buf()]
            B_pp, B_psn = [Buf() for _ in range(6)], Buf()
            c6 = {"w": 0, "p": 0, "t": 0, "f": 0}
            out_ops = []

            def ldw(src, bsrc, kcn):
                i = c6["w"] % 3
                c6["w"] += 1
                P.op("sp", lambda e: e.dma_start(out=wq[i][:, 0:kcn, :], in_=src), reads=[bsrc], writes=[B_wq[i]], dma=True)
                return wq[i], B_wq[i]

            def nextp():
                i = c6["p"] % 6
                c6["p"] += 1
                return pp[i], B_pp[i]

            def norm_to(src, bsrc, dst, bdst, si, bi, final=False, it=None):
                P.op("act", lambda e: e.activation(out=sq, in_=src[:], func=AF.Square), reads=[bsrc], writes=[B_sq])
                for c in range(KC):
                    P.op("pe", lambda e, c=c: e.matmul(psn[:], ones_bf[:], sq[:, c, :], start=(c == 0), stop=(c == KC - 1)),
                         reads=[B_sq, B_const], pwrites=[B_psn])
                P.op("act", lambda e: e.activation(out=rs[:], in_=psn[:], func=AF.Sqrt, scale=1.0 / D, bias=EPS),
                     reads=[B_psn], writes=[B_rs])
                P.op("dve", lambda e: e.reciprocal(out=rs[:], in_=rs[:]), reads=[B_rs], writes=[B_rs])
                for c in range(KC):
                    if not final:
                        i = c6["t"] % 2
                        c6["t"] += 1
                        P.op("dve", lambda e, c=c, i=i: e.tensor_tensor(out=t1[i][:], in0=src[:, c, :], in1=rs[:], op=ALU.mult),
                             reads=[bsrc, B_rs], writes=[B_t1[i]])
                        P.op("act", lambda e, c=c, i=i: e.activation(out=dst[:, c, :], in_=t1[i][:], func=AF.Identity,
                                                                   scale=prm[:, si, c:c + 1], bias=prm[:, bi, c:c + 1]),
                             reads=[B_t1[i], B_PRM], pwrites=[bdst])
                    else:
                        i = c6["t"] % 2
                        c6["t"] += 1
                        P.op("dve", lambda e, c=c, i=i: e.scalar_tensor_tensor(
                            out=t1[i][:], in0=src[:, c, :], scalar=gn[:, 2, c:c + 1], in1=rs[:], op0=ALU.mult, op1=ALU.mult),
                            reads=[bsrc, B_rs, B_gn], writes=[B_t1[i]])
                        out_ops.append(P.op("pool", lambda e, c=c, i=i: e.dma_start(
                            out=outT[c * 128:(c + 1) * 128, it * T:(it + 1) * T], in_=t1[i][:]), reads=[B_t1[i]], dma=True))

            for it in range(NT_OWN):
                t0 = it * T
                sl = slice(t0, t0 + T)
                P.op("sp", lambda e, sl=sl: e.dma_start(out=xx[:], in_=xT[:, sl].rearrange("(c p) t -> p c t", p=128)),
                     writes=[B_xx], dma=True)
                P.op("sp", lambda e, sl=sl: e.dma_start(out=ona, in_=onaT_s[:, :, sl].rearrange("c p t -> p c t")),
                     reads=[Bs["onaT"]], writes=[B_ona], dma=True)
                P.op("sp", lambda e, sl=sl: e.dma_start(out=ohg, in_=ohgT_s[:, :, sl].rearrange("c p t -> p c t")),
                     reads=[Bs["ohgT"]], pwrites=[B_ohg], dma=True)
                for n in range(4):
                    P.op("sp", lambda e, sl=sl, n=n: e.dma_start(out=ga[:], in_=gaT_s[n * 4:(n + 1) * 4, :, sl].rearrange("c p t -> p c t")),
                         reads=[Bs["gaT"]], writes=[B_ga], dma=True)
                    P.op("sp", lambda e, sl=sl, n=n: e.dma_start(out=gb[:], in_=gbT_s[n * 4:(n + 1) * 4, :, sl].rearrange("c p t -> p c t")),
                         reads=[Bs["gbT"]], writes=[B_gb], dma=True)
                    wa_, bwa = ldw(wpa_s[n], B_wpa[n], 8)
                    wb_, bwb = ldw(wpb_s[n], B_wpb[n], 8)
                    for j in range(4):
                        cj = n * 4 + j
                        p1, bp1 = nextp()
                        p2, bp2 = nextp()
                        for kc in range(8):
                            P.op("pe", lambda e, kc=kc, j=j, p1=p1, wa_=wa_: e.matmul(p1[:], wa_[:, kc, j * 128:(j + 1) * 128], ona[:, kc, :],
                                                                                     start=(kc == 0), stop=(kc == 7)),
                                 reads=[bwa, B_ona], pwrites=[bp1])
                        for kc in range(8):
                            P.op("pe", lambda e, kc=kc, j=j, p2=p2, wb_=wb_: e.matmul(p2[:], wb_[:, kc, j * 128:(j + 1) * 128], ohg[:, kc, :],
                                                                                     start=(kc == 0), stop=(kc == 7)),
                                 reads=[bwb, B_ohg], pwrites=[bp2])
                        i = c6["t"] % 2
                        c6["t"] += 1
                        P.op("dve", lambda e, cj=cj, p1=p1, i=i: e.tensor_tensor(out=t1[i][:], in0=p1[:], in1=ga[:, cj % 4, :], op=ALU.mult),
                             reads=[bp1, B_ga], writes=[B_t1[i]])
                        P.op("dve", lambda e, cj=cj, p2=p2, i=i: e.tensor_tensor(out=t2[i][:], in0=p2[:], in1=gb[:, cj % 4, :], op=ALU.mult),
                             reads=[bp2, B_gb], writes=[B_t2[i]])
                        P.op("pool", lambda e, cj=cj, i=i: e.tensor_tensor(out=mm_[:, cj, :], in0=t1[i][:], in1=t2[i][:], op=ALU.add),
                             reads=[B_t1[i], B_t2[i]], pwrites=[B_mm])
                B_x1 = Buf()
                for n in range(4):
                    w_, bw_ = ldw(wo_s[n], B_wo[n], KC)
                    for j in range(4):
                        cj = n * 4 + j
                        p1, bp1 = nextp()
                        for kc in range(KC):
                            P.op("pe", lambda e, kc=kc, j=j, p1=p1, w_=w_: e.matmul(p1[:], w_[:, kc, j * 128:(j + 1) * 128], mm_[:, kc, :],
                                                                                   start=(kc == 0), stop=(kc == KC - 1)),
                                 reads=[bw_, B_mm], pwrites=[bp1])
                        P.op("dve", lambda e, cj=cj, p1=p1: e.scalar_tensor_tensor(
                            out=xx[:, cj, :], in0=p1[:], scalar=prm[:, 2, cj:cj + 1], in1=xx[:, cj, :], op0=ALU.mult, op1=ALU.add),
                            reads=[bp1, B_xx, B_PRM], pwrites=[B_x1])
                norm_to(xx, B_x1, h2, B_h2, 3, 4)
                for g in range(11):
                    wa_, bwa = ldw(wfi_s[g], B_wfi[g], KC)
                    wu_, bwu = ldw(wfi_s[11 + g], B_wfi[11 + g], KC)
                    for j in range(4):
                        cj = g * 4 + j
                        p1, bp1 = nextp()
                        p2, bp2 = nextp()
                        for kc in range(KC):
                            P.op("pe", lambda e, kc=kc, j=j, p1=p1, wa_=wa_: e.matmul(p1[:], wa_[:, kc, j * 128:(j + 1) * 128], h2[:, kc, :],
                                                                                     start=(kc == 0), stop=(kc == KC - 1)),
                                 reads=[bwa, B_h2], pwrites=[bp1])
                        for kc in range(KC):
                            P.op("pe", lambda e, kc=kc, j=j, p2=p2, wu_=wu_: e.matmul(p2[:], wu_[:, kc, j * 128:(j + 1) * 128], h2[:, kc, :],
                                                                                     start=(kc == 0), stop=(kc == KC - 1)),
                                 reads=[bwu, B_h2], pwrites=[bp2])
                        i = c6["t"] % 2
                        c6["t"] += 1
                        P.op("act", lambda e, p1=p1, i=i: e.activation(out=t1[i][:], in_=p1[:], func=AF.Silu), reads=[bp1], writes=[B_t1[i]])
                        P.op("dve", lambda e, cj=cj, p2=p2, i=i: e.tensor_tensor(out=hid[:, cj, :], in0=p2[:], in1=t1[i][:], op=ALU.mult),
                             reads=[bp2, B_t1[i]], pwrites=[B_hid])
                B_x2 = Buf()
                for cj in range(KC):
                    p1, bp1 = nextp()
                    for half in range(2):
                        i = c6["f"] % 2
                        c6["f"] += 1
                        k0 = half * (FKC // 2)
                        P.op("sp", lambda e, cj=cj, i=i, k0=k0: e.dma_start(out=wfo_b[i][:], in_=wfo_s[cj][:, k0:k0 + FKC // 2, :]),
                             reads=[B_wfo[cj]], writes=[B_wfob[i]], dma=True)
                        for kk in range(FKC // 2):
                            kc = k0 + kk
                            P.op("pe", lambda e, kc=kc, kk=kk, p1=p1, i=i: e.matmul(p1[:], wfo_b[i][:, kk, :], hid[:, kc, :],
                                                                                 start=(kc == 0), stop=(kc == FKC - 1)),
                                 reads=[B_wfob[i], B_hid], pwrites=[bp1])
                    P.op("dve", lambda e, cj=cj, p1=p1: e.scalar_tensor_tensor(
                        out=xx[:, cj, :], in0=p1[:], scalar=prm[:, 5, cj:cj + 1], in1=xx[:, cj, :], op0=ALU.mult, op1=ALU.add),
                        reads=[bp1, B_x1, B_PRM], pwrites=[B_x2])
                norm_to(xx, B_x2, None, None, None, None, final=True, it=it)
                P.op("dve", lambda e: e.tensor_copy(out=rs[:, 0:1], in_=rs[:, 0:1]),
                     reads=[B_x2, B_rs, B_t1[0], B_t1[1]], writes=[B_xx, B_rs])
            P.finalize(out_ops)
        return _finish(nc, P, top, outT)


def _finish(nc, P, top, outT):
    if not P.final_ops:
        with nc.sbuf_tensor("zz", [128, 512], F32) as zz:
            bz = Buf()
            P.op("pool", lambda e: e.memset(zz[:], 0.0), writes=[bz])
            o = P.op("sp", lambda e: e.dma_start(out=outT[0:128, 0:512], in_=zz[:]), reads=[bz], dma=True)
            P.barrier()
            o2 = P.op("sp", lambda e: e.dma_start(out=outT[128:256, 0:512], in_=zz[:]), reads=[bz], dma=True)
            P.finalize([o, o2])
            P.emit(top)
            return nc
    P.emit(top)
    return nc


def _bias_tables(rpb, flipped):
    tab = np.full((3, 16, 5, 2, 64, 2, 64), NEG, np.float32)
    qc = np.arange(64)[:, None]
    kc = np.arange(64)[None, :]
    for cls, R in enumerate((0, 1, 2)):
        bs_t = max(R - 2, 0)
        for slot in range(5):
            for a in range(2):
                for kr2 in range(2):
                    qr = 2 * R + a
                    kr = (bs_t + slot) * 2 + kr2
                    if flipped:
                        oqr, okr, oqc, okc = 127 - qr, 127 - kr, 63 - qc, 63 - kc
                    else:
                        oqr, okr, oqc, okc = qr, kr, qc, kc
                    rs = min(max(oqr - 4, 0), 120)
                    if not (rs <= okr < rs + 8):
                        continue
                    cs = np.clip(oqc - 8, 0, 48)
                    valid_c = (okc >= cs) & (okc < cs + 16)
                    dr = okr - oqr + 7
                    dc = np.clip(okc - oqc, -15, 15) + 15
                    vals = rpb[:, dr, :][:, dc]
                    tab[cls, :, slot, a, :, kr2, :] = np.where(valid_c[None], vals, NEG)
    t = tab.reshape(3, 8, 2, 5, 2, 64, 2, 64).transpose(1, 4, 5, 0, 2, 3, 6, 7)
    return np.ascontiguousarray(t).reshape(8, 128, 3 * 2 * 5 * 128)


def _masks():
    s = np.arange(128)[:, None]
    t = np.arange(128)[None, :]
    same = (s // 64) == (t // 64)
    mA = (same & (s <= t)).astype(np.uint8)
    mB = (same & (s >= t)).astype(np.uint8)
    return np.ascontiguousarray(np.stack([mA, mB], 1))


def _fm(v, nchunk):
    return np.ascontiguousarray(v.reshape(nchunk, 128).T)


def prep_inputs(inp):
    x, c, ctx, c_ctx = inp["x"], inp["c"], inp["ctx"], inp["c_ctx"]
    w_in = inp["w_in"][0]
    w_in_f = np.concatenate([w_in[:, :4096], w_in[:, 5120:6144], w_in[:, 4096:5120], w_in[:, 6144:]], axis=1)
    w_in_f = np.ascontiguousarray(w_in_f)
    lbl_raw = inp["hg_lb_logits"]
    def lbl_of(flip):
        l = lbl_raw[:, ::-1, :] if flip else lbl_raw
        return np.ascontiguousarray(l.reshape(2, 2, 8, 128).transpose(3, 0, 1, 2))
    gns = np.ascontiguousarray(np.stack([_fm(inp["norm1_g"][0], 16), _fm(inp["norm2_g"][0], 16), _fm(inp["final_g"], 16)], 1))
    bada = _fm(inp["b_ada"][0], 96)
    bada = np.ascontiguousarray(np.stack([bada, bada], -1))
    rpb = inp["na_rpb"][0]
    btabs = [_bias_tables(rpb, False), _bias_tables(rpb, True)]
    masks = _masks()
    shared = dict(w_ada=inp["w_ada"][0], bada=bada, gns=gns, hgg=np.ascontiguousarray(inp["hg_norm_g"][0].reshape(128, 1)),
                  masks=masks, w_pa=inp["w_pa"][0], w_pb=inp["w_pb"][0], w_out=inp["w_out"][0],
                  w_fi=inp["w_ffn_in"][0], w_fo=inp["w_ffn_out"][0])
    maps = []
    for core in range(8):
        b, hf = core // 2, core % 2
        if hf == 0:
            xl = x[b]
            cl = ctx[b]
        else:
            xl = x[b][::-1]
            cl = ctx[b][::-1]
        cv = np.stack([_fm(c[b], 16), _fm(c_ctx, 16)], -1)
        m = dict(shared)
        m.update(xT=np.ascontiguousarray(xl.T), ctxT=np.ascontiguousarray(cl.T), cvec=np.ascontiguousarray(cv),
                 w_in=(w_in_f if hf else w_in), lbl=lbl_of(hf), btab=btabs[hf])
        maps.append(m)
    return maps


def assemble(results):
    out = np.empty((4, SEQ, D), np.float32)
    for core in range(8):
        b, hf = core // 2, core % 2
        o = results[core]["outT"].T
        if hf == 0:
            out[b, :OWN] = o
        else:
            out[b, OWN:] = o[::-1]
    return out


_NC = None


def kernel(**inputs):
    global _NC
    inp = {k: np.asarray(v) for k, v in inputs.items()}
    maps = prep_inputs(inp)
    if _NC is None:
        _NC = build_program()
    res = run_bass_kernel_spmd(_NC, maps, core_ids=list(range(8)))
    return assemble(res.results)


def _stats(P):
    return {e: len(P.ops[e]) for e in ENGS}
```
